# Optimizing a Trainium2 kernel written in Bass

```python
import math
import functools
import jax
import jax.numpy as jnp
from jax import lax
import numpy as np

D_MODEL = 1024
BATCH = 4
SEQ = 4096
DEPTH = 2
DEC_BATCH = 128
DEC_SEQ = 4
PAST_LEN = 8192
PAGE_SIZE = 128

HEAD_DIM = 64
SSM_HEAD_DIM = 64
SSM_HEADS = D_MODEL // SSM_HEAD_DIM
SSM_WIDTH = SSM_HEADS * SSM_HEAD_DIM
SSM_STATE = 128
SSM_GROUPS = 2
SSM_CHUNK = 128
CONV_K = 4
CONV_DIM = SSM_WIDTH + 2 * SSM_GROUPS * SSM_STATE
SWA_HEADS = D_MODEL // HEAD_DIM
SWA_KV_HEADS = 4
WINDOW = 128
FOX_HEADS = D_MODEL // HEAD_DIM
FOX_KV_HEADS = 4
FOX_Q_BLOCK = 128
FOX_FORGET_BIAS = 3.0
MEM_LEN = 256
XA_HEADS = 4
XA_HEAD_DIM = 128
XA_WIDTH = XA_HEADS * XA_HEAD_DIM
FFN_HIDDEN = ((8 * D_MODEL + 3 * 256 - 1) // (3 * 256)) * 256
N_EVEN = (DEPTH + 1) // 2
N_ODD = DEPTH // 2
EVEN_PROJ = SSM_WIDTH + CONV_DIM + SSM_HEADS + (SWA_HEADS + 2 * SWA_KV_HEADS) * HEAD_DIM
ODD_PROJ = (FOX_HEADS + 2 * FOX_KV_HEADS) * HEAD_DIM + FOX_HEADS
RMS_EPS = 1e-6
NEG_INF = -1e30

kernel_name = 'hybrid_ssd_swa_fox_step'


def rms_norm(x, g):
    xf = x.astype(jnp.float32)
    y = xf * lax.rsqrt(jnp.mean(xf * xf, axis=-1, keepdims=True) + RMS_EPS)
    return (y * g.astype(jnp.float32)).astype(x.dtype)


def gqa_attend(q, k, v, mask=None, bias=None, sink=None):
    s = jnp.einsum('...qkgd,...skd->...kgqs', q, k, preferred_element_type=jnp.float32)
    s = s * (q.shape[-1] ** -0.5)
    if bias is not None:
        s = s + bias
    if mask is not None:
        s = jnp.where(mask, s, NEG_INF)
    if sink is not None:
        sk = jnp.broadcast_to(sink.astype(jnp.float32)[:, :, None, None], s.shape[:-1] + (1,))
        p = jax.nn.softmax(jnp.concatenate([s, sk], axis=-1), axis=-1)[..., :-1]
    else:
        p = jax.nn.softmax(s, axis=-1)
    return jnp.einsum('...kgqs,...skd->...qkgd', p.astype(v.dtype), v)


def causal_conv(xpad, w, b):
    t = xpad.shape[1] - (CONV_K - 1)
    y = b
    for j in range(CONV_K):
        y = y + xpad[:, j:j + t] * w[j]
    return y


def ssd_chunk(h, inp, a_neg):
    x, dt, bm, cm = inp
    L = x.shape[1]
    rep = SSM_HEADS // SSM_GROUPS
    bh = jnp.repeat(bm, rep, axis=2)
    ch = jnp.repeat(cm, rep, axis=2)
    cs = jnp.cumsum(dt * a_neg, axis=1)
    causal = jnp.tril(jnp.ones((L, L), dtype=bool))
    decay = jnp.exp(jnp.where(causal[None, :, :, None], cs[:, :, None, :] - cs[:, None, :, :], NEG_INF))
    xdt = x * dt[..., None]
    scores = jnp.einsum('bthn,bshn->btsh', ch, bh) * decay
    y = jnp.einsum('btsh,bshp->bthp', scores, xdt)
    y = y + jnp.einsum('bthn,bhpn->bthp', ch, h) * jnp.exp(cs)[..., None]
    w_end = jnp.exp(cs[:, -1:, :] - cs)
    h_new = h * jnp.exp(cs[:, -1, :])[:, :, None, None] + jnp.einsum('bsh,bshn,bshp->bhpn', w_end, bh, xdt)
    return h_new, y


def ssm_branch(z, xbc_pad, dt_raw, h0, conv_w, conv_b, dt_bias, a_log, d_skip, ssm_norm):
    xbc = jax.nn.silu(causal_conv(xbc_pad, conv_w, conv_b)).astype(jnp.float32)
    b, t, _ = xbc.shape
    gn = SSM_GROUPS * SSM_STATE
    xs = xbc[..., :SSM_WIDTH].reshape(b, t, SSM_HEADS, SSM_HEAD_DIM)
    bm = xbc[..., SSM_WIDTH:SSM_WIDTH + gn].reshape(b, t, SSM_GROUPS, SSM_STATE)
    cm = xbc[..., SSM_WIDTH + gn:].reshape(b, t, SSM_GROUPS, SSM_STATE)
    dt = jax.nn.softplus(dt_raw.astype(jnp.float32) + dt_bias.astype(jnp.float32))
    a_neg = -jnp.exp(a_log.astype(jnp.float32))
    L = min(SSM_CHUNK, t)
    nc = t // L

    def chunks(u):
        return jnp.swapaxes(u.reshape((b, nc, L) + u.shape[2:]), 0, 1)

    h, ys = lax.scan(functools.partial(ssd_chunk, a_neg=a_neg), h0.astype(jnp.float32),
                     (chunks(xs), chunks(dt), chunks(bm), chunks(cm)))
    y = jnp.swapaxes(ys, 0, 1).reshape(b, t, SSM_HEADS, SSM_HEAD_DIM)
    y = (y + d_skip.astype(jnp.float32)[:, None] * xs).reshape(b, t, SSM_WIDTH)
    y = rms_norm(y * jax.nn.silu(z.astype(jnp.float32)), ssm_norm)
    return y.astype(z.dtype), h


def even_split(proj):
    b, t, _ = proj.shape
    sizes = [SSM_WIDTH, CONV_DIM, SSM_HEADS, SWA_HEADS * HEAD_DIM, SWA_KV_HEADS * HEAD_DIM]
    z, xbc, dt, q, k, v = jnp.split(proj, [int(i) for i in np.cumsum(sizes)], axis=-1)
    q = q.reshape(b, t, SWA_KV_HEADS, SWA_HEADS // SWA_KV_HEADS, HEAD_DIM)
    k = k.reshape(b, t, SWA_KV_HEADS, HEAD_DIM)
    v = v.reshape(b, t, SWA_KV_HEADS, HEAD_DIM)
    return z, xbc, dt, q, k, v


def swa_prompt(q, k, v, sink):
    b, t = q.shape[:2]
    nb = t // WINDOW
    qb = q.reshape((b, nb, WINDOW) + q.shape[2:])
    kb = k.reshape((b, nb, WINDOW) + k.shape[2:])
    vb = v.reshape((b, nb, WINDOW) + v.shape[2:])

    def with_prev(u):
        prev = jnp.concatenate([jnp.zeros_like(u[:, :1]), u[:, :-1]], axis=1)
        return jnp.concatenate([prev, u], axis=2)

    blk = jnp.arange(nb)[:, None] * WINDOW
    qpos = blk + jnp.arange(WINDOW)[None]
    kpos = blk - WINDOW + jnp.arange(2 * WINDOW)[None]
    diff = qpos[:, :, None] - kpos[:, None, :]
    mask = (diff >= 0) & (diff < WINDOW) & (kpos[:, None, :] >= 0)
    o = gqa_attend(qb, with_prev(kb), with_prev(vb), mask=mask[:, None, None], sink=sink)
    return o.reshape(b, t, SWA_HEADS * HEAD_DIM)


def swa_sample(q, k, v, ck, cv, sink):
    b, s = q.shape[:2]
    kk = jnp.concatenate([ck.astype(k.dtype), k], axis=1)
    vv = jnp.concatenate([cv.astype(v.dtype), v], axis=1)
    diff = jnp.arange(s)[:, None] - (jnp.arange(WINDOW + s) - WINDOW)[None]
    mask = (diff >= 0) & (diff < WINDOW)
    o = gqa_attend(q, kk, vv, mask=mask, sink=sink)
    return o.reshape(b, s, SWA_HEADS * HEAD_DIM), kk[:, -WINDOW:], vv[:, -WINDOW:]


def odd_split(proj, fb):
    b, t, _ = proj.shape
    sizes = [FOX_HEADS * HEAD_DIM, FOX_KV_HEADS * HEAD_DIM, FOX_KV_HEADS * HEAD_DIM]
    q, k, v, fg = jnp.split(proj, [int(i) for i in np.cumsum(sizes)], axis=-1)
    q = q.reshape(b, t, FOX_KV_HEADS, FOX_HEADS // FOX_KV_HEADS, HEAD_DIM)
    k = k.reshape(b, t, FOX_KV_HEADS, HEAD_DIM)
    v = v.reshape(b, t, FOX_KV_HEADS, HEAD_DIM)
    logf = jax.nn.log_sigmoid(fg.astype(jnp.float32) + fb.astype(jnp.float32))
    return q, k, v, logf


def fox_attend(q, k, v, cum, q_off):
    b, tq = q.shape[:2]
    tk = k.shape[1]
    c = jnp.transpose(cum.reshape(b, tk, FOX_KV_HEADS, FOX_HEADS // FOX_KV_HEADS), (0, 2, 3, 1))
    cq = lax.dynamic_slice_in_dim(c, q_off, tq, axis=3)
    bias = cq[..., :, None] - c[..., None, :]
    mask = jnp.arange(tk)[None, :] <= (q_off + jnp.arange(tq))[:, None]
    return gqa_attend(q, k, v, mask=mask, bias=bias)


def fox_prompt(q, k, v, logf):
    b, t = q.shape[:2]
    cum = jnp.cumsum(logf, axis=1)
    nb = t // FOX_Q_BLOCK
    qb = jnp.swapaxes(q.reshape((b, nb, FOX_Q_BLOCK) + q.shape[2:]), 0, 1)
    offs = jnp.arange(nb) * FOX_Q_BLOCK
    o = lax.map(lambda a: fox_attend(a[0], k, v, cum, a[1]), (qb, offs))
    return jnp.swapaxes(o, 0, 1).reshape(b, t, FOX_HEADS * HEAD_DIM)


def mem_kv(mem, g, wk, wv):
    b, m, _ = mem.shape
    mn = rms_norm(mem, g)
    return ((mn @ wk).reshape(b, m, XA_HEADS, XA_HEAD_DIM),
            (mn @ wv).reshape(b, m, XA_HEADS, XA_HEAD_DIM))


def cross_attn(xn, mk, mv, wq, wo):
    b, t, _ = xn.shape
    q = (xn @ wq).reshape(b, t, XA_HEADS, 1, XA_HEAD_DIM)
    o = gqa_attend(q, mk.astype(q.dtype), mv.astype(q.dtype))
    return o.reshape(b, t, XA_WIDTH) @ wo


def swiglu(xn, w_in, w_out):
    g, u = jnp.split(xn @ w_in, 2, axis=-1)
    return (jax.nn.silu(g) * u) @ w_out


def setup_inputs(seed: int = 0) -> dict:
    key = jax.random.key(seed)
    counter = [0]

    def nk():
        counter[0] += 1
        return jax.random.fold_in(key, counter[0])

    def nrm(shape, scale=1.0):
        return jax.random.normal(nk(), shape, jnp.float32) * scale

    def gain(shape):
        return 1.0 + nrm(shape, 0.02)

    n_pages = PAST_LEN // PAGE_SIZE
    n_pool = (DEC_BATCH * n_pages * 5) // 4
    page_table = jax.random.permutation(nk(), n_pool)[:DEC_BATCH * n_pages].reshape(DEC_BATCH, n_pages).astype(jnp.int32)
    dt0 = jnp.exp(jax.random.uniform(nk(), (N_EVEN, SSM_HEADS), jnp.float32, math.log(1e-3), math.log(1e-1)))
    a_log = jnp.log(jax.random.uniform(nk(), (N_EVEN, SSM_HEADS), jnp.float32, 1.0, 16.0))
    out_even_in = SSM_WIDTH + SWA_HEADS * HEAD_DIM
    return {
        'x_prompt': nrm((BATCH, SEQ, D_MODEL)),
        'x_sample': nrm((DEC_BATCH, DEC_SEQ, D_MODEL)),
        'state_ssm': nrm((N_EVEN, DEC_BATCH, SSM_HEADS, SSM_HEAD_DIM, SSM_STATE), 0.1),
        'state_conv': nrm((N_EVEN, DEC_BATCH, CONV_K - 1, CONV_DIM)),
        'cache_swa_k': nrm((N_EVEN, DEC_BATCH, WINDOW, SWA_KV_HEADS, HEAD_DIM)),
        'cache_swa_v': nrm((N_EVEN, DEC_BATCH, WINDOW, SWA_KV_HEADS, HEAD_DIM)),
        'cache_fox_k': nrm((N_ODD, n_pool, PAGE_SIZE, FOX_KV_HEADS, HEAD_DIM)),
        'cache_fox_v': nrm((N_ODD, n_pool, PAGE_SIZE, FOX_KV_HEADS, HEAD_DIM)),
        'cache_fox_logf': jax.nn.log_sigmoid(FOX_FORGET_BIAS + nrm((N_ODD, n_pool, PAGE_SIZE, FOX_HEADS))),
        'cache_mem_k': nrm((DEPTH, DEC_BATCH, MEM_LEN, XA_HEADS, XA_HEAD_DIM)),
        'cache_mem_v': nrm((DEPTH, DEC_BATCH, MEM_LEN, XA_HEADS, XA_HEAD_DIM)),
        'page_table': page_table,
        'mem_prompt': nrm((BATCH, MEM_LEN, D_MODEL)),
        'norm_mix': gain((DEPTH, D_MODEL)),
        'norm_xa': gain((DEPTH, D_MODEL)),
        'norm_mem': gain((DEPTH, D_MODEL)),
        'norm_ffn': gain((DEPTH, D_MODEL)),
        'w_in_even': nrm((N_EVEN, D_MODEL, EVEN_PROJ), D_MODEL ** -0.5),
        'conv_w': nrm((N_EVEN, CONV_K, CONV_DIM), CONV_K ** -0.5),
        'conv_b': nrm((N_EVEN, CONV_DIM), 0.02),
        'dt_bias': dt0 + jnp.log(-jnp.expm1(-dt0)),
        'a_log': a_log,
        'd_skip': 1.0 + nrm((N_EVEN, SSM_HEADS), 0.1),
        'ssm_norm': gain((N_EVEN, SSM_WIDTH)),
        'swa_sink': nrm((N_EVEN, SWA_HEADS), 0.5),
        'w_out_even': nrm((N_EVEN, out_even_in, D_MODEL), out_even_in ** -0.5),
        'w_in_odd': nrm((N_ODD, D_MODEL, ODD_PROJ), D_MODEL ** -0.5),
        'fox_fb': FOX_FORGET_BIAS + nrm((N_ODD, FOX_HEADS), 0.1),
        'w_out_odd': nrm((N_ODD, FOX_HEADS * HEAD_DIM, D_MODEL), (FOX_HEADS * HEAD_DIM) ** -0.5),
        'w_xq': nrm((DEPTH, D_MODEL, XA_WIDTH), D_MODEL ** -0.5),
        'w_xk': nrm((DEPTH, D_MODEL, XA_WIDTH), D_MODEL ** -0.5),
        'w_xv': nrm((DEPTH, D_MODEL, XA_WIDTH), D_MODEL ** -0.5),
        'w_xo': nrm((DEPTH, XA_WIDTH, D_MODEL), XA_WIDTH ** -0.5),
        'w_ffn_in': nrm((DEPTH, D_MODEL, 2 * FFN_HIDDEN), D_MODEL ** -0.5),
        'w_ffn_out': nrm((DEPTH, FFN_HIDDEN, D_MODEL), FFN_HIDDEN ** -0.5),
        'norm_final': gain((D_MODEL,)),
    }


def reference(x_prompt, x_sample, state_ssm, state_conv, cache_swa_k, cache_swa_v,
              cache_fox_k, cache_fox_v, cache_fox_logf, cache_mem_k, cache_mem_v,
              page_table, mem_prompt, norm_mix, norm_xa, norm_mem, norm_ffn,
              w_in_even, conv_w, conv_b, dt_bias, a_log, d_skip, ssm_norm, swa_sink,
              w_out_even, w_in_odd, fox_fb, w_out_odd, w_xq, w_xk, w_xv, w_xo,
              w_ffn_in, w_ffn_out, norm_final):
    xp, xs = x_prompt, x_sample
    bp, bs = xp.shape[0], xs.shape[0]
    past = page_table.shape[1] * PAGE_SIZE
    p_ssm_l, p_conv_l, p_swk_l, p_swv_l = [], [], [], []
    s_ssm_l, s_conv_l, s_swk_l, s_swv_l = [], [], [], []
    p_fk_l, p_fv_l, p_fl_l, s_fk_l, s_fv_l, s_fl_l = [], [], [], [], [], []
    p_mk_l, p_mv_l = [], []
    for l in range(DEPTH):
        li = l // 2
        hp = rms_norm(xp, norm_mix[l])
        hs = rms_norm(xs, norm_mix[l])
        if l % 2 == 0:
            ssm_params = (conv_w[li], conv_b[li], dt_bias[li], a_log[li], d_skip[li], ssm_norm[li])
            sink = swa_sink[li].reshape(SWA_KV_HEADS, SWA_HEADS // SWA_KV_HEADS)
            z, xbc, dt, q, k, v = even_split(hp @ w_in_even[li])
            xbc_pad = jnp.concatenate([jnp.zeros((bp, CONV_K - 1, CONV_DIM), xbc.dtype), xbc], axis=1)
            h0 = jnp.zeros((bp, SSM_HEADS, SSM_HEAD_DIM, SSM_STATE), jnp.float32)
            y_ssm, h = ssm_branch(z, xbc_pad, dt, h0, *ssm_params)
            y_att = swa_prompt(q, k, v, sink)
            xp = xp + jnp.concatenate([y_ssm, y_att.astype(y_ssm.dtype)], axis=-1) @ w_out_even[li]
            p_ssm_l.append(h)
            p_conv_l.append(xbc_pad[:, -(CONV_K - 1):])
            p_swk_l.append(k[:, -WINDOW:])
            p_swv_l.append(v[:, -WINDOW:])
            z, xbc, dt, q, k, v = even_split(hs @ w_in_even[li])
            xbc_pad = jnp.concatenate([state_conv[li].astype(xbc.dtype), xbc], axis=1)
            y_ssm, h = ssm_branch(z, xbc_pad, dt, state_ssm[li], *ssm_params)
            y_att, new_k, new_v = swa_sample(q, k, v, cache_swa_k[li], cache_swa_v[li], sink)
            xs = xs + jnp.concatenate([y_ssm, y_att.astype(y_ssm.dtype)], axis=-1) @ w_out_even[li]
            s_ssm_l.append(h)
            s_conv_l.append(xbc_pad[:, -(CONV_K - 1):])
            s_swk_l.append(new_k)
            s_swv_l.append(new_v)
        else:
            q, k, v, logf = odd_split(hp @ w_in_odd[li], fox_fb[li])
            xp = xp + fox_prompt(q, k, v, logf) @ w_out_odd[li]
            p_fk_l.append(k)
            p_fv_l.append(v)
            p_fl_l.append(logf)
            q, k, v, logf = odd_split(hs @ w_in_odd[li], fox_fb[li])
            kp = cache_fox_k[li, page_table].reshape(bs, past, FOX_KV_HEADS, HEAD_DIM)
            vp = cache_fox_v[li, page_table].reshape(bs, past, FOX_KV_HEADS, HEAD_DIM)
            lp = cache_fox_logf[li, page_table].reshape(bs, past, FOX_HEADS).astype(jnp.float32)
            kk = jnp.concatenate([kp.astype(k.dtype), k], axis=1)
            vv = jnp.concatenate([vp.astype(v.dtype), v], axis=1)
            cum = jnp.cumsum(jnp.concatenate([lp, logf], axis=1), axis=1)
            y = fox_attend(q, kk, vv, cum, past).reshape(bs, -1, FOX_HEADS * HEAD_DIM)
            xs = xs + y @ w_out_odd[li]
            s_fk_l.append(k)
            s_fv_l.append(v)
            s_fl_l.append(logf)
        mk, mv = mem_kv(mem_prompt, norm_mem[l], w_xk[l], w_xv[l])
        xp = xp + cross_attn(rms_norm(xp, norm_xa[l]), mk, mv, w_xq[l], w_xo[l])
        xs = xs + cross_attn(rms_norm(xs, norm_xa[l]), cache_mem_k[l], cache_mem_v[l], w_xq[l], w_xo[l])
        p_mk_l.append(mk)
        p_mv_l.append(mv)
        xp = xp + swiglu(rms_norm(xp, norm_ffn[l]), w_ffn_in[l], w_ffn_out[l])
        xs = xs + swiglu(rms_norm(xs, norm_ffn[l]), w_ffn_in[l], w_ffn_out[l])
    y_prompt = rms_norm(xp, norm_final)
    y_sample = rms_norm(xs, norm_final)
    return (y_prompt, y_sample,
            jnp.stack(p_ssm_l), jnp.stack(p_conv_l), jnp.stack(p_swk_l), jnp.stack(p_swv_l),
            jnp.stack(p_fk_l), jnp.stack(p_fv_l), jnp.stack(p_fl_l),
            jnp.stack(p_mk_l), jnp.stack(p_mv_l),
            jnp.stack(s_ssm_l), jnp.stack(s_conv_l), jnp.stack(s_swk_l), jnp.stack(s_swv_l),
            jnp.stack(s_fk_l), jnp.stack(s_fv_l), jnp.stack(s_fl_l))
```

```python
import numpy as np
import concourse.bass as bass
import concourse.mybir as mybir
from concourse.bass_utils import run_bass_kernel_spmd
from contextlib import ExitStack

F32 = mybir.dt.float32
BF16 = mybir.dt.bfloat16
I32 = mybir.dt.int32
ALU = mybir.AluOpType
AF = mybir.ActivationFunctionType
AX = mybir.AxisListType

CFG = dict(SEQ=4096, DEC_BATCH=128, PAST=8192)
NCORES = 8
D = 1024
EVEN_PROJ = 4112
ODD_PROJ = 1552
FFN_H = 2816
NEG = -30000.0


class Op:
    __slots__ = ("eng", "fn", "deps", "chan", "sem", "val", "needed", "idx")


class Prog:
    ENGS = ["pe", "act", "dve", "pool", "sp"]

    def __init__(self, nc, es):
        self.nc = nc
        self.es = es
        self.ops = []
        self.lastw = {}
        self.readers = {}
        self.chan_last = {}
        self.pending = {e: set() for e in self.ENGS}
        self.last_on_eng = {}
        self.n_sb = 0
        self.chanmap = {}
        self.NDMA = 96

    def add(self, eng, fn, reads=(), writes=(), chan=None):
        op = Op()
        op.eng, op.fn, op.chan = eng, fn, chan
        op.needed = False
        op.idx = len(self.ops)
        deps = set(self.pending[eng])
        self.pending[eng] = set()
        for k in list(reads) + list(writes):
            w = self.lastw.get(k)
            if w is not None:
                deps.add(w)
        for k in writes:
            for r in self.readers.get(k, ()):
                deps.add(r)
        if chan is not None:
            ci = self.chanmap.get(chan)
            if ci is None:
                ci = len(self.chanmap) % self.NDMA
                self.chanmap[chan] = ci
            chan = ci
            op.chan = ci
            p = self.chan_last.get(chan)
            if p is not None:
                deps.add(p)
            self.chan_last[chan] = op
        deps.discard(op)
        op.deps = deps
        for k in writes:
            self.lastw[k] = op
            self.readers[k] = []
        for k in reads:
            self.readers.setdefault(k, []).append(op)
        self.ops.append(op)
        if chan is None:
            self.last_on_eng[eng] = op
        return op

    def barrier(self):
        alld = set(self.last_on_eng.values()) | set(self.chan_last.values())
        for e in self.ENGS:
            self.pending[e] |= alld
        self.lastw = {}
        self.readers = {}
        self.chanmap = {}

    def finalize(self):
        nc = self.nc
        self.barrier()
        for e in self.ENGS:
            self.add(e, None)
        for op in self.ops:
            for d in op.deps:
                if d.eng == "pe" and op.eng == "pe" and d.chan is None and op.chan is None:
                    continue
                d.needed = True
        sems = {}
        cnt = {}
        for op in self.ops:
            if not op.needed:
                continue
            key = ("c", op.chan) if op.chan is not None else ("e", op.eng)
            if key not in sems:
                sems[key] = self.es.enter_context(nc.semaphore(f"s_{len(sems)}"))
                cnt[key] = 0
            op.sem = sems[key]
            cnt[key] += 16 if op.chan is not None else 1
            op.val = cnt[key]
        self.nsems = len(sems)
        self.maxval = max(cnt.values()) if cnt else 0
        block = self.es.enter_context(nc.Block())
        per = {e: [op for op in self.ops if op.eng == e] for e in self.ENGS}

        def emit(e, h):
            waited = {}
            for op in per[e]:
                need = {}
                for d in op.deps:
                    if not d.needed:
                        continue
                    if d.eng == "pe" and e == "pe" and d.chan is None and op.chan is None:
                        continue
                    k = id(d.sem)
                    if need.get(k, (None, 0))[1] < d.val:
                        need[k] = (d.sem, d.val)
                for k, (s, v) in need.items():
                    if waited.get(k, 0) >= v:
                        continue
                    h.wait_ge(s, v)
                    waited[k] = v
                if op.fn is None:
                    continue
                ins = op.fn(h)
                if op.needed:
                    ins.then_inc(op.sem, 16 if op.chan is not None else 1)

        @block.tensor
        def _(h):
            emit("pe", h)

        @block.scalar
        def _(h):
            emit("act", h)

        @block.vector
        def _(h):
            emit("dve", h)

        @block.gpsimd
        def _(h):
            emit("pool", h)

        @block.sync
        def _(h):
            emit("sp", h)


class Rot:
    def __init__(self, tensors, name):
        self.t = tensors
        self.name = name
        self.i = -1

    def next(self):
        self.i = (self.i + 1) % len(self.t)
        return self.t[self.i], f"{self.name}{self.i}"


def build(cfg, debug=()):
    SEQ, DEC_BATCH, PAST = cfg["SEQ"], cfg["DEC_BATCH"], cfg["PAST"]
    TP = SEQ
    BPC = DEC_BATCH // NCORES
    TS = BPC * 4
    TT = TP + TS
    NTP = TP // 128
    NPG = PAST // 128
    NPOOL = (DEC_BATCH * NPG * 5) // 4
    tiles = [(i * 128, 128) for i in range(NTP)] + [(TP, TS)]
    NTL = len(tiles)

    nc = bass.Bass("TRN2", target_bir_lowering=False)
    es = ExitStack()
    P = Prog(nc, es)
    A = P.add

    def din(name, shape, dt=F32):
        return nc.dram_tensor(name, list(shape), dt, kind="ExternalInput").ap()

    def dout(name, shape, dt=F32):
        return nc.dram_tensor(name, list(shape), dt, kind="ExternalOutput").ap()

    def dscr(name, shape, dt=F32):
        kind = "ExternalOutput" if name in debug else "Internal"
        return nc.dram_tensor(name, list(shape), dt, kind=kind).ap()

    i_xp = din("xp", [TP, D])
    i_xs = din("xs", [TS, D])
    i_ssm = din("state_ssm", [BPC, 16 * 64, 128])
    i_conv = din("state_conv", [BPC, 3, 1536])
    i_cswk = din("cswk", [BPC, 128, 256])
    i_cswv = din("cswv", [BPC, 128, 256])
    i_fkv = din("fox_kv", [NPOOL * 128, 512])
    i_flp = din("fox_lp", [NPOOL, 2048])
    i_cmk = din("cmk", [2, BPC, 256, 512])
    i_cmv = din("cmv", [2, BPC, 256, 512])
    i_pt = din("pt", [BPC, NPG], I32)
    i_memp = din("memp", [256, D])
    i_nmix = din("norm_mix", [2, D])
    i_nxa = din("norm_xa", [2, D])
    i_nmem = din("norm_mem", [2, D])
    i_nffn = din("norm_ffn", [2, D])
    i_wie = din("w_in_even", [D, EVEN_PROJ])
    i_cw = din("conv_w", [4, 1536])
    i_cb = din("conv_b", [1, 1536])
    i_dtb = din("dt_bias", [1, 16])
    i_alog = din("a_log", [1, 16])
    i_dsk = din("d_skip", [1, 16])
    i_ssmn = din("ssm_norm", [1, D])
    i_sink = din("swa_sink", [1, 16])
    i_woe = din("w_out_even", [2048, D])
    i_wio = din("w_in_odd", [D, ODD_PROJ])
    i_fb = din("fox_fb", [1, 16])
    i_woo = din("w_out_odd", [D, D])
    i_wxq = din("w_xq", [2, D, 512])
    i_wxk = din("w_xk", [2, D, 512])
    i_wxv = din("w_xv", [2, D, 512])
    i_wxo = din("w_xo", [2, 512, D])
    i_wfi = din("w_ffn_in", [2, D, 2 * FFN_H])
    i_wfo = din("w_ffn_out", [2, FFN_H, D])
    i_nfin = din("norm_final", [1, D])
    i_iota = din("iota", [128, 1])
    o_yp = dout("y_p", [TP, D])
    o_ys = dout("y_s", [TS, D])
    o_pssm = dout("p_ssm", [16 * 64, 128])
    o_pconv = dout("p_conv", [3, 1536])
    o_pswk = dout("p_swk", [128, 256])
    o_pswv = dout("p_swv", [128, 256])
    o_pfk = dout("p_fk", [TP, 256])
    o_pfv = dout("p_fv", [TP, 256])
    o_pfl = dout("p_fl", [TP, 16])
    o_pmk = dout("p_mk", [2, 256, 512])
    o_pmv = dout("p_mv", [2, 256, 512])
    o_sssm = dout("s_ssm", [BPC, 16 * 64, 128])
    o_sconv = dout("s_conv", [BPC, 3, 1536])
    o_sswk = dout("s_swk", [BPC, 128, 256])
    o_sswv = dout("s_swv", [BPC, 128, 256])
    o_sfk = dout("s_fk", [TS, 256])
    o_sfv = dout("s_fv", [TS, 256])
    o_sfl = dout("s_fl", [TS, 16])
    X = dscr("X", [TT, D])
    PROJ = dscr("PROJ", [TT, EVEN_PROJ])
    XPADP = dscr("XPADP", [TP + 3, 1536])
    XPADS = dscr("XPADS", [BPC, 7, 1536])
    XC = dscr("XC", [TT, 1536])
    CAT = dscr("CAT", [TT, 2048])
    XQ = dscr("XQ", [TT, 512])
    MKV = dscr("MKV", [256, 1024])
    XA = dscr("XA", [TT, 512])
    ACTT = dscr("ACTT", [22, 128, TT], BF16)

    uid = [0]

    AW = 53000
    arena = es.enter_context(nc.sbuf_tensor("arena", [128, AW], F32))
    atop = [0]

    def sb(shape, dt=F32):
        n = 1
        for d_ in shape[1:]:
            n *= d_
        words = n if dt in (F32, I32) else (n + 1) // 2
        off = atop[0]
        atop[0] += words
        assert atop[0] <= AW, f"arena overflow {atop[0]}"
        ap = arena[:, off:off + words]
        if dt != F32:
            ap = ap.bitcast(dt)
        if len(shape) == 3:
            ap = ap.rearrange("p (a b) -> p a b", a=shape[1])
        elif len(shape) == 4:
            ap = ap.rearrange("p (a b c) -> p a b c", a=shape[1], b=shape[2])
        return ap[:shape[0]]

    class Phase:
        def __enter__(self):
            self.mark = atop[0]
            return self

        def __exit__(self, *a):
            P.barrier()
            atop[0] = self.mark

    def ps(shape, dt=F32):
        uid[0] += 1
        return es.enter_context(nc.psum_tensor(f"ps{uid[0]}", list(shape), dt))

    def rot(n, shape, dt, name, psum=False):
        return Rot([(ps if psum else sb)(shape, dt) for _ in range(n)], name)

    identf = sb([128, 128])
    identb = sb([128, 128], BF16)
    tri = sb([128, 128])
    ones = sb([128, 128])
    A("pool", lambda h: h.memset(identf[:], 0.0), writes=["identf"])
    A("pool", lambda h: h.affine_select(out=identf[:], in_=identf[:], pattern=[[-1, 128]], compare_op=ALU.not_equal,
                                        fill=1.0, base=0, channel_multiplier=1), reads=["identf"], writes=["identf"])
    A("dve", lambda h: h.tensor_copy(identb[:], identf[:]), reads=["identf"], writes=["identb"])
    A("pool", lambda h: h.memset(tri[:], 1.0), writes=["tri"])
    A("pool", lambda h: h.affine_select(out=tri[:], in_=tri[:], pattern=[[1, 128]], compare_op=ALU.is_ge,
                                        fill=0.0, base=0, channel_multiplier=-1), reads=["tri"], writes=["tri"])
    A("pool", lambda h: h.memset(ones[:], 1.0), writes=["ones"])

    pbank = [ps([128, 512]) for _ in range(8)]
    PT = Rot([pbank[0], pbank[1]], "pT")
    PM = Rot([pbank[2], pbank[3], pbank[4], pbank[5]], "pM")
    PX = Rot([pbank[6], pbank[7]], "pX")

    gb = sb([128, D])
    hT = None

    def new_hT():
        nonlocal hT
        hT = sb([128, 8, TT], BF16)
        return hT
    xt_r = rot(2, [128, D], F32, "xt")
    xn_r = rot(2, [128, D], BF16, "xn")
    sq = sb([128, D])
    ss_r = rot(2, [128, 1], F32, "ss")
    wb_r = rot(2, [128, 8, 512], BF16, "wb")
    ev_r = rot(3, [128, 512], F32, "ev")

    def rstd_ops(src_t, src_k, rows, ssk_t, ssk):
        A("act", lambda h: h.activation(out=sq[:rows, :], in_=src_t[:rows, :], func=AF.Square, accum_out=ssk_t[:rows, :]),
          reads=[src_k], writes=["sq", ssk])
        A("dve", lambda h: h.tensor_scalar(ssk_t[:rows, :], ssk_t[:rows, :], 1.0 / D, 1e-6, ALU.mult, ALU.add),
          reads=[ssk], writes=[ssk])
        A("act", lambda h: h.sqrt(ssk_t[:rows, :], ssk_t[:rows, :]), reads=[ssk], writes=[ssk])
        A("dve", lambda h: h.reciprocal(ssk_t[:rows, :], ssk_t[:rows, :]), reads=[ssk], writes=[ssk])

    def load_gain(row_ap):
        A("sp", lambda h: h.dma_start(out=gb[:], in_=row_ap.partition_broadcast(128)), writes=["gb"], chan="gb")

    def norm_T(src, gain_row, tl, dst, dstname, col0=0):
        load_gain(gain_row)
        for (r0, rows) in tl:
            xt, xk = xt_r.next()
            A("sp", lambda h, xt=xt, r0=r0, rows=rows: h.dma_start(out=xt[:rows, :], in_=src[r0:r0 + rows, :]),
              writes=[xk], chan=xk)
            st, sk = ss_r.next()
            rstd_ops(xt, xk, rows, st, sk)
            xn, nk = xn_r.next()
            A("dve", lambda h, xn=xn, xt=xt, st=st, rows=rows: h.scalar_tensor_tensor(
                xn[:rows, :], xt[:rows, :], st[:rows, :], gb[:rows, :], ALU.mult, ALU.mult),
              reads=[xk, sk, "gb"], writes=[nk])
            pt, pk = PT.next()
            ptb = pt[:].bitcast(BF16)

            def tr(h, xn=xn, ptb=ptb, rows=rows):
                for c in range(8):
                    ins = h.transpose(ptb[:, c * 128:c * 128 + rows], xn[:rows, c * 128:(c + 1) * 128], identb[:rows, :rows])
                return ins
            A("pe", tr, reads=[nk], writes=[pk])
            A("act", lambda h, ptb=ptb, r0=r0, rows=rows: h.copy(
                dst[:, :, col0 + r0:col0 + r0 + rows], ptb.rearrange("p (c t) -> p c t", c=8)[:, :, :rows]),
              reads=[pk], writes=[f"{dstname}{r0}"])

    def load_w(W, r0k, KC, c0, w):
        wb, wk = wb_r.next()
        A("pool", lambda h, wb=wb: h.dma_start(out=wb[:, :KC, :w],
                                              in_=W[r0k:r0k + KC * 128, c0:c0 + w].rearrange("(c p) n -> p c n", p=128)),
          writes=[wk], chan=wk)
        return wb, wk

    evtog = [0]

    def evac(dst_ap, src_ap, reads, writes):
        evtog[0] ^= 1
        if evtog[0]:
            A("act", lambda h: h.copy(dst_ap, src_ap), reads=reads, writes=writes)
        else:
            A("dve", lambda h: h.tensor_copy(dst_ap, src_ap), reads=reads, writes=writes)

    def linear_to_dram(src, srcname, W, N, OUT, tl, oc0=0):
        for c0 in range(0, N, 512):
            w = min(512, N - c0)
            wb, wk = load_w(W, 0, 8, c0, w)
            for (r0, rows) in tl:
                pm, pk = PM.next()

                def mm(h, pm=pm, wb=wb, r0=r0, rows=rows, w=w):
                    for c in range(8):
                        ins = h.matmul(pm[:rows, :w], src[:, c, r0:r0 + rows], wb[:, c, :w], start=(c == 0), stop=(c == 7))
                    return ins
                A("pe", mm, reads=[f"{srcname}{r0}", wk], writes=[pk])
                ev, ek = ev_r.next()
                evac(ev[:rows, :w], pm[:rows, :w], [pk], [ek])
                A("sp", lambda h, ev=ev, r0=r0, rows=rows, c0=c0, w=w: h.dma_start(
                    out=OUT[r0:r0 + rows, oc0 + c0:oc0 + c0 + w], in_=ev[:rows, :w]),
                  reads=[ek], writes=[f"{OUT.name}:{r0}"], chan=ek + "st")

    def proj_residual(ACTD, K, W, tl):
      with Phase():
        KC = K // 128
        wres = sb([128, KC, D], BF16)
        for c0 in (0, 512):
            A("pool", lambda h, c0=c0: h.dma_start(out=wres[:, :, c0:c0 + 512],
                                                  in_=W[:, c0:c0 + 512].rearrange("(c p) n -> p c n", p=128)),
              writes=[f"wres{c0}"], chan=f"wres{c0}")
        ab_r = rot(3, [128, K], BF16, "prab")
        aT_r = rot(2, [128, KC, 128], BF16, "praT")
        xo_r = rot(2, [128, D], F32, "prxo")
        for (r0, rows) in tl:
            ab, bk = ab_r.next()
            A("pool", lambda h, ab=ab, r0=r0, rows=rows: h.dma_start(out=ab[:rows, :], in_=ACTD[r0:r0 + rows, 0:K]),
              reads=[f"{ACTD.name}:{r0}"], writes=[bk], chan=bk)
            aT, tk = aT_r.next()
            for g0 in range(0, KC, 8):
                gn = min(8, KC - g0)
                pt, pk = PT.next()
                ptb = pt[:].bitcast(BF16)

                def tr(h, ab=ab, ptb=ptb, rows=rows, g0=g0, gn=gn):
                    for c in range(gn):
                        ins = h.transpose(ptb[:, c * 128:c * 128 + rows], ab[:rows, (g0 + c) * 128:(g0 + c + 1) * 128],
                                          identb[:rows, :rows])
                    return ins
                A("pe", tr, reads=[bk], writes=[pk])
                A("act", lambda h, aT=aT, ptb=ptb, rows=rows, g0=g0, gn=gn: h.copy(
                    aT[:, g0:g0 + gn, :rows], ptb.rearrange("p (c t) -> p c t", c=8)[:, :gn, :rows]),
                  reads=[pk], writes=[tk + f"g{g0}"])
            xt, xk = xt_r.next()
            A("sp", lambda h, xt=xt, r0=r0, rows=rows: h.dma_start(out=xt[:rows, :], in_=X[r0:r0 + rows, :]),
              reads=[f"X:{r0}"], writes=[xk], chan=xk)
            xo, ok = xo_r.next()
            for hf, c0 in enumerate((0, 512)):
                pm, pk = PM.next()

                def mm(h, pm=pm, aT=aT, rows=rows, c0=c0):
                    for c in range(KC):
                        ins = h.matmul(pm[:rows, :], aT[:, c, :rows], wres[:, c, c0:c0 + 512], start=(c == 0), stop=(c == KC - 1))
                    return ins
                A("pe", mm, reads=[tk + f"g{g0}" for g0 in range(0, KC, 8)] + [f"wres{c0}"], writes=[pk])
                A("dve", lambda h, xo=xo, xt=xt, pm=pm, rows=rows, c0=c0: h.tensor_tensor(
                    xo[:rows, c0:c0 + 512], pm[:rows, :], xt[:rows, c0:c0 + 512], ALU.add),
                  reads=[pk, xk], writes=[ok + f"h{hf}"])
            A("sp", lambda h, xo=xo, r0=r0, rows=rows: h.dma_start(out=X[r0:r0 + rows, :], in_=xo[:rows, :]),
              reads=[ok + "h0", ok + "h1"], writes=[f"X:{r0}"], chan=ok + "st")
        P.barrier()

    def ffn(l):
        new_hT()
        norm_T(X, i_nffn[l:l + 1, :], tiles, hT, "hT")
        groups = [(g * 512, min(512, TP - g * 512)) for g in range((TP + 511) // 512)] + [(TP, TS)]
        wg_r = rot(2, [128, 8, 128], BF16, "wg")
        wu_r = rot(2, [128, 8, 128], BF16, "wu")
        sg_r = rot(2, [128, 512], F32, "sg")
        ao_r = rot(2, [128, 512], BF16, "ao")
        Wi = i_wfi[l]
        Wo = i_wfo[l]
        for j in range(22):
            wg, gk = wg_r.next()
            wu, uk = wu_r.next()
            A("pool", lambda h, wg=wg, j=j: h.dma_start(out=wg[:], in_=Wi[:, j * 128:(j + 1) * 128].rearrange("(c p) n -> p c n", p=128)),
              writes=[gk], chan=gk)
            A("pool", lambda h, wu=wu, j=j: h.dma_start(out=wu[:], in_=Wi[:, FFN_H + j * 128:FFN_H + (j + 1) * 128].rearrange("(c p) n -> p c n", p=128)),
              writes=[uk], chan=uk)
            for (t0, tw) in groups:
                pg, pgk = PM.next()
                pu, puk = PM.next()
                hk = [f"hT{r0}" for (r0, rows) in tiles if r0 >= t0 and r0 < t0 + tw]

                def mmg(h, pg=pg, wg=wg, t0=t0, tw=tw, hT=hT):
                    for c in range(8):
                        ins = h.matmul(pg[:, :tw], wg[:, c, :], hT[:, c, t0:t0 + tw], start=(c == 0), stop=(c == 7))
                    return ins
                A("pe", mmg, reads=hk + [gk], writes=[pgk])

                def mmu(h, pu=pu, wu=wu, t0=t0, tw=tw, hT=hT):
                    for c in range(8):
                        ins = h.matmul(pu[:, :tw], wu[:, c, :], hT[:, c, t0:t0 + tw], start=(c == 0), stop=(c == 7))
                    return ins
                A("pe", mmu, reads=hk + [uk], writes=[puk])
                sg, sk = sg_r.next()
                A("act", lambda h, sg=sg, pg=pg, tw=tw: h.activation(out=sg[:, :tw], in_=pg[:, :tw], func=AF.Silu),
                  reads=[pgk], writes=[sk])
                ao, aok = ao_r.next()
                A("dve", lambda h, ao=ao, sg=sg, pu=pu, tw=tw: h.tensor_tensor(ao[:, :tw], pu[:, :tw], sg[:, :tw], ALU.mult),
                  reads=[sk, puk], writes=[aok])
                A("sp", lambda h, ao=ao, j=j, t0=t0, tw=tw: h.dma_start(out=ACTT[j, :, t0:t0 + tw], in_=ao[:, :tw]),
                  reads=[aok], writes=[f"ACTT{j}:{t0}"], chan=aok + "st")
        P.barrier()
        wres = sb([128, 22, D], BF16)
        for c0 in (0, 512):
            A("pool", lambda h, c0=c0: h.dma_start(out=wres[:, :, c0:c0 + 512],
                                                  in_=Wo[:, c0:c0 + 512].rearrange("(c p) n -> p c n", p=128)),
              writes=[f"fwres{c0}"], chan=f"fwres{c0}")
        aT_r = rot(2, [128, 22, 128], BF16, "faT")
        xo_r = rot(2, [128, D], F32, "fxo")
        for (r0, rows) in tiles:
            aT, tk = aT_r.next()
            A("sp", lambda h, aT=aT, r0=r0, rows=rows: h.dma_start(
                out=aT[:, :, :rows], in_=ACTT[:, :, r0:r0 + rows].rearrange("j p t -> p j t")),
              writes=[tk], chan=tk)
            xt, xk = xt_r.next()
            A("sp", lambda h, xt=xt, r0=r0, rows=rows: h.dma_start(out=xt[:rows, :], in_=X[r0:r0 + rows, :]),
              writes=[xk], chan=xk)
            xo, ok = xo_r.next()
            for hf, c0 in enumerate((0, 512)):
                pm, pk = PM.next()

                def mm(h, pm=pm, aT=aT, rows=rows, c0=c0):
                    for c in range(22):
                        ins = h.matmul(pm[:rows, :], aT[:, c, :rows], wres[:, c, c0:c0 + 512], start=(c == 0), stop=(c == 21))
                    return ins
                A("pe", mm, reads=[tk, f"fwres{c0}"], writes=[pk])
                A("dve", lambda h, xo=xo, xt=xt, pm=pm, rows=rows, c0=c0: h.tensor_tensor(
                    xo[:rows, c0:c0 + 512], pm[:rows, :], xt[:rows, c0:c0 + 512], ALU.add),
                  reads=[pk, xk], writes=[ok + f"h{hf}"])
            A("sp", lambda h, xo=xo, r0=r0, rows=rows: h.dma_start(out=X[r0:r0 + rows, :], in_=xo[:rows, :]),
              reads=[ok + "h0", ok + "h1"], writes=[f"X:{r0}"], chan=ok + "st")
        P.barrier()

    def final_norm():
        load_gain(i_nfin[0:1, :])
        yo_r = rot(2, [128, D], F32, "fyo")
        for (r0, rows) in tiles:
            xt, xk = xt_r.next()
            A("sp", lambda h, xt=xt, r0=r0, rows=rows: h.dma_start(out=xt[:rows, :], in_=X[r0:r0 + rows, :]),
              writes=[xk], chan=xk)
            st, sk = ss_r.next()
            rstd_ops(xt, xk, rows, st, sk)
            yo, yk = yo_r.next()
            A("dve", lambda h, yo=yo, xt=xt, st=st, rows=rows: h.scalar_tensor_tensor(
                yo[:rows, :], xt[:rows, :], st[:rows, :], gb[:rows, :], ALU.mult, ALU.mult),
              reads=[xk, sk, "gb"], writes=[yk])
            dst = o_yp[r0:r0 + rows, :] if r0 < TP else o_ys[:, :]
            A("sp", lambda h, yo=yo, dst=dst, rows=rows: h.dma_start(out=dst, in_=yo[:rows, :]),
              reads=[yk], writes=[f"y:{r0}"], chan=yk + "st")


    E_r = rot(4, [128, 512], BF16, "E")
    rd_r = rot(3, [128, 4], F32, "rd")

    def attn_step(kT_ap, kkeys, qT_ap, qkeys, nk, ncols, scale, bias_ap, bkeys, mask_ap, pvs, pok, vkeys, SB=None):
        pm, pk = (SB or PM).next()
        A("pe", lambda h: h.matmul(pm[:nk, :ncols], kT_ap, qT_ap, start=True, stop=True), reads=kkeys + qkeys, writes=[pk])
        E, ek = E_r.next()
        if bias_ap is None:
            A("act", lambda h: h.activation(out=E[:nk, :ncols], in_=pm[:nk, :ncols], func=AF.Exp, scale=scale),
              reads=[pk], writes=[ek])
        else:
            A("act", lambda h: h.activation(out=E[:nk, :ncols], in_=pm[:nk, :ncols], func=AF.Exp, bias=bias_ap, scale=scale),
              reads=[pk] + bkeys, writes=[ek])
        if mask_ap is not None:
            A("dve", lambda h: h.tensor_tensor(E[:nk, :ncols], E[:nk, :ncols], mask_ap, ALU.mult), reads=[ek], writes=[ek])

        if pvs is None:
            return E, ek

        def pvf(h):
            for (po_ap, c0, nqq, v_ap, st, sp_) in pvs:
                ins = h.matmul(po_ap, E[:nk, c0:c0 + nqq], v_ap, start=st, stop=sp_)
            return ins
        A("pe", pvf, reads=[ek] + vkeys, writes=[pok])

    class Defer:
        def __init__(self):
            self.p = []

        def push(self, fn):
            if self.p:
                self.p.pop()()
            self.p.append(fn)

        def flush(self):
            while self.p:
                self.p.pop()()

    def dram_copy(dst, src, key):
        A("sp", lambda h: h.dma_start(out=dst, in_=src), writes=[key], chan=key)

    def xattn(l):
        mt = [(0, 128), (128, 128)]
        with Phase():
            new_hT()
            norm_T(i_memp, i_nmem[l:l + 1, :], mt, hT, "hT")
            linear_to_dram(hT, "hT", i_wxk[l], 512, o_pmk[l], mt)
            linear_to_dram(hT, "hT", i_wxv[l], 512, o_pmv[l], mt)
        with Phase():
            new_hT()
            norm_T(X, i_nxa[l:l + 1, :], tiles, hT, "hT")
            linear_to_dram(hT, "hT", i_wxq[l], 512, XQ, tiles)
        with Phase():
            seqs = [(o_pmk[l], o_pmv[l], tiles[:NTP])] + [(i_cmk[l, b], i_cmv[l, b], [(TP + 4 * b, 4)]) for b in range(BPC)]
            kb_r = rot(2, [128, 2, 512], BF16, "xkb")
            kT_r = rot(2, [128, 4, 256], BF16, "xkT")
            va_r = rot(2, [128, 2, 4, 129], BF16, "xva")
            qb_r = rot(3, [128, 512], BF16, "xqb")
            qT_r = rot(2, [128, 512], BF16, "xqT")
            xo_r = rot(2, [128, 512], F32, "xxo")
            sc = 128 ** -0.5
            dfr = Defer()
            E_r.t = E_r.t
            for (kd, vd, qtl) in seqs:
                kb, kbk = kb_r.next()
                A("pool", lambda h, kb=kb, kd=kd: h.dma_start(out=kb[:], in_=kd.rearrange("(b p) n -> p b n", p=128)), writes=[kbk], chan=kbk)
                pt, pk = PT.next()
                ptb = pt[:].bitcast(BF16)

                def trk(h, kb=kb, ptb=ptb):
                    for hh in range(4):
                        for bl in range(2):
                            ins = h.transpose(ptb[:, (hh * 2 + bl) * 128:(hh * 2 + bl + 1) * 128], kb[:, bl, hh * 128:(hh + 1) * 128], identb[:])
                    return ins
                A("pe", trk, reads=[kbk], writes=[pk])
                kT, kTk = kT_r.next()
                A("act", lambda h, kT=kT, ptb=ptb: h.copy(kT[:].rearrange("p a b -> p (a b)"), ptb), reads=[pk], writes=[kTk])
                va, vak = va_r.next()
                A("pool", lambda h, va=va: h.memset(va[:, :, :, 128:129], 1.0), writes=[vak + "o"])
                for bl_ in range(2):
                    A("pool", lambda h, va=va, vd=vd, bl_=bl_: h.dma_start(
                        out=va[:, bl_, :, 0:128], in_=vd[bl_ * 128:(bl_ + 1) * 128, :].rearrange("p (h d) -> p h d", h=4)),
                      writes=[vak + f"b{bl_}"], chan=vak + f"b{bl_}")
                for (r0, rows) in qtl:
                    qb, qbk = qb_r.next()
                    A("pool", lambda h, qb=qb, r0=r0, rows=rows: h.dma_start(out=qb[:rows, :], in_=XQ[r0:r0 + rows, :]), writes=[qbk], chan=qbk)
                    pt, pk = PT.next()
                    ptb = pt[:].bitcast(BF16)

                    def trq(h, qb=qb, ptb=ptb, rows=rows):
                        for hh in range(4):
                            ins = h.transpose(ptb[:, hh * 128:hh * 128 + rows], qb[:rows, hh * 128:(hh + 1) * 128], identb[:rows, :rows])
                        return ins
                    A("pe", trq, reads=[qbk], writes=[pk])
                    qT, qTk = qT_r.next()
                    A("act", lambda h, qT=qT, ptb=ptb, rows=rows: h.copy(
                        qT[:, :4 * rows].rearrange("p (h q) -> p h q", h=4), ptb.rearrange("p (h t) -> p h t", h=8)[:, :4, :rows]),
                      reads=[pk], writes=[qTk])
                    xo, xok = xo_r.next()
                    Es = []
                    for bl in range(2):
                        pm, pk = PM.next()

                        def msx(h, pm=pm, bl=bl, kT=kT, qT=qT, rows=rows):
                            for hh in range(4):
                                ins = h.matmul(pm[:, hh * rows:(hh + 1) * rows], kT[:, hh, bl * 128:(bl + 1) * 128], qT[:, hh * rows:(hh + 1) * rows],
                                               start=True, stop=True)
                            return ins
                        A("pe", msx, reads=[kTk, qTk], writes=[pk])
                        E, ek = E_r.next()
                        A("act", lambda h, E=E, pm=pm, rows=rows: h.activation(out=E[:, :4 * rows], in_=pm[:, :4 * rows], func=AF.Exp, scale=sc),
                          reads=[pk], writes=[ek])
                        Es.append((E, ek))

                    def tail(Es=Es, va=va, vak=vak, xo=xo, xok=xok, rows=rows, r0=r0):
                        for pr in range(2):
                            po, pok = PX.next()

                            def pvf(h, po=po, pr=pr):
                                for hl in range(2):
                                    hh = pr * 2 + hl
                                    for bl in range(2):
                                        ins = h.matmul(po[:rows, hl * 129:(hl + 1) * 129], Es[bl][0][:128, hh * rows:(hh + 1) * rows], va[:, bl, hh, :],
                                                       start=(bl == 0), stop=(bl == 1))
                                return ins
                            A("pe", pvf, reads=[Es[0][1], Es[1][1], vak + "b0", vak + "b1", vak + "o"], writes=[pok])
                            rd, rdk = rd_r.next()
                            A("dve", lambda h, rd=rd, po=po: h.reciprocal(rd[:rows, 0:2].unsqueeze(2),
                                                                          po[:rows, :258].rearrange("p (h e) -> p h e", h=2)[:, :, 128:129]),
                              reads=[pok], writes=[rdk])
                            A("dve", lambda h, rd=rd, po=po, pr=pr: h.tensor_tensor(
                                xo[:rows, pr * 256:(pr + 1) * 256].rearrange("p (h d) -> p h d", h=2),
                                po[:rows, :258].rearrange("p (h e) -> p h e", h=2)[:, :, 0:128],
                                rd[:rows, 0:2].unsqueeze(2).broadcast_to([rows, 2, 128]), ALU.mult),
                              reads=[pok, rdk], writes=[xok + f"h{pr}"])
                        A("sp", lambda h: h.dma_start(out=XA[r0:r0 + rows, :], in_=xo[:rows, :]),
                          reads=[xok + "h0", xok + "h1"], writes=[f"XA:{r0}"], chan=xok + "st")
                    dfr.push(tail)
            dfr.flush()
        proj_residual(XA, 512, i_wxo[l], tiles)

    zt = sb([128, 1536])
    A("pool", lambda h: h.memset(zt[:], 0.0), writes=["zt"])

    def even_inproj():
        with Phase():
            new_hT()
            norm_T(X, i_nmix[0:1, :], tiles, hT, "hT")
            linear_to_dram(hT, "hT", i_wie, EVEN_PROJ, PROJ, tiles)
        A("sp", lambda h: h.dma_start(out=XPADP[0:3, :], in_=zt[0:3, :]), reads=["zt"], writes=["xpz"], chan="xpz")
        for t0 in range(0, TP, 512):
            tw = min(512, TP - t0)
            dram_copy(XPADP[3 + t0:3 + t0 + tw, :], PROJ[t0:t0 + tw, 1024:2560], f"xpp{t0}")
        dram_copy(XPADS[:, 0:3, :], i_conv[:, :, :], "xps0")
        dram_copy(XPADS[:, 3:7, :], PROJ[TP:TT, 1024:2560].rearrange("(b i) c -> b i c", i=4), "xps1")
        dram_copy(o_pswk[:, :], PROJ[TP - 128:TP, 3600:3856], "opswk")
        dram_copy(o_pswv[:, :], PROJ[TP - 128:TP, 3856:4112], "opswv")
        dram_copy(o_sswk[:, 0:124, :], i_cswk[:, 4:128, :], "osswk0")
        dram_copy(o_sswv[:, 0:124, :], i_cswv[:, 4:128, :], "osswv0")
        dram_copy(o_sswk[:, 124:128, :], PROJ[TP:TT, 3600:3856].rearrange("(b i) c -> b i c", i=4), "osswk1")
        dram_copy(o_sswv[:, 124:128, :], PROJ[TP:TT, 3856:4112].rearrange("(b i) c -> b i c", i=4), "osswv1")
        P.barrier()
        dram_copy(o_pconv[:, :], XPADP[TP:TP + 3, :], "opconv")
        dram_copy(o_sconv[:, :, :], XPADS[:, 4:7, :], "osconv")
        P.barrier()


    def conv_phase():
        with Phase():
            cwb = sb([128, 4, 1536])
            cbb = sb([128, 1536])
            for j in range(4):
                A("sp", lambda h, j=j: h.dma_start(out=cwb[:, j, :], in_=i_cw[j:j + 1, :].partition_broadcast(128)),
                  writes=[f"cwb{j}"], chan=f"cwb{j}")
            A("sp", lambda h: h.dma_start(out=cbb[:], in_=i_cb[0:1, :].partition_broadcast(128)), writes=["cbb"], chan="cbb")
            xj_r = [rot(2, [128, 1536], F32, f"cx{j}") for j in range(4)]
            acc_r = rot(2, [128, 1536], F32, "cacc")
            tmp_r = rot(2, [128, 1536], F32, "ctmp")
            for (r0, rows) in tiles:
                xs_ = []
                for j in range(4):
                    xj, xjk = xj_r[j].next()
                    if r0 < TP:
                        src = XPADP[r0 + j:r0 + j + rows, :]
                        A("sp", lambda h, xj=xj, src=src, rows=rows: h.dma_start(out=xj[:rows, :], in_=src), writes=[xjk], chan=xjk)
                    else:
                        for b in range(BPC):
                            A("sp", lambda h, xj=xj, b=b, j=j: h.dma_start(out=xj[4 * b:4 * b + 4, :], in_=XPADS[b, j:j + 4, :]),
                              writes=[xjk], chan=xjk)
                    xs_.append((xj, xjk))
                acc, ack = acc_r.next()
                A("dve", lambda h, acc=acc, x=xs_[0][0], rows=rows: h.tensor_tensor(acc[:rows, :], x[:rows, :], cwb[:rows, 0, :], ALU.mult),
                  reads=[xs_[0][1], "cwb0"], writes=[ack])
                for j in range(1, 4):
                    tmp, tk = tmp_r.next()
                    A("pool", lambda h, tmp=tmp, x=xs_[j][0], rows=rows, j=j: h.tensor_tensor(tmp[:rows, :], x[:rows, :], cwb[:rows, j, :], ALU.mult),
                      reads=[xs_[j][1], f"cwb{j}"], writes=[tk])
                    A("dve", lambda h, acc=acc, tmp=tmp, rows=rows: h.tensor_tensor(acc[:rows, :], acc[:rows, :], tmp[:rows, :], ALU.add),
                      reads=[tk, ack], writes=[ack])
                A("dve", lambda h, acc=acc, rows=rows: h.tensor_tensor(acc[:rows, :], acc[:rows, :], cbb[:rows, :], ALU.add),
                  reads=[ack, "cbb"], writes=[ack])
                tmp, tk = tmp_r.next()
                A("act", lambda h, tmp=tmp, acc=acc, rows=rows: h.activation(out=tmp[:rows, :], in_=acc[:rows, :], func=AF.Silu),
                  reads=[ack], writes=[tk])
                A("sp", lambda h, tmp=tmp, r0=r0, rows=rows: h.dma_start(out=XC[r0:r0 + rows, :], in_=tmp[:rows, :]),
                  reads=[tk], writes=[f"XC:{r0}"], chan=tk + "st")

    def ssd_phase():
        with Phase():
            dtb = sb([128, 16])
            aneg = sb([128, 16])
            dskb = sb([128, 16])
            A("sp", lambda h: h.dma_start(out=dtb[:], in_=i_dtb[0:1, :].partition_broadcast(128)), writes=["dtb"], chan="dtb")
            A("sp", lambda h: h.dma_start(out=aneg[:], in_=i_alog[0:1, :].partition_broadcast(128)), writes=["aneg"], chan="aneg")
            A("sp", lambda h: h.dma_start(out=dskb[:], in_=i_dsk[0:1, :].partition_broadcast(128)), writes=["dskb"], chan="dskb")
            A("act", lambda h: h.activation(out=aneg[:], in_=aneg[:], func=AF.Exp), reads=["aneg"], writes=["aneg"])
            A("dve", lambda h: h.tensor_scalar(aneg[:], aneg[:], -1.0, None, ALU.mult), reads=["aneg"], writes=["aneg"])
            load_gain(i_ssmn[0:1, :])
            state = sb([128, 1024])
            stateb = sb([128, 1024], BF16)
            stf = sb([128, 8, 128])
            xs_r = rot(2, [128, 1024], F32, "sxs")
            bc_r = rot(2, [128, 512], F32, "sbc")
            bcb_r = rot(2, [128, 512], BF16, "sbcb")
            bcT_r = rot(2, [128, 4, 128], BF16, "sbcT")
            z_r = rot(2, [128, 1024], F32, "sz")
            dt_r = rot(2, [128, 16], F32, "sdt")
            d1_r = rot(2, [128, 16], F32, "sd1")
            d2_r = rot(2, [128, 16], F32, "sd2")
            a_r = rot(2, [128, 16], F32, "sa")
            cs_r = rot(2, [128, 16], F32, "scs")
            ecs_r = rot(2, [128, 16], F32, "secs")
            edl_r = rot(2, [128, 16], F32, "sedl")
            rhsA_r = rot(2, [128, 16, 128], F32, "srhsA")
            dec_r = rot(2, [128, 16, 128], F32, "sdec")
            GT_r = rot(2, [128, 2, 128], F32, "sGT")
            sc_r = rot(2, [128, 16, 128], BF16, "ssc")
            xsb_r = rot(2, [128, 1024], BF16, "sxsb")
            identD = sb([128, 16, 128], BF16)
            A("dve", lambda h: h.tensor_tensor(identD[:], identf[:].unsqueeze(1).broadcast_to([128, 16, 128]),
                                               dskb[:].unsqueeze(2).broadcast_to([128, 16, 128]), ALU.mult), reads=["dskb"], writes=["identD"])
            xdt_r = rot(2, [128, 16, 64], BF16, "sxdt")
            xdw_r = rot(2, [128, 16, 64], BF16, "sxdw")
            t1_r = rot(2, [128, 1024], F32, "st1")
            t2_r = rot(2, [128, 1024], F32, "st2")
            yn_r = rot(2, [128, 1024], F32, "syn")

            def chunk(L, row0):
                xs, xsk = xs_r.next()
                A("sp", lambda h: h.dma_start(out=xs[:L, :], in_=XC[row0:row0 + L, 0:1024]), writes=[xsk], chan=xsk)
                bc, bck = bc_r.next()
                A("sp", lambda h: h.dma_start(out=bc[:L, :], in_=XC[row0:row0 + L, 1024:1536]), writes=[bck], chan=bck)
                z, zk = z_r.next()
                A("sp", lambda h: h.dma_start(out=z[:L, :], in_=PROJ[row0:row0 + L, 0:1024]), writes=[zk], chan=zk)
                dt, dtk = dt_r.next()
                A("sp", lambda h: h.dma_start(out=dt[:L, :], in_=PROJ[row0:row0 + L, 2560:2576]), writes=[dtk], chan=dtk)
                d1, d1k = d1_r.next()
                d2, d2k = d2_r.next()
                A("dve", lambda h: h.tensor_tensor(dt[:L, :], dt[:L, :], dtb[:L, :], ALU.add), reads=[dtk, "dtb"], writes=[dtk])
                A("dve", lambda h: h.tensor_scalar(d1[:L, :], dt[:L, :], -1.0, None, ALU.mult), reads=[dtk], writes=[d1k])
                A("dve", lambda h: h.tensor_tensor(d1[:L, :], d1[:L, :], dt[:L, :], ALU.min), reads=[dtk, d1k], writes=[d1k])
                A("act", lambda h: h.activation(out=d1[:L, :], in_=d1[:L, :], func=AF.Exp), reads=[d1k], writes=[d1k])
                A("dve", lambda h: h.tensor_scalar(d1[:L, :], d1[:L, :], 1.0, None, ALU.add), reads=[d1k], writes=[d1k])
                A("act", lambda h: h.activation(out=d1[:L, :], in_=d1[:L, :], func=AF.Ln), reads=[d1k], writes=[d1k])
                A("dve", lambda h: h.tensor_scalar(d2[:L, :], dt[:L, :], 0.0, None, ALU.max), reads=[dtk], writes=[d2k])
                A("dve", lambda h: h.tensor_tensor(dt[:L, :], d1[:L, :], d2[:L, :], ALU.add), reads=[d1k, d2k], writes=[dtk])
                a, ak = a_r.next()
                A("dve", lambda h: h.tensor_tensor(a[:L, :], dt[:L, :], aneg[:L, :], ALU.mult), reads=[dtk, "aneg"], writes=[ak])
                pc, pck = PT.next()
                A("pe", lambda h: h.matmul(pc[:L, 0:16], tri[:L, :L], a[:L, :], start=True, stop=True), reads=[ak], writes=[pck])
                cs, csk = cs_r.next()
                A("dve", lambda h: h.tensor_copy(cs[:L, :], pc[:L, 0:16]), reads=[pck], writes=[csk])
                ecs, ecsk = ecs_r.next()
                A("act", lambda h: h.activation(out=ecs[:L, :], in_=pc[:L, 0:16], func=AF.Exp), reads=[pck], writes=[ecsk])
                pl, plk = PT.next()
                A("pe", lambda h: h.matmul(pl[:, 0:16], ones[:L, :], a[:L, :], start=True, stop=True), reads=[ak], writes=[plk])
                edl, edlk = edl_r.next()
                A("act", lambda h: h.activation(out=edl[:], in_=pl[:, 0:16], func=AF.Exp), reads=[plk], writes=[edlk])
                rhsA, rak = rhsA_r.next()
                A("dve", lambda h: h.tensor_tensor(rhsA[:L, :, :L], tri[:L, :L].unsqueeze(1).broadcast_to([L, 16, L]),
                                                   a[:L, :].unsqueeze(2).broadcast_to([L, 16, L]), ALU.mult), reads=[ak], writes=[rak])
                dec, deck = dec_r.next()
                HG = max(1, min(16, 512 // L))
                for g0 in range(0, 16, HG):
                    pm, pmk = PM.next()
                    A("pe", lambda h, pm=pm, g0=g0: h.matmul(pm[:L, :HG * L], ones[:L, :L], rhsA[:L, g0:g0 + HG, :L], start=True, stop=True),
                      reads=[rak], writes=[pmk])
                    A("dve", lambda h, pm=pm, g0=g0: h.tensor_tensor(
                        dec[:L, g0:g0 + HG, :L], pm[:L, :HG * L].rearrange("p (a b) -> p a b", a=HG),
                        cs[:L, g0:g0 + HG].unsqueeze(2).broadcast_to([L, HG, L]), ALU.subtract),
                      reads=[pmk, csk], writes=[deck + f"g{g0}"])
                dks = [deck + f"g{g0}" for g0 in range(0, 16, HG)]
                A("act", lambda h: h.activation(out=dec[:L, :, :L], in_=dec[:L, :, :L], func=AF.Relu, scale=-1.0), reads=dks, writes=[deck])
                A("act", lambda h: h.activation(out=dec[:L, :, :L], in_=dec[:L, :, :L], func=AF.Exp, scale=-1.0), reads=[deck], writes=[deck])
                A("dve", lambda h: h.tensor_tensor(dec[:L, :, :L], dec[:L, :, :L], tri[:L, :L].unsqueeze(1).broadcast_to([L, 16, L]), ALU.mult),
                  reads=[deck], writes=[deck])
                bcb, bcbk = bcb_r.next()
                A("act", lambda h: h.copy(bcb[:L, :], bc[:L, :]), reads=[bck], writes=[bcbk])
                xsb, xsbk = xsb_r.next()
                A("act", lambda h: h.copy(xsb[:L, :], xs[:L, :]), reads=[xsk], writes=[xsbk])
                pt, ptk = PT.next()
                ptb = pt[:].bitcast(BF16)

                def trb(h):
                    for c in range(4):
                        ins = h.transpose(ptb[:, c * 128:c * 128 + L], bcb[:L, c * 128:(c + 1) * 128], identb[:L, :L])
                    return ins
                A("pe", trb, reads=[bcbk], writes=[ptk])
                bcT, bcTk = bcT_r.next()
                A("act", lambda h: h.copy(bcT[:, :, :L], ptb.rearrange("p (c t) -> p c t", c=8)[:, :4, :L]), reads=[ptk], writes=[bcTk])
                pg, pgk = PX.next()

                def mg(h):
                    for g in range(2):
                        ins = h.matmul(pg[:L, g * L:(g + 1) * L], bcT[:, g, :L], bcT[:, 2 + g, :L], start=True, stop=True)
                    return ins
                A("pe", mg, reads=[bcTk], writes=[pgk])
                GT, GTk = GT_r.next()
                A("act", lambda h: h.copy(GT[:L, :, :L], pg[:L, :2 * L].rearrange("p (g t) -> p g t", g=2)), reads=[pgk], writes=[GTk])
                sc, sck = sc_r.next()
                for g in range(2):
                    A("dve", lambda h, g=g: h.tensor_tensor(sc[:L, 8 * g:8 * g + 8, :L], dec[:L, 8 * g:8 * g + 8, :L],
                                                            GT[:L, g, :L].unsqueeze(1).broadcast_to([L, 8, L]), ALU.mult),
                      reads=[deck, GTk], writes=[sck + f"g{g}"])
                xdt, xdtk = xdt_r.next()
                A("dve", lambda h: h.tensor_tensor(xdt[:L], xs[:L, :].rearrange("p (h d) -> p h d", h=16),
                                                    dt[:L, :].unsqueeze(2).broadcast_to([L, 16, 64]), ALU.mult), reads=[xsk, dtk], writes=[xdtk])
                pys = []
                for half in range(2):
                    py, pyk = PM.next()

                    def my(h, py=py, half=half):
                        for hh in range(8):
                            hd = half * 8 + hh
                            h.matmul(py[:L, hh * 64:(hh + 1) * 64], sc[:L, hd, :L], xdt[:L, hd, :], start=True, stop=False)
                            ins = h.matmul(py[:L, hh * 64:(hh + 1) * 64], identD[:L, hd, :L], xsb[:L, hd * 64:(hd + 1) * 64], start=False, stop=True)
                        return ins
                    A("pe", my, reads=[sck + "g0", sck + "g1", xdtk, xsbk, "identD"], writes=[pyk])
                    pys.append((py, pyk))
                A("act", lambda h: h.copy(stateb[:], state[:]), reads=["state"], writes=["stateb"])
                pis = []
                for g in range(2):
                    pi, pik = PM.next()
                    A("pe", lambda h, pi=pi, g=g: h.matmul(pi[:L, :], bcT[:, 2 + g, :L], stateb[:, g * 512:(g + 1) * 512], start=True, stop=True),
                      reads=[bcTk, "stateb"], writes=[pik])
                    pis.append((pi, pik))
                t1, t1k = t1_r.next()
                t2, t2k = t2_r.next()
                for g in range(2):
                    A("dve", lambda h, g=g: h.tensor_tensor(
                        t1[:L, g * 512:(g + 1) * 512].rearrange("p (h d) -> p h d", h=8), pis[g][0][:L, :].rearrange("p (h d) -> p h d", h=8),
                        ecs[:L, 8 * g:8 * g + 8].unsqueeze(2).broadcast_to([L, 8, 64]), ALU.mult), reads=[pis[g][1], ecsk], writes=[t1k + f"a{g}"])
                    A("dve", lambda h, g=g: h.tensor_tensor(t1[:L, g * 512:(g + 1) * 512], t1[:L, g * 512:(g + 1) * 512], pys[g][0][:L, :], ALU.add),
                      reads=[pys[g][1], t1k + f"a{g}"], writes=[t1k + f"b{g}"])
                A("act", lambda h: h.activation(out=t2[:L, :], in_=z[:L, :], func=AF.Silu), reads=[zk], writes=[t2k])
                A("dve", lambda h: h.tensor_tensor(t1[:L, :], t1[:L, :], t2[:L, :], ALU.mult), reads=[t2k, t1k + "b0", t1k + "b1"], writes=[t1k])
                st, sk = ss_r.next()
                rstd_ops(t1, t1k, L, st, sk)
                yn, ynk = yn_r.next()
                A("dve", lambda h: h.scalar_tensor_tensor(yn[:L, :], t1[:L, :], st[:L, :], gb[:L, :], ALU.mult, ALU.mult),
                  reads=[t1k, sk, "gb"], writes=[ynk])
                A("sp", lambda h: h.dma_start(out=CAT[row0:row0 + L, 0:1024], in_=yn[:L, :]), reads=[ynk], writes=[f"CAT:{row0}"], chan=ynk + "st")
                xdw, xdwk = xdw_r.next()
                A("pool", lambda h: h.tensor_tensor(xdw[:L], xdt[:L], dec[:L, :, L - 1:L].broadcast_to([L, 16, 64]), ALU.mult),
                  reads=[xdtk, deck], writes=[xdwk])
                for g in range(2):
                    pst, pstk = PX.next()
                    A("pe", lambda h, pst=pst, g=g: h.matmul(pst[:, :], bcb[:L, g * 128:(g + 1) * 128], xdw[:L, 8 * g:8 * g + 8, :], start=True, stop=True),
                      reads=[bcbk, xdwk], writes=[pstk])
                    A("dve", lambda h, g=g: h.tensor_tensor(
                        state[:, g * 512:(g + 1) * 512].rearrange("p (h d) -> p h d", h=8), state[:, g * 512:(g + 1) * 512].rearrange("p (h d) -> p h d", h=8),
                        edl[:, 8 * g:8 * g + 8].unsqueeze(2).broadcast_to([128, 8, 64]), ALU.mult), reads=["state", "stateb", edlk], writes=["state"])
                    A("dve", lambda h, pst=pst, g=g: h.tensor_tensor(state[:, g * 512:(g + 1) * 512], state[:, g * 512:(g + 1) * 512], pst[:, :], ALU.add),
                      reads=["state", pstk], writes=["state"])

            def state_out(dst):
                for half in range(2):
                    pt, ptk = PM.next()

                    def trs(h, pt=pt, half=half):
                        for c in range(4):
                            cc = half * 4 + c
                            ins = h.transpose(pt[:, c * 128:(c + 1) * 128], state[:, cc * 128:(cc + 1) * 128], identf[:])
                        return ins
                    A("pe", trs, reads=["state"], writes=[ptk])
                    A("act", lambda h, pt=pt, half=half: h.copy(stf[:, half * 4:half * 4 + 4, :], pt[:, :].rearrange("p (c n) -> p c n", c=4)),
                      reads=[ptk], writes=[f"stf{half}"])
                A("sp", lambda h: h.dma_start(out=dst.rearrange("(c p) n -> p c n", p=128), in_=stf[:]), reads=["stf0", "stf1"], writes=["stout"], chan="stout")

            A("pool", lambda h: h.memset(state[:], 0.0), writes=["state"])
            for i in range(NTP):
                chunk(128, i * 128)
            state_out(o_pssm)
            for b in range(BPC):
                A("sp", lambda h, b=b: h.dma_start(out=stf[:], in_=i_ssm[b].rearrange("(c p) n -> p c n", p=128)),
                  reads=["stf0", "stf1", "stout"], writes=["stfin"], chan="stfin")
                for half in range(2):
                    pt, ptk = PM.next()

                    def trs(h, pt=pt, half=half):
                        for c in range(4):
                            ins = h.transpose(pt[:, c * 128:(c + 1) * 128], stf[:, half * 4 + c, :], identf[:])
                        return ins
                    A("pe", trs, reads=["stfin"], writes=[ptk])
                    A("act", lambda h, pt=pt, half=half: h.copy(state[:, half * 512:(half + 1) * 512], pt[:, :]), reads=[ptk, "stfin"], writes=["state"])
                chunk(4, TP + 4 * b)
                state_out(o_sssm[b])


    notri = sb([128, 128])
    A("dve", lambda h: h.tensor_scalar(notri[:], tri[:], -1.0, 1.0, ALU.mult, ALU.add), reads=["tri"], writes=["notri"])
    masks = {}
    for nq_ in (128, 4):
        mo = sb([128, 4 * nq_], BF16)
        mp = sb([128, 4 * nq_], BF16)
        A("dve", lambda h, mo=mo, nq_=nq_: h.tensor_copy(mo[:].rearrange("p (h q) -> p h q", h=4), tri[:, :nq_].unsqueeze(1).broadcast_to([128, 4, nq_])),
          reads=["tri"], writes=[f"mo{nq_}"])
        A("dve", lambda h, mp=mp, nq_=nq_: h.tensor_copy(mp[:].rearrange("p (h q) -> p h q", h=4), notri[:, :nq_].unsqueeze(1).broadcast_to([128, 4, nq_])),
          reads=["notri"], writes=[f"mp{nq_}"])
        masks[nq_] = (mo, mp)

    def swa_phase():
        with Phase():
            esink = sb([128, 16])
            A("sp", lambda h: h.dma_start(out=esink[:], in_=i_sink[0:1, :].partition_broadcast(128)), writes=["esink"], chan="esink")
            A("act", lambda h: h.activation(out=esink[:], in_=esink[:], func=AF.Exp), reads=["esink"], writes=["esink"])
            qb_r = rot(3, [128, 1024], BF16, "wqb")
            qT_r = rot(2, [128, 2048], BF16, "wqT")
            kb_r = rot(3, [128, 256], BF16, "wkb")
            kT_r = rot(5, [128, 4, 128], BF16, "wkT")
            va_r = rot(5, [128, 4, 65], BF16, "wva")
            yo_r = rot(2, [128, 1024], F32, "wyo")
            for t_ in kT_r.t:
                A("dve", lambda h, t_=t_: h.memset(t_[64:128, :, :], 0.0), writes=["wz"])
            for t_ in qT_r.t:
                A("dve", lambda h, t_=t_: h.memset(t_[64:128, :], 0.0), writes=["wz"])
            sc = 0.125
            dfr = Defer()

            def seq(r0, nq, blocks):
                mo, mp = masks[nq]
                qb, qbk = qb_r.next()
                A("pool", lambda h: h.dma_start(out=qb[:nq, :], in_=PROJ[r0:r0 + nq, 2576:3600]), writes=[qbk], chan=qbk)
                qT, qTk = qT_r.next()
                for g in range(2):
                    pt, pk = PT.next()
                    ptb = pt[:].bitcast(BF16)

                    def trq(h, ptb=ptb, g=g):
                        for hh in range(8):
                            hd = g * 8 + hh
                            ins = h.transpose(ptb[:64, hh * nq:(hh + 1) * nq], qb[:nq, hd * 64:(hd + 1) * 64], identb[:nq, :nq])
                        return ins
                    A("pe", trq, reads=[qbk], writes=[pk])
                    A("act", lambda h, ptb=ptb, g=g: h.copy(qT[:64, g * 8 * nq:(g + 1) * 8 * nq], ptb[:64, :8 * nq]), reads=[pk], writes=[qTk + f"g{g}"])
                blk = []
                for (ksrc, vsrc, nk, kind) in blocks:
                    kb, kbk = kb_r.next()
                    A("pool", lambda h, kb=kb, ksrc=ksrc, nk=nk: h.dma_start(out=kb[:nk, :], in_=ksrc), writes=[kbk], chan=kbk)
                    pt, pk = PT.next()
                    ptb = pt[:].bitcast(BF16)

                    def trk(h, ptb=ptb, kb=kb, nk=nk):
                        for j in range(4):
                            ins = h.transpose(ptb[:64, j * 128:j * 128 + nk], kb[:nk, j * 64:(j + 1) * 64], identb[:nk, :nk])
                        return ins
                    A("pe", trk, reads=[kbk], writes=[pk])
                    kT, kTk = kT_r.next()
                    A("act", lambda h, kT=kT, ptb=ptb, nk=nk: h.copy(kT[:64, :, :nk], ptb[:64, :512].rearrange("p (j t) -> p j t", j=4)[:, :, :nk]),
                      reads=[pk], writes=[kTk])
                    va, vak = va_r.next()
                    A("pool", lambda h, va=va: h.memset(va[:, :, 64:65], 1.0), writes=[vak + "o"])
                    A("pool", lambda h, va=va, vsrc=vsrc, nk=nk: h.dma_start(out=va[:nk, :, 0:64], in_=vsrc.rearrange("p (j d) -> p j d", j=4)),
                      writes=[vak], chan=vak)
                    blk.append((kT, kTk, va, vak, nk, kind))
                yo, yok = yo_r.next()
                for j in range(4):
                    Es = []
                    for bi, (kT, kTk, va, vak, nk, kind) in enumerate(blk):
                        m = mo if kind == "own" else mp
                        Es.append(attn_step(kT[:, j, :nk], [kTk, "wz"], qT[:, 4 * j * nq:(4 * j + 4) * nq], [qTk + "g0", qTk + "g1"], nk, 4 * nq, sc,
                                            None, [], m[:nk, :4 * nq], None, None, None))

                    def tail(Es=Es, j=j):
                        po, pok = PX.next()

                        def pvf(h):
                            for hq in range(4):
                                for bi, (kT, kTk, va, vak, nk, kind) in enumerate(blk):
                                    ins = h.matmul(po[:nq, hq * 65:(hq + 1) * 65], Es[bi][0][:nk, hq * nq:(hq + 1) * nq], va[:nk, j, :],
                                                   start=(bi == 0), stop=(bi == len(blk) - 1))
                            return ins
                        A("pe", pvf, reads=[e[1] for e in Es] + [b_[3] for b_ in blk] + [b_[3] + "o" for b_ in blk], writes=[pok])
                        rd, rdk = rd_r.next()
                        A("dve", lambda h: h.tensor_tensor(
                            rd[:nq, 0:4].unsqueeze(2), po[:nq, :260].rearrange("p (h e) -> p h e", h=4)[:, :, 64:65],
                            esink[:nq, 4 * j:4 * j + 4].unsqueeze(2), ALU.add), reads=[pok, "esink"], writes=[rdk])
                        A("dve", lambda h: h.reciprocal(rd[:nq, 0:4], rd[:nq, 0:4]), reads=[rdk], writes=[rdk])
                        for hq in range(4):
                            hd = 4 * j + hq
                            A("dve", lambda h, hq=hq, hd=hd: h.tensor_scalar(
                                yo[:nq, hd * 64:(hd + 1) * 64], po[:nq, hq * 65:hq * 65 + 64], rd[:nq, hq:hq + 1], None, ALU.mult),
                              reads=[pok, rdk], writes=[yok + f"h{hd}"])
                        if j == 3:
                            A("sp", lambda h: h.dma_start(out=CAT[r0:r0 + nq, 1024:2048], in_=yo[:nq, :]),
                              reads=[yok + f"h{hd}" for hd in range(16)], writes=[f"CATa:{r0}"], chan=yok + "st")
                    dfr.push(tail)

            for i in range(NTP):
                blocks = []
                if i > 0:
                    blocks.append((PROJ[(i - 1) * 128:i * 128, 3600:3856], PROJ[(i - 1) * 128:i * 128, 3856:4112], 128, "prev"))
                blocks.append((PROJ[i * 128:(i + 1) * 128, 3600:3856], PROJ[i * 128:(i + 1) * 128, 3856:4112], 128, "own"))
                seq(i * 128, 128, blocks)
            for b in range(BPC):
                r = TP + 4 * b
                seq(r, 4, [(i_cswk[b], i_cswv[b], 128, "prev"), (PROJ[r:r + 4, 3600:3856], PROJ[r:r + 4, 3856:4112], 4, "own")])
            dfr.flush()


    trib = sb([128, 128], BF16)
    A("dve", lambda h: h.tensor_copy(trib[:], tri[:]), reads=["tri"], writes=["trib"])
    iotaf = sb([128, 1])
    A("sp", lambda h: h.dma_start(out=iotaf[:], in_=i_iota[:, :]), writes=["iotaf"], chan="iotaf")

    def odd_inproj():
        with Phase():
            new_hT()
            norm_T(X, i_nmix[1:2, :], tiles, hT, "hT")
            linear_to_dram(hT, "hT", i_wio, ODD_PROJ, PROJ, tiles)
        with Phase():
            fbb = sb([128, 16])
            A("sp", lambda h: h.dma_start(out=fbb[:], in_=i_fb[0:1, :].partition_broadcast(128)), writes=["fbb"], chan="fbb")
            f_r = rot(2, [128, 16], F32, "of")
            d1_r = rot(2, [128, 16], F32, "od1")
            d2_r = rot(2, [128, 16], F32, "od2")
            for (r0, rows) in tiles:
                f, fk = f_r.next()
                d1, d1k = d1_r.next()
                d2, d2k = d2_r.next()
                A("sp", lambda h, f=f, r0=r0, rows=rows: h.dma_start(out=f[:rows, :], in_=PROJ[r0:r0 + rows, 1536:1552]), writes=[fk], chan=fk)
                A("dve", lambda h, f=f, rows=rows: h.tensor_tensor(f[:rows, :], f[:rows, :], fbb[:rows, :], ALU.add), reads=[fk, "fbb"], writes=[fk])
                A("dve", lambda h, f=f, d1=d1, rows=rows: h.tensor_scalar(d1[:rows, :], f[:rows, :], -1.0, None, ALU.mult), reads=[fk], writes=[d1k])
                A("dve", lambda h, f=f, d2=d2, d1=d1, rows=rows: h.tensor_tensor(d2[:rows, :], d1[:rows, :], f[:rows, :], ALU.min), reads=[fk, d1k], writes=[d2k])
                A("act", lambda h, d2=d2, rows=rows: h.activation(out=d2[:rows, :], in_=d2[:rows, :], func=AF.Exp), reads=[d2k], writes=[d2k])
                A("dve", lambda h, d2=d2, rows=rows: h.tensor_scalar(d2[:rows, :], d2[:rows, :], 1.0, None, ALU.add), reads=[d2k], writes=[d2k])
                A("act", lambda h, d2=d2, rows=rows: h.activation(out=d2[:rows, :], in_=d2[:rows, :], func=AF.Ln), reads=[d2k], writes=[d2k])
                A("dve", lambda h, d1=d1, rows=rows: h.tensor_scalar(d1[:rows, :], d1[:rows, :], 0.0, None, ALU.max), reads=[d1k], writes=[d1k])
                A("dve", lambda h, d1=d1, d2=d2, rows=rows: h.tensor_tensor(d1[:rows, :], d1[:rows, :], d2[:rows, :], ALU.add), reads=[d1k, d2k], writes=[d1k])
                A("dve", lambda h, f=f, d1=d1, rows=rows: h.tensor_scalar(f[:rows, :], d1[:rows, :], -1.0, None, ALU.mult), reads=[d1k, fk], writes=[fk])
                dst = o_pfl[r0:r0 + rows, :] if r0 < TP else o_sfl[:, :]
                A("sp", lambda h, f=f, dst=dst, rows=rows: h.dma_start(out=dst, in_=f[:rows, :]), reads=[fk], writes=[f"lf:{r0}"], chan=fk + "st")
            for t0 in range(0, TP, 1024):
                tw = min(1024, TP - t0)
                dram_copy(o_pfk[t0:t0 + tw, :], PROJ[t0:t0 + tw, 1024:1280], f"opfk{t0}")
                dram_copy(o_pfv[t0:t0 + tw, :], PROJ[t0:t0 + tw, 1280:1536], f"opfv{t0}")
            dram_copy(o_sfk[:, :], PROJ[TP:TT, 1024:1280], "osfk")
            dram_copy(o_sfv[:, :], PROJ[TP:TT, 1280:1536], "osfv")

    def fox_prompt():
        with Phase():
            kT = sb([128, 4, TP], BF16)
            vaug = sb([128, NTP, 4, 65], BF16)
            cum = sb([128, NTP, 16])
            cend = sb([128, NTP, 16])
            A("pool", lambda h: h.memset(vaug[:, :, :, 64:65], 1.0), writes=["fvo"])
            kb_r = rot(3, [128, 256], BF16, "fkb")
            A("dve", lambda h: h.memset(kT[64:128, :, :], 0.0), writes=["fz"])
            lf_r = rot(2, [128, 16], F32, "flf")
            for i in range(NTP):
                kb, kbk = kb_r.next()
                A("pool", lambda h, kb=kb, i=i: h.dma_start(out=kb[:], in_=PROJ[i * 128:(i + 1) * 128, 1024:1280]), writes=[kbk], chan=kbk)
                pt, pk = PT.next()
                ptb = pt[:].bitcast(BF16)

                def trk(h, ptb=ptb, kb=kb):
                    for j in range(4):
                        ins = h.transpose(ptb[:64, j * 128:(j + 1) * 128], kb[:, j * 64:(j + 1) * 64], identb[:])
                    return ins
                A("pe", trk, reads=[kbk], writes=[pk])
                A("act", lambda h, ptb=ptb, i=i: h.copy(kT[:64, :, i * 128:(i + 1) * 128], ptb[:64, :512].rearrange("p (j t) -> p j t", j=4)),
                  reads=[pk], writes=[f"fkT{i}"])
                A("pool", lambda h, i=i: h.dma_start(out=vaug[:, i, :, 0:64], in_=PROJ[i * 128:(i + 1) * 128, 1280:1536].rearrange("p (j d) -> p j d", j=4)),
                  writes=[f"fva{i}"], chan=f"fva{i % 4}")
                lf, lfk = lf_r.next()
                A("sp", lambda h, lf=lf, i=i: h.dma_start(out=lf[:], in_=o_pfl[i * 128:(i + 1) * 128, :]), writes=[lfk], chan=lfk)
                pc, pck = PT.next()
                A("pe", lambda h, pc=pc, lf=lf: h.matmul(pc[:, 0:16], tri[:], lf[:], start=True, stop=True), reads=[lfk], writes=[pck])
                pl, plk = PT.next()
                A("pe", lambda h, pl=pl, lf=lf: h.matmul(pl[:, 0:16], ones[:], lf[:], start=True, stop=True), reads=[lfk], writes=[plk])
                if i == 0:
                    A("dve", lambda h, pc=pc: h.tensor_copy(cum[:, 0, :], pc[:, 0:16]), reads=[pck], writes=["fcum0"])
                    A("dve", lambda h, pl=pl: h.tensor_copy(cend[:, 0, :], pl[:, 0:16]), reads=[plk], writes=["fcend0"])
                else:
                    A("dve", lambda h, pc=pc, i=i: h.tensor_tensor(cum[:, i, :], pc[:, 0:16], cend[:, i - 1, :], ALU.add),
                      reads=[pck, f"fcend{i - 1}"], writes=[f"fcum{i}"])
                    A("dve", lambda h, pl=pl, i=i: h.tensor_tensor(cend[:, i, :], pl[:, 0:16], cend[:, i - 1, :], ALU.add),
                      reads=[plk, f"fcend{i - 1}"], writes=[f"fcend{i}"])
            qb_r = rot(3, [128, 1024], BF16, "fqb")
            qTG = sb([128, 16 * 512], BF16)
            A("dve", lambda h: h.memset(qTG[64:128, :], 0.0), writes=["fz2"])
            yG = sb([128, 4, 1024])
            NG = (NTP + 3) // 4
            dfr = Defer()
            for G in range(NG):
                qbs = list(range(4 * G, min(4 * G + 4, NTP)))
                nqb = len(qbs)
                lastJ = qbs[-1]
                biasG = sb([128, NTP, 16]) if G == 0 else biasG
                for qi, qb_ in enumerate(qbs):
                    qb, qbk = qb_r.next()
                    A("pool", lambda h, qb=qb, qb_=qb_: h.dma_start(out=qb[:], in_=PROJ[qb_ * 128:(qb_ + 1) * 128, 0:1024]), writes=[qbk], chan=qbk)
                    for g in range(2):
                        pt, pk = PT.next()
                        ptb = pt[:].bitcast(BF16)

                        def trq(h, ptb=ptb, g=g, qb=qb):
                            for hh in range(8):
                                hd = g * 8 + hh
                                ins = h.transpose(ptb[:64, hh * 128:(hh + 1) * 128], qb[:, hd * 64:(hd + 1) * 64], identb[:])
                            return ins
                        A("pe", trq, reads=[qbk], writes=[pk])
                        A("act", lambda h, ptb=ptb, g=g, qi=qi: h.copy(
                            qTG[:64, :].rearrange("p (h c) -> p h c", h=16)[:, g * 8:(g + 1) * 8, qi * 128:(qi + 1) * 128],
                            ptb[:64, :].rearrange("p (h t) -> p h t", h=8)), reads=[pk], writes=[f"fqT{qi}g{g}"])
                for J in range(lastJ + 1):
                    A("dve", lambda h, J=J, lastJ=lastJ: h.tensor_tensor(biasG[:, J, :], cend[:, lastJ, :], cum[:, J, :], ALU.subtract),
                      reads=[f"fcend{lastJ}", f"fcum{J}"], writes=[f"fbias{J}"])
                qkeys = [f"fqT{qi}g{g}" for qi in range(nqb) for g in range(2)]
                for hd in range(16):
                    kvh = hd // 4
                    for J in range(lastJ + 1):
                        qlo = max(0, J - 4 * G)
                        ncols = (nqb - qlo) * 128
                        E, ek = attn_step(kT[:, kvh, J * 128:(J + 1) * 128], [f"fkT{J}", "fz"],
                                          qTG[:, hd * 512 + qlo * 128:hd * 512 + nqb * 128], qkeys + ["fz2"], 128, ncols, 0.125,
                                          biasG[:, J, hd:hd + 1], [f"fbias{J}"], None, None, None, None, SB=PX)
                        if J >= 4 * G:
                            A("dve", lambda h, E=E: h.tensor_tensor(E[:, 0:128], E[:, 0:128], trib[:], ALU.mult), reads=[ek], writes=[ek])

                        def tail(E=E, ek=ek, J=J, qlo=qlo, kvh=kvh):
                            def pvf(h):
                                for qi in range(qlo, nqb):
                                    ins = h.matmul(PM.t[qi][:, 0:65], E[:, (qi - qlo) * 128:(qi - qlo + 1) * 128], vaug[:, J, kvh, :],
                                                   start=(J == 0), stop=(J == 4 * G + qi))
                                return ins
                            A("pe", pvf, reads=[ek, f"fva{J}", "fvo"], writes=[f"pM{qi}" for qi in range(qlo, nqb)])
                        dfr.push(tail)
                    dfr.flush()
                    for qi in range(nqb):
                        rd, rdk = rd_r.next()
                        A("dve", lambda h, rd=rd, qi=qi: h.reciprocal(rd[:, 0:1], PM.t[qi][:, 64:65]), reads=[f"pM{qi}"], writes=[rdk])
                        A("dve", lambda h, rd=rd, qi=qi, hd=hd: h.tensor_scalar(
                            yG[:, qi, hd * 64:(hd + 1) * 64], PM.t[qi][:, 0:64], rd[:, 0:1], None, ALU.mult),
                          reads=[f"pM{qi}", rdk], writes=[f"fyG{qi}h{hd}"])
                for qi, qb_ in enumerate(qbs):
                    A("sp", lambda h, qi=qi, qb_=qb_: h.dma_start(out=CAT[qb_ * 128:(qb_ + 1) * 128, 0:1024], in_=yG[:, qi, :]),
                      reads=[f"fyG{qi}h{hd}" for hd in range(16)], writes=[f"CATf:{qb_}"], chan=f"fyst{qi}")

    def fox_sample():
        with Phase():
            pti = sb([128, NPG], I32)
            ptc = sb([128, 1], I32)
            ptf = sb([128, NPG])
            idx_r = rot(2, [128, NPG], I32, "gidx")
            lfT_r = rot(2, [128, 2048], F32, "glfT")
            lfp = sb([128, NPG, 16])
            cumw = sb([128, NPG, 16])
            inc = [sb([128, NPG, 16]), sb([128, NPG, 16])]
            tot = sb([128, NPG, 16])
            biasP_r = rot(2, [128, NPG, 16], F32, "gbiasP")
            cref = sb([128, 16])
            lnew = sb([128, 16])
            cnew = sb([128, 16])
            biasN_r = rot(2, [128, 16], F32, "gbiasN")
            qf = sb([128, 1024])
            qb = sb([128, 1024], BF16)
            qTb_r = rot(2, [128, 64], BF16, "gqT")
            kvf_r = rot(4, [128, 2, 512], F32, "gkvf")
            kb_r = rot(3, [128, 2, 256], BF16, "gkb")
            kT_r = rot(3, [128, 8, 128], BF16, "gkT")
            va_r = rot(4, [128, 2, 4, 65], BF16, "gva")
            ar_r = rot(3, [128, 128], F32, "gar")
            E_g = rot(4, [128, 128], BF16, "gE")
            o_r = rot(2, [128, 64], F32, "go")
            for va_t in va_r.t:
                A("pool", lambda h, va_t=va_t: h.memset(va_t[:, :, :, 64:65], 1.0), writes=["gvao"])
            dfr = Defer()
            for b in range(BPC):
                r = TP + 4 * b
                idx, idxk = idx_r.next()
                A("sp", lambda h, b=b: h.dma_start(out=pti[:], in_=i_pt[b:b + 1, :].partition_broadcast(128)), writes=["pti"], chan="pti")
                A("sp", lambda h, b=b: h.dma_start(out=ptc[:NPG, :], in_=i_pt[b, :].rearrange("(j o) -> j o", o=1)), writes=["ptc"], chan="ptc")
                A("dve", lambda h: h.tensor_copy(ptf[:], pti[:]), reads=["pti"], writes=["ptf"])
                A("dve", lambda h: h.tensor_scalar(ptf[:], ptf[:], 128.0, iotaf[:, 0:1], ALU.mult, ALU.add), reads=["ptf", "iotaf"], writes=["ptf"])
                A("dve", lambda h, idx=idx: h.tensor_copy(idx[:], ptf[:]), reads=["ptf"], writes=[idxk])
                lfT, lfTk = lfT_r.next()
                A("pool", lambda h, lfT=lfT: h.indirect_dma_start(out=lfT[:NPG, :], out_offset=None, in_=i_flp[:, :],
                                                                  in_offset=bass.IndirectOffsetOnAxis(ap=ptc[:NPG, 0:1], axis=0)),
                  reads=["ptc"], writes=[lfTk], chan=lfTk)
                HB = max(1, min(16, 512 // NPG))
                for g0 in range(0, 16, HB):
                    pm, pmk = PM.next()

                    def trl(h, pm=pm, g0=g0, lfT=lfT):
                        for hh in range(HB):
                            ins = h.transpose(pm[:, hh * NPG:(hh + 1) * NPG],
                                              lfT[:NPG, :].rearrange("p (k h) -> p k h", h=16)[:, :, g0 + hh], identf[:NPG, :NPG])
                        return ins
                    A("pe", trl, reads=[lfTk], writes=[pmk])
                    A("act", lambda h, pm=pm, g0=g0: h.copy(lfp[:].rearrange("p j h -> p h j")[:, g0:g0 + HB, :],
                                                            pm[:, :HB * NPG].rearrange("p (h j) -> p h j", h=HB)),
                      reads=[pmk], writes=[f"lfp{g0}"])
                lkeys = [f"lfp{g0}" for g0 in range(0, 16, HB)]
                A("sp", lambda h, b=b: h.dma_start(out=lnew[:4, :], in_=o_sfl[4 * b:4 * b + 4, :]), writes=["lnew"], chan="lnew")
                A("sp", lambda h, r=r: h.dma_start(out=qf[:4, :], in_=PROJ[r:r + 4, 0:1024]), writes=["gqf"], chan="gqf")
                A("act", lambda h: h.copy(qb[:4, :], qf[:4, :]), reads=["gqf"], writes=["gqb"])
                pt, pk = PT.next()
                ptb = pt[:].bitcast(BF16)

                def trq(h, ptb=ptb):
                    for hd in range(16):
                        ins = h.transpose(ptb[:64, hd * 4:(hd + 1) * 4], qb[:4, hd * 64:(hd + 1) * 64], identb[:4, :4])
                    return ins
                A("pe", trq, reads=["gqb"], writes=[pk])
                qTb, qTk = qTb_r.next()
                A("act", lambda h, ptb=ptb, qTb=qTb: h.copy(qTb[:64, :], ptb[:64, 0:64]), reads=[pk], writes=[qTk])
                CW = NPG * 16
                for c0 in range(0, CW, 512):
                    cw = min(512, CW - c0)
                    pm, pmk = PM.next()
                    A("pe", lambda h, pm=pm, c0=c0, cw=cw: h.matmul(pm[:, :cw], tri[:], lfp[:].rearrange("p a b -> p (a b)")[:, c0:c0 + cw],
                                                                  start=True, stop=True), reads=lkeys, writes=[pmk])
                    A("dve", lambda h, pm=pm, c0=c0, cw=cw: h.tensor_copy(cumw[:].rearrange("p a b -> p (a b)")[:, c0:c0 + cw], pm[:, :cw]),
                      reads=[pmk], writes=[f"cumw{c0}"])
                    pm2, pm2k = PM.next()
                    A("pe", lambda h, pm2=pm2, c0=c0, cw=cw: h.matmul(pm2[:, :cw], ones[:], lfp[:].rearrange("p a b -> p (a b)")[:, c0:c0 + cw],
                                                                    start=True, stop=True), reads=lkeys, writes=[pm2k])
                    A("dve", lambda h, pm2=pm2, c0=c0, cw=cw: h.tensor_copy(tot[:].rearrange("p a b -> p (a b)")[:, c0:c0 + cw], pm2[:, :cw]),
                      reads=[pm2k], writes=[f"tot{c0}"])
                ck = [f"cumw{c0}" for c0 in range(0, CW, 512)]
                tk_ = [f"tot{c0}" for c0 in range(0, CW, 512)]
                A("dve", lambda h: h.tensor_copy(inc[0][:], tot[:]), reads=tk_, writes=["inc0"])
                cur = 0
                k_ = 1
                while k_ < NPG:
                    nx = 1 - cur
                    A("dve", lambda h, cur=cur, nx=nx, k_=k_: h.tensor_copy(inc[nx][:, :k_, :], inc[cur][:, :k_, :]), reads=[f"inc{cur}"], writes=[f"inc{nx}"])
                    A("dve", lambda h, cur=cur, nx=nx, k_=k_: h.tensor_tensor(inc[nx][:, k_:, :], inc[cur][:, k_:, :], inc[cur][:, :NPG - k_, :], ALU.add),
                      reads=[f"inc{cur}"], writes=[f"inc{nx}"])
                    cur = nx
                    k_ *= 2
                incf = inc[cur]
                ik = f"inc{cur}"
                A("dve", lambda h, incf=incf: h.tensor_tensor(cumw[:], cumw[:], incf[:], ALU.add), reads=ck + [ik], writes=["cumP"])
                A("dve", lambda h: h.tensor_tensor(cumw[:], cumw[:], tot[:], ALU.subtract), reads=tk_ + ["cumP"], writes=["cumP"])
                pl, plk = PT.next()
                A("pe", lambda h, pl=pl: h.matmul(pl[:, 0:16], ones[:4, :], lnew[:4, :], start=True, stop=True), reads=["lnew"], writes=[plk])
                A("dve", lambda h, pl=pl, incf=incf: h.tensor_tensor(cref[:], pl[:, 0:16], incf[:, NPG - 1, :], ALU.add), reads=[plk, ik], writes=["cref"])
                pc, pck = PT.next()
                A("pe", lambda h, pc=pc: h.matmul(pc[:4, 0:16], tri[:4, :4], lnew[:4, :], start=True, stop=True), reads=["lnew"], writes=[pck])
                A("dve", lambda h, pc=pc, incf=incf: h.tensor_tensor(cnew[:4, :], pc[:4, 0:16], incf[:4, NPG - 1, :], ALU.add), reads=[pck, ik], writes=["cnew"])
                biasN, bNk = biasN_r.next()
                biasP, bPk = biasP_r.next()
                A("dve", lambda h, biasN=biasN: h.tensor_tensor(biasN[:4, :], cref[:4, :], cnew[:4, :], ALU.subtract), reads=["cref", "cnew"], writes=[bNk])
                A("dve", lambda h, biasP=biasP: h.tensor_tensor(biasP[:], cref[:].unsqueeze(1).broadcast_to([128, NPG, 16]), cumw[:], ALU.subtract),
                  reads=["cref", "cumP"], writes=[bPk])
                PGS = cfg.get("PGS", 2) if NPG % 2 == 0 else 1
                steps = [(j0, PGS, False) for j0 in range(0, NPG, PGS)] + [(NPG, 1, True)]
                for (j0, npg, new) in steps:
                    nk = 4 if new else 128
                    kvf, kvk = kvf_r.next()
                    if new:
                        A("sp", lambda h, kvf=kvf, r=r: h.dma_start(out=kvf[:4, 0, :], in_=PROJ[r:r + 4, 1024:1536]),
                          writes=[kvk + "p0", kvk + "p1"], chan=kvk)
                    else:
                        for pg in range(npg):
                            A("pool", lambda h, kvf=kvf, j=j0 + pg, pg=pg, idx=idx: h.indirect_dma_start(
                                out=kvf[:, pg, :], out_offset=None, in_=i_fkv[:, :], in_offset=bass.IndirectOffsetOnAxis(ap=idx[:, j:j + 1], axis=0)),
                              reads=[idxk], writes=[kvk + f"p{pg}"], chan=kvk + f"p{pg}")
                    kvks = [kvk + "p0", kvk + "p1"] if new else [kvk + f"p{pg}" for pg in range(npg)]
                    kb, kbk = kb_r.next()
                    A("dve", lambda h, kb=kb, kvf=kvf, nk=nk, npg=npg: h.tensor_copy(kb[:nk, :npg, :], kvf[:nk, :npg, 0:256]), reads=kvks, writes=[kbk])
                    pt, pk = PT.next()
                    ptb = pt[:].bitcast(BF16)

                    def trk(h, ptb=ptb, kb=kb, nk=nk, npg=npg):
                        for pg in range(npg):
                            for jj in range(4):
                                c = (pg * 4 + jj) * 128
                                ins = h.transpose(ptb[:64, c:c + nk], kb[:nk, pg, jj * 64:(jj + 1) * 64], identb[:nk, :nk])
                        return ins
                    A("pe", trk, reads=[kbk], writes=[pk])
                    kT, kTk = kT_r.next()
                    A("act", lambda h, kT=kT, ptb=ptb, nk=nk, npg=npg: h.copy(
                        kT[:64, :npg * 4, :nk], ptb[:64, :npg * 512].rearrange("p (j t) -> p j t", j=npg * 4)[:, :, :nk]),
                      reads=[pk], writes=[kTk])
                    va, vak = va_r.next()
                    A("act", lambda h, va=va, kvf=kvf, nk=nk, npg=npg: h.copy(
                        va[:nk, :npg, :, 0:64], kvf[:nk, :npg, 256:512].rearrange("p g (j d) -> p g j d", j=4)),
                      reads=kvks + ["gvao"], writes=[vak])
                    ps_, psk = PX.next()

                    def ms(h, ps_=ps_, kT=kT, nk=nk, qTb=qTb, npg=npg):
                        for pg in range(npg):
                            for kvh in range(4):
                                ins = h.matmul(ps_[:nk, pg * 64 + kvh * 16:pg * 64 + (kvh + 1) * 16], kT[:64, pg * 4 + kvh, :nk],
                                               qTb[:64, kvh * 16:(kvh + 1) * 16], start=True, stop=True)
                        return ins
                    A("pe", ms, reads=[kTk, qTk], writes=[psk])
                    ar, ark = ar_r.next()
                    if new:
                        bsrc = biasN[:4, :].unsqueeze(1).unsqueeze(3).broadcast_to([4, 1, 16, 4])
                    else:
                        bsrc = biasP[:, j0:j0 + npg, :].unsqueeze(3).broadcast_to([128, npg, 16, 4])
                    A("dve", lambda h, ar=ar, ps_=ps_, bsrc=bsrc, nk=nk, npg=npg: h.scalar_tensor_tensor(
                        ar[:nk, :npg * 64].rearrange("p (g h q) -> p g h q", g=npg, h=16),
                        ps_[:nk, 0:npg * 64].rearrange("p (g h q) -> p g h q", g=npg, h=16), 0.125, bsrc, ALU.mult, ALU.add),
                      reads=[psk, bNk if new else bPk], writes=[ark])
                    E, ek = E_g.next()
                    A("act", lambda h, E=E, ar=ar, nk=nk, npg=npg: h.activation(out=E[:nk, :npg * 64], in_=ar[:nk, :npg * 64], func=AF.Exp),
                      reads=[ark], writes=[ek])
                    if new:
                        A("dve", lambda h, E=E: h.tensor_tensor(E[:4, :64].rearrange("p (h q) -> p h q", h=16), E[:4, :64].rearrange("p (h q) -> p h q", h=16),
                                                                trib[:4, :4].unsqueeze(1).broadcast_to([4, 16, 4]), ALU.mult), reads=[ek], writes=[ek])

                    def tail(E=E, ek=ek, va=va, vak=vak, nk=nk, j0=j0, npg=npg, new=new, r=r):
                        def pvf(h):
                            for kvh in range(4):
                                for pg in range(npg):
                                    j = j0 + pg
                                    ins = h.matmul(PM.t[kvh][:16, 0:65], E[:nk, pg * 64 + kvh * 16:pg * 64 + (kvh + 1) * 16], va[:nk, pg, kvh, :],
                                                   start=(j == 0), stop=(j == NPG))
                            return ins
                        A("pe", pvf, reads=[ek, vak], writes=["pM0", "pM1", "pM2", "pM3"])
                        if new:
                            for kvh in range(4):
                                rd, rdk = rd_r.next()
                                A("dve", lambda h, rd=rd, kvh=kvh: h.reciprocal(rd[:16, 0:1], PM.t[kvh][:16, 64:65]), reads=[f"pM{kvh}"], writes=[rdk])
                                o, ok = o_r.next()
                                A("dve", lambda h, o=o, rd=rd, kvh=kvh: h.tensor_scalar(o[:16, :], PM.t[kvh][:16, 0:64], rd[:16, 0:1], None, ALU.mult),
                                  reads=[f"pM{kvh}", rdk], writes=[ok])
                                for hl in range(4):
                                    hd = 4 * kvh + hl
                                    A("sp", lambda h, o=o, hl=hl, hd=hd: h.dma_start(out=CAT[r:r + 4, hd * 64:(hd + 1) * 64], in_=o[hl * 4:(hl + 1) * 4, :]),
                                      reads=[ok], writes=[f"CATs:{r}:{hd}"], chan=f"gst{hl}")
                    dfr.push(tail)
                dfr.flush()

    A("sp", lambda h: h.dma_start(out=X[0:TP, :], in_=i_xp[:, :]), writes=["Xinit"], chan="xi0")
    A("sp", lambda h: h.dma_start(out=X[TP:TT, :], in_=i_xs[:, :]), writes=["Xinit2"], chan="xi1")
    P.barrier()

    STAGE = cfg.get("STAGE", 99)
    for l in range(2):
        if STAGE == 1:
            xattn(0)
            break
        if l == 0:
            even_inproj()
            conv_phase()
            ssd_phase()
            swa_phase()
            proj_residual(CAT, 2048, i_woe, tiles)
        else:
            odd_inproj()
            fox_prompt()
            fox_sample()
            proj_residual(CAT, 1024, i_woo, tiles)
        xattn(l)
        with Phase():
            ffn(l)
    with Phase():
        final_norm()

    P.finalize()
    es.close()
    return nc, dict(nsems=P.nsems, maxval=P.maxval, nops=len(P.ops))


def make_in_maps(cfg, inputs):
    SEQ, DEC_BATCH, PAST = cfg["SEQ"], cfg["DEC_BATCH"], cfg["PAST"]
    BPC = DEC_BATCH // NCORES
    f = lambda a: np.ascontiguousarray(np.asarray(a))
    g = inputs
    npool = g["cache_fox_k"].shape[1]
    shared = {
        "fox_kv": np.concatenate([np.asarray(g["cache_fox_k"][0]).reshape(npool * 128, 256),
                                  np.asarray(g["cache_fox_v"][0]).reshape(npool * 128, 256)], axis=1),
        "fox_lp": f(g["cache_fox_logf"][0]).reshape(npool, 2048),
        "norm_mix": f(g["norm_mix"]), "norm_xa": f(g["norm_xa"]), "norm_mem": f(g["norm_mem"]),
        "norm_ffn": f(g["norm_ffn"]), "w_in_even": f(g["w_in_even"][0]), "conv_w": f(g["conv_w"][0]),
        "conv_b": f(g["conv_b"]), "dt_bias": f(g["dt_bias"]), "a_log": f(g["a_log"]), "d_skip": f(g["d_skip"]),
        "ssm_norm": f(g["ssm_norm"]), "swa_sink": f(g["swa_sink"]), "w_out_even": f(g["w_out_even"][0]),
        "w_in_odd": f(g["w_in_odd"][0]), "fox_fb": f(g["fox_fb"]), "w_out_odd": f(g["w_out_odd"][0]),
        "w_xq": f(g["w_xq"]), "w_xk": f(g["w_xk"]), "w_xv": f(g["w_xv"]), "w_xo": f(g["w_xo"]),
        "w_ffn_in": f(g["w_ffn_in"]), "w_ffn_out": f(g["w_ffn_out"]),
        "norm_final": f(g["norm_final"]).reshape(1, D),
        "iota": np.arange(128, dtype=np.float32).reshape(128, 1),
    }
    maps = []
    for c in range(NCORES):
        b0, b1 = c * BPC, (c + 1) * BPC
        sq = c % 4
        m = dict(shared)
        m["xp"] = f(g["x_prompt"][sq])
        m["xs"] = f(g["x_sample"][b0:b1]).reshape(BPC * 4, D)
        m["state_ssm"] = f(g["state_ssm"][0, b0:b1]).reshape(BPC, 1024, 128)
        m["state_conv"] = f(g["state_conv"][0, b0:b1])
        m["cswk"] = f(g["cache_swa_k"][0, b0:b1]).reshape(BPC, 128, 256)
        m["cswv"] = f(g["cache_swa_v"][0, b0:b1]).reshape(BPC, 128, 256)
        m["cmk"] = f(g["cache_mem_k"][:, b0:b1]).reshape(2, BPC, 256, 512)
        m["cmv"] = f(g["cache_mem_v"][:, b0:b1]).reshape(2, BPC, 256, 512)
        m["pt"] = f(g["page_table"][b0:b1]).astype(np.int32)
        m["memp"] = f(g["mem_prompt"][sq])
        maps.append(m)
    return maps


def gather_outputs(cfg, res):
    SEQ, DEC_BATCH = cfg["SEQ"], cfg["DEC_BATCH"]
    BPC = DEC_BATCH // NCORES
    R = res
    st = lambda name, cores: np.stack([np.asarray(R[c][name]) for c in cores])
    cat = lambda name: np.concatenate([np.asarray(R[c][name]) for c in range(NCORES)], axis=0)
    pc = range(4)
    y_p = st("y_p", pc)
    y_s = cat("y_s").reshape(DEC_BATCH, 4, D)
    p_ssm = st("p_ssm", pc).reshape(1, 4, 16, 64, 128)
    p_conv = st("p_conv", pc).reshape(1, 4, 3, 1536)
    p_swk = st("p_swk", pc).reshape(1, 4, 128, 4, 64)
    p_swv = st("p_swv", pc).reshape(1, 4, 128, 4, 64)
    p_fk = st("p_fk", pc).reshape(1, 4, SEQ, 4, 64)
    p_fv = st("p_fv", pc).reshape(1, 4, SEQ, 4, 64)
    p_fl = st("p_fl", pc).reshape(1, 4, SEQ, 16)
    p_mk = np.swapaxes(st("p_mk", pc), 0, 1).reshape(2, 4, 256, 4, 128)
    p_mv = np.swapaxes(st("p_mv", pc), 0, 1).reshape(2, 4, 256, 4, 128)
    s_ssm = cat("s_ssm").reshape(1, DEC_BATCH, 16, 64, 128)
    s_conv = cat("s_conv").reshape(1, DEC_BATCH, 3, 1536)
    s_swk = cat("s_swk").reshape(1, DEC_BATCH, 128, 4, 64)
    s_swv = cat("s_swv").reshape(1, DEC_BATCH, 128, 4, 64)
    s_fk = cat("s_fk").reshape(1, DEC_BATCH, 4, 4, 64)
    s_fv = cat("s_fv").reshape(1, DEC_BATCH, 4, 4, 64)
    s_fl = cat("s_fl").reshape(1, DEC_BATCH, 4, 16)
    outs = (y_p, y_s, p_ssm, p_conv, p_swk, p_swv, p_fk, p_fv, p_fl, p_mk, p_mv,
            s_ssm, s_conv, s_swk, s_swv, s_fk, s_fv, s_fl)
    return tuple(np.ascontiguousarray(o, dtype=np.float32) for o in outs)


def run(cfg, inputs, debug=()):
    nc, info = build(cfg, debug)
    maps = make_in_maps(cfg, inputs)
    res = run_bass_kernel_spmd(nc, maps, core_ids=list(range(NCORES)))
    return res.results, info


def kernel(**inputs):
    res, _ = run(CFG, inputs)
    return gather_outputs(CFG, res)
```

```python
import numpy as np
import concourse.bass as bass
import concourse.mybir as mybir
from concourse.bass_utils import run_bass_kernel_spmd
from contextlib import ExitStack

F32 = mybir.dt.float32
BF16 = mybir.dt.bfloat16
I32 = mybir.dt.int32
ALU = mybir.AluOpType
AF = mybir.ActivationFunctionType
AX = mybir.AxisListType

CFG = dict(SEQ=4096, DEC_BATCH=128, PAST=8192)
NCORES = 8
D = 1024
EVEN_PROJ = 4112
ODD_PROJ = 1552
FFN_H = 2816
NEG = -30000.0


class Op:
    __slots__ = ("eng", "fn", "deps", "chan", "sem", "val", "needed", "idx")


class Prog:
    ENGS = ["pe", "act", "dve", "pool", "sp"]

    def __init__(self, nc, es):
        self.nc = nc
        self.es = es
        self.ops = []
        self.lastw = {}
        self.readers = {}
        self.chan_last = {}
        self.pending = {e: set() for e in self.ENGS}
        self.last_on_eng = {}
        self.n_sb = 0
        self.chanmap = {}
        self.NDMA = 96

    def add(self, eng, fn, reads=(), writes=(), chan=None):
        op = Op()
        op.eng, op.fn, op.chan = eng, fn, chan
        op.needed = False
        op.idx = len(self.ops)
        deps = set(self.pending[eng])
        self.pending[eng] = set()
        for k in list(reads) + list(writes):
            w = self.lastw.get(k)
            if w is not None:
                deps.add(w)
        for k in writes:
            for r in self.readers.get(k, ()):
                deps.add(r)
        if chan is not None:
            ci = self.chanmap.get(chan)
            if ci is None:
                ci = len(self.chanmap) % self.NDMA
                self.chanmap[chan] = ci
            chan = ci
            op.chan = ci
            p = self.chan_last.get(chan)
            if p is not None:
                deps.add(p)
            self.chan_last[chan] = op
        deps.discard(op)
        op.deps = deps
        for k in writes:
            self.lastw[k] = op
            self.readers[k] = []
        for k in reads:
            self.readers.setdefault(k, []).append(op)
        self.ops.append(op)
        if chan is None:
            self.last_on_eng[eng] = op
        return op

    def barrier(self):
        alld = set(self.last_on_eng.values()) | set(self.chan_last.values())
        for e in self.ENGS:
            self.pending[e] |= alld
        self.lastw = {}
        self.readers = {}
        self.chanmap = {}

    def finalize(self):
        nc = self.nc
        self.barrier()
        for e in self.ENGS:
            self.add(e, None)
        for op in self.ops:
            for d in op.deps:
                if d.eng == "pe" and op.eng == "pe" and d.chan is None and op.chan is None:
                    continue
                d.needed = True
        sems = {}
        cnt = {}
        for op in self.ops:
            if not op.needed:
                continue
            key = ("c", op.chan) if op.chan is not None else ("e", op.eng)
            if key not in sems:
                sems[key] = self.es.enter_context(nc.semaphore(f"s_{len(sems)}"))
                cnt[key] = 0
            op.sem = sems[key]
            cnt[key] += 16 if op.chan is not None else 1
            op.val = cnt[key]
        self.nsems = len(sems)
        self.maxval = max(cnt.values()) if cnt else 0
        block = self.es.enter_context(nc.Block())
        per = {e: [op for op in self.ops if op.eng == e] for e in self.ENGS}

        def emit(e, h):
            waited = {}
            for op in per[e]:
                need = {}
                for d in op.deps:
                    if not d.needed:
                        continue
                    if d.eng == "pe" and e == "pe" and d.chan is None and op.chan is None:
                        continue
                    k = id(d.sem)
                    if need.get(k, (None, 0))[1] < d.val:
                        need[k] = (d.sem, d.val)
                for k, (s, v) in need.items():
                    if waited.get(k, 0) >= v:
                        continue
                    h.wait_ge(s, v)
                    waited[k] = v
                if op.fn is None:
                    continue
                ins = op.fn(h)
                if op.needed:
                    ins.then_inc(op.sem, 16 if op.chan is not None else 1)

        @block.tensor
        def _(h):
            emit("pe", h)

        @block.scalar
        def _(h):
            emit("act", h)

        @block.vector
        def _(h):
            emit("dve", h)

        @block.gpsimd
        def _(h):
            emit("pool", h)

        @block.sync
        def _(h):
            emit("sp", h)


class Rot:
    def __init__(self, tensors, name):
        self.t = tensors
        self.name = name
        self.i = -1

    keys = None

    def next(self):
        self.i = (self.i + 1) % len(self.t)
        return self.t[self.i], (self.keys[self.i] if self.keys else f"{self.name}{self.i}")


def build(cfg, debug=()):
    SEQ, DEC_BATCH, PAST = cfg["SEQ"], cfg["DEC_BATCH"], cfg["PAST"]
    TP = SEQ
    BPC = DEC_BATCH // NCORES
    TS = BPC * 4
    TT = TP + TS
    NTP = TP // 128
    NPG = PAST // 128
    NPOOL = (DEC_BATCH * NPG * 5) // 4
    tiles = [(i * 128, 128) for i in range(NTP)] + [(TP, TS)]
    NTL = len(tiles)

    nc = bass.Bass("TRN2", target_bir_lowering=False)
    es = ExitStack()
    P = Prog(nc, es)
    A = P.add

    def din(name, shape, dt=F32):
        return nc.dram_tensor(name, list(shape), dt, kind="ExternalInput").ap()

    def dout(name, shape, dt=F32):
        return nc.dram_tensor(name, list(shape), dt, kind="ExternalOutput").ap()

    def dscr(name, shape, dt=F32):
        kind = "ExternalOutput" if name in debug else "Internal"
        return nc.dram_tensor(name, list(shape), dt, kind=kind).ap()

    i_xp = din("xp", [TP, D])
    i_xs = din("xs", [TS, D])
    i_ssm = din("state_ssm", [BPC, 16 * 64, 128])
    i_conv = din("state_conv", [BPC, 3, 1536])
    i_cswk = din("cswk", [BPC, 128, 256])
    i_cswv = din("cswv", [BPC, 128, 256])
    i_fkv = din("fox_kv", [NPOOL * 128, 512])
    i_flp = din("fox_lp", [NPOOL, 2048])
    i_cmk = din("cmk", [2, BPC, 256, 512])
    i_cmv = din("cmv", [2, BPC, 256, 512])
    i_pt = din("pt", [BPC, NPG], I32)
    i_memp = din("memp", [256, D])
    i_nmix = din("norm_mix", [2, D])
    i_nxa = din("norm_xa", [2, D])
    i_nmem = din("norm_mem", [2, D])
    i_nffn = din("norm_ffn", [2, D])
    i_wie = din("w_in_even", [D, EVEN_PROJ])
    i_cw = din("conv_w", [4, 1536])
    i_cb = din("conv_b", [1, 1536])
    i_dtb = din("dt_bias", [1, 16])
    i_alog = din("a_log", [1, 16])
    i_dsk = din("d_skip", [1, 16])
    i_ssmn = din("ssm_norm", [1, D])
    i_sink = din("swa_sink", [1, 16])
    i_woe = din("w_out_even", [2048, D])
    i_wio = din("w_in_odd", [D, ODD_PROJ])
    i_fb = din("fox_fb", [1, 16])
    i_woo = din("w_out_odd", [D, D])
    i_wxq = din("w_xq", [2, D, 512])
    i_wxk = din("w_xk", [2, D, 512])
    i_wxv = din("w_xv", [2, D, 512])
    i_wxo = din("w_xo", [2, 512, D])
    i_wfi = din("w_ffn_in", [2, D, 2 * FFN_H])
    i_wfo = din("w_ffn_out", [2, FFN_H, D])
    i_nfin = din("norm_final", [1, D])
    i_iota = din("iota", [128, 1])
    o_yp = dout("y_p", [TP, D])
    o_ys = dout("y_s", [TS, D])
    o_pssm = dout("p_ssm", [16 * 64, 128])
    o_pconv = dout("p_conv", [3, 1536])
    o_pswk = dout("p_swk", [128, 256])
    o_pswv = dout("p_swv", [128, 256])
    o_pfk = dout("p_fk", [TP, 256])
    o_pfv = dout("p_fv", [TP, 256])
    o_pfl = dout("p_fl", [TP, 16])
    o_pmk = dout("p_mk", [2, 256, 512])
    o_pmv = dout("p_mv", [2, 256, 512])
    o_sssm = dout("s_ssm", [BPC, 16 * 64, 128])
    o_sconv = dout("s_conv", [BPC, 3, 1536])
    o_sswk = dout("s_swk", [BPC, 128, 256])
    o_sswv = dout("s_swv", [BPC, 128, 256])
    o_sfk = dout("s_fk", [TS, 256])
    o_sfv = dout("s_fv", [TS, 256])
    o_sfl = dout("s_fl", [TS, 16])
    X = dscr("X", [TT, D])
    PROJ = dscr("PROJ", [TT, EVEN_PROJ])
    XPADP = dscr("XPADP", [TP + 3, 1536])
    XPADS = dscr("XPADS", [BPC, 7, 1536])
    XC = dscr("XC", [TT, 1536])
    CAT = dscr("CAT", [TT, 2048])
    XQ = dscr("XQ", [TT, 512])
    MKV = dscr("MKV", [256, 1024])
    XA = dscr("XA", [TT, 512])
    ACTT = dscr("ACTT", [22, 128, TT], BF16)

    uid = [0]

    AW = 53000
    arena = es.enter_context(nc.sbuf_tensor("arena", [128, AW], F32))
    atop = [0]

    def sb(shape, dt=F32):
        n = 1
        for d_ in shape[1:]:
            n *= d_
        words = n if dt in (F32, I32) else (n + 1) // 2
        off = atop[0]
        atop[0] += words
        assert atop[0] <= AW, f"arena overflow {atop[0]}"
        ap = arena[:, off:off + words]
        if dt != F32:
            ap = ap.bitcast(dt)
        if len(shape) == 3:
            ap = ap.rearrange("p (a b) -> p a b", a=shape[1])
        elif len(shape) == 4:
            ap = ap.rearrange("p (a b c) -> p a b c", a=shape[1], b=shape[2])
        return ap[:shape[0]]

    class Phase:
        def __enter__(self):
            self.mark = atop[0]
            return self

        def __exit__(self, *a):
            P.barrier()
            atop[0] = self.mark

    def ps(shape, dt=F32):
        uid[0] += 1
        return es.enter_context(nc.psum_tensor(f"ps{uid[0]}", list(shape), dt))

    def rot(n, shape, dt, name, psum=False):
        return Rot([(ps if psum else sb)(shape, dt) for _ in range(n)], name)

    identf = sb([128, 128])
    identb = sb([128, 128], BF16)
    tri = sb([128, 128])
    ones = sb([128, 128])
    A("pool", lambda h: h.memset(identf[:], 0.0), writes=["identf"])
    A("pool", lambda h: h.affine_select(out=identf[:], in_=identf[:], pattern=[[-1, 128]], compare_op=ALU.not_equal,
                                        fill=1.0, base=0, channel_multiplier=1), reads=["identf"], writes=["identf"])
    A("dve", lambda h: h.tensor_copy(identb[:], identf[:]), reads=["identf"], writes=["identb"])
    A("pool", lambda h: h.memset(tri[:], 1.0), writes=["tri"])
    A("pool", lambda h: h.affine_select(out=tri[:], in_=tri[:], pattern=[[1, 128]], compare_op=ALU.is_ge,
                                        fill=0.0, base=0, channel_multiplier=-1), reads=["tri"], writes=["tri"])
    A("pool", lambda h: h.memset(ones[:], 1.0), writes=["ones"])

    pbank = [ps([128, 512]) for _ in range(8)]
    PT = Rot([pbank[0], pbank[1]], "pT")
    PM = Rot([pbank[2], pbank[3], pbank[4], pbank[5]], "pM")
    PX = Rot([pbank[6], pbank[7]], "pX")

    gb = sb([128, D])
    hT = None

    def new_hT():
        nonlocal hT
        hT = sb([128, 8, TT], BF16)
        return hT
    xt_r = rot(2, [128, D], F32, "xt")
    xn_r = rot(2, [128, D], BF16, "xn")
    sq = sb([128, D])
    ss_r = rot(2, [128, 1], F32, "ss")
    wb_r = rot(2, [128, 8, 512], BF16, "wb")
    ev_r = rot(3, [128, 512], F32, "ev")

    def rstd_ops(src_t, src_k, rows, ssk_t, ssk):
        A("act", lambda h: h.activation(out=sq[:rows, :], in_=src_t[:rows, :], func=AF.Square, accum_out=ssk_t[:rows, :]),
          reads=[src_k], writes=["sq", ssk])
        A("dve", lambda h: h.tensor_scalar(ssk_t[:rows, :], ssk_t[:rows, :], 1.0 / D, 1e-6, ALU.mult, ALU.add),
          reads=[ssk], writes=[ssk])
        A("act", lambda h: h.sqrt(ssk_t[:rows, :], ssk_t[:rows, :]), reads=[ssk], writes=[ssk])
        A("dve", lambda h: h.reciprocal(ssk_t[:rows, :], ssk_t[:rows, :]), reads=[ssk], writes=[ssk])

    def load_gain(row_ap):
        A("sp", lambda h: h.dma_start(out=gb[:], in_=row_ap.partition_broadcast(128)), writes=["gb"], chan="gb")

    def norm_T(src, gain_row, tl, dst, dstname, col0=0):
        load_gain(gain_row)
        for (r0, rows) in tl:
            xt, xk = xt_r.next()
            A("sp", lambda h, xt=xt, r0=r0, rows=rows: h.dma_start(out=xt[:rows, :], in_=src[r0:r0 + rows, :]),
              writes=[xk], chan=xk)
            st, sk = ss_r.next()
            rstd_ops(xt, xk, rows, st, sk)
            xn, nk = xn_r.next()
            A("dve", lambda h, xn=xn, xt=xt, st=st, rows=rows: h.scalar_tensor_tensor(
                xn[:rows, :], xt[:rows, :], st[:rows, :], gb[:rows, :], ALU.mult, ALU.mult),
              reads=[xk, sk, "gb"], writes=[nk])
            pt, pk = PT.next()
            ptb = pt[:].bitcast(BF16)

            def tr(h, xn=xn, ptb=ptb, rows=rows):
                for c in range(8):
                    ins = h.transpose(ptb[:, c * 128:c * 128 + rows], xn[:rows, c * 128:(c + 1) * 128], identb[:rows, :rows])
                return ins
            A("pe", tr, reads=[nk], writes=[pk])
            A("act", lambda h, ptb=ptb, r0=r0, rows=rows: h.copy(
                dst[:, :, col0 + r0:col0 + r0 + rows], ptb.rearrange("p (c t) -> p c t", c=8)[:, :, :rows]),
              reads=[pk], writes=[f"{dstname}{r0}"])

    def load_w(W, r0k, KC, c0, w):
        wb, wk = wb_r.next()
        A("pool", lambda h, wb=wb: h.dma_start(out=wb[:, :KC, :w],
                                              in_=W[r0k:r0k + KC * 128, c0:c0 + w].rearrange("(c p) n -> p c n", p=128)),
          writes=[wk], chan=wk)
        return wb, wk

    evtog = [0]

    def evac(dst_ap, src_ap, reads, writes):
        evtog[0] ^= 1
        if evtog[0]:
            A("act", lambda h: h.copy(dst_ap, src_ap), reads=reads, writes=writes)
        else:
            A("dve", lambda h: h.tensor_copy(dst_ap, src_ap), reads=reads, writes=writes)

    def linear_to_dram(src, srcname, W, N, OUT, tl, oc0=0):
        for c0 in range(0, N, 512):
            w = min(512, N - c0)
            wb, wk = load_w(W, 0, 8, c0, w)
            for (r0, rows) in tl:
                pm, pk = PM.next()

                def mm(h, pm=pm, wb=wb, r0=r0, rows=rows, w=w):
                    for c in range(8):
                        ins = h.matmul(pm[:rows, :w], src[:, c, r0:r0 + rows], wb[:, c, :w], start=(c == 0), stop=(c == 7))
                    return ins
                A("pe", mm, reads=[f"{srcname}{r0}", wk], writes=[pk])
                ev, ek = ev_r.next()
                evac(ev[:rows, :w], pm[:rows, :w], [pk], [ek])
                A("sp", lambda h, ev=ev, r0=r0, rows=rows, c0=c0, w=w: h.dma_start(
                    out=OUT[r0:r0 + rows, oc0 + c0:oc0 + c0 + w], in_=ev[:rows, :w]),
                  reads=[ek], writes=[f"{OUT.name}:{r0}"], chan=ek + "st")

    def proj_residual(ACTD, K, W, tl):
      with Phase():
        KC = K // 128
        wres = sb([128, KC, D], BF16)
        for c0 in (0, 512):
            A("pool", lambda h, c0=c0: h.dma_start(out=wres[:, :, c0:c0 + 512],
                                                  in_=W[:, c0:c0 + 512].rearrange("(c p) n -> p c n", p=128)),
              writes=[f"wres{c0}"], chan=f"wres{c0}")
        ab_r = rot(3, [128, K], BF16, "prab")
        aT_r = rot(2, [128, KC, 128], BF16, "praT")
        xo_r = rot(2, [128, D], F32, "prxo")
        for (r0, rows) in tl:
            ab, bk = ab_r.next()
            A("pool", lambda h, ab=ab, r0=r0, rows=rows: h.dma_start(out=ab[:rows, :], in_=ACTD[r0:r0 + rows, 0:K]),
              reads=[f"{ACTD.name}:{r0}"], writes=[bk], chan=bk)
            aT, tk = aT_r.next()
            for g0 in range(0, KC, 8):
                gn = min(8, KC - g0)
                pt, pk = PT.next()
                ptb = pt[:].bitcast(BF16)

                def tr(h, ab=ab, ptb=ptb, rows=rows, g0=g0, gn=gn):
                    for c in range(gn):
                        ins = h.transpose(ptb[:, c * 128:c * 128 + rows], ab[:rows, (g0 + c) * 128:(g0 + c + 1) * 128],
                                          identb[:rows, :rows])
                    return ins
                A("pe", tr, reads=[bk], writes=[pk])
                A("act", lambda h, aT=aT, ptb=ptb, rows=rows, g0=g0, gn=gn: h.copy(
                    aT[:, g0:g0 + gn, :rows], ptb.rearrange("p (c t) -> p c t", c=8)[:, :gn, :rows]),
                  reads=[pk], writes=[tk + f"g{g0}"])
            xt, xk = xt_r.next()
            A("sp", lambda h, xt=xt, r0=r0, rows=rows: h.dma_start(out=xt[:rows, :], in_=X[r0:r0 + rows, :]),
              reads=[f"X:{r0}"], writes=[xk], chan=xk)
            xo, ok = xo_r.next()
            for hf, c0 in enumerate((0, 512)):
                pm, pk = PM.next()

                def mm(h, pm=pm, aT=aT, rows=rows, c0=c0):
                    for c in range(KC):
                        ins = h.matmul(pm[:rows, :], aT[:, c, :rows], wres[:, c, c0:c0 + 512], start=(c == 0), stop=(c == KC - 1))
                    return ins
                A("pe", mm, reads=[tk + f"g{g0}" for g0 in range(0, KC, 8)] + [f"wres{c0}"], writes=[pk])
                A("dve", lambda h, xo=xo, xt=xt, pm=pm, rows=rows, c0=c0: h.tensor_tensor(
                    xo[:rows, c0:c0 + 512], pm[:rows, :], xt[:rows, c0:c0 + 512], ALU.add),
                  reads=[pk, xk], writes=[ok + f"h{hf}"])
            A("sp", lambda h, xo=xo, r0=r0, rows=rows: h.dma_start(out=X[r0:r0 + rows, :], in_=xo[:rows, :]),
              reads=[ok + "h0", ok + "h1"], writes=[f"X:{r0}"], chan=ok + "st")
        P.barrier()

    def ffn(l):
        new_hT()
        norm_T(X, i_nffn[l:l + 1, :], tiles, hT, "hT")
        groups = [(g * 512, min(512, TP - g * 512)) for g in range((TP + 511) // 512)] + [(TP, TS)]
        wg_r = rot(2, [128, 8, 128], BF16, "wg")
        wu_r = rot(2, [128, 8, 128], BF16, "wu")
        sg_r = rot(2, [128, 512], F32, "sg")
        ao_r = rot(2, [128, 512], BF16, "ao")
        Wi = i_wfi[l]
        Wo = i_wfo[l]
        for j in range(22):
            wg, gk = wg_r.next()
            wu, uk = wu_r.next()
            A("pool", lambda h, wg=wg, j=j: h.dma_start(out=wg[:], in_=Wi[:, j * 128:(j + 1) * 128].rearrange("(c p) n -> p c n", p=128)),
              writes=[gk], chan=gk)
            A("pool", lambda h, wu=wu, j=j: h.dma_start(out=wu[:], in_=Wi[:, FFN_H + j * 128:FFN_H + (j + 1) * 128].rearrange("(c p) n -> p c n", p=128)),
              writes=[uk], chan=uk)
            for (t0, tw) in groups:
                pg, pgk = PM.next()
                pu, puk = PM.next()
                hk = [f"hT{r0}" for (r0, rows) in tiles if r0 >= t0 and r0 < t0 + tw]

                def mmg(h, pg=pg, wg=wg, t0=t0, tw=tw, hT=hT):
                    for c in range(8):
                        ins = h.matmul(pg[:, :tw], wg[:, c, :], hT[:, c, t0:t0 + tw], start=(c == 0), stop=(c == 7))
                    return ins
                A("pe", mmg, reads=hk + [gk], writes=[pgk])

                def mmu(h, pu=pu, wu=wu, t0=t0, tw=tw, hT=hT):
                    for c in range(8):
                        ins = h.matmul(pu[:, :tw], wu[:, c, :], hT[:, c, t0:t0 + tw], start=(c == 0), stop=(c == 7))
                    return ins
                A("pe", mmu, reads=hk + [uk], writes=[puk])
                sg, sk = sg_r.next()
                A("act", lambda h, sg=sg, pg=pg, tw=tw: h.activation(out=sg[:, :tw], in_=pg[:, :tw], func=AF.Silu),
                  reads=[pgk], writes=[sk])
                ao, aok = ao_r.next()
                A("dve", lambda h, ao=ao, sg=sg, pu=pu, tw=tw: h.tensor_tensor(ao[:, :tw], pu[:, :tw], sg[:, :tw], ALU.mult),
                  reads=[sk, puk], writes=[aok])
                A("sp", lambda h, ao=ao, j=j, t0=t0, tw=tw: h.dma_start(out=ACTT[j, :, t0:t0 + tw], in_=ao[:, :tw]),
                  reads=[aok], writes=[f"ACTT{j}:{t0}"], chan=aok + "st")
        P.barrier()
        wres = sb([128, 22, D], BF16)
        for c0 in (0, 512):
            A("pool", lambda h, c0=c0: h.dma_start(out=wres[:, :, c0:c0 + 512],
                                                  in_=Wo[:, c0:c0 + 512].rearrange("(c p) n -> p c n", p=128)),
              writes=[f"fwres{c0}"], chan=f"fwres{c0}")
        aT_r = rot(2, [128, 22, 128], BF16, "faT")
        xo_r = rot(2, [128, D], F32, "fxo")
        for (r0, rows) in tiles:
            aT, tk = aT_r.next()
            A("sp", lambda h, aT=aT, r0=r0, rows=rows: h.dma_start(
                out=aT[:, :, :rows], in_=ACTT[:, :, r0:r0 + rows].rearrange("j p t -> p j t")),
              writes=[tk], chan=tk)
            xt, xk = xt_r.next()
            A("sp", lambda h, xt=xt, r0=r0, rows=rows: h.dma_start(out=xt[:rows, :], in_=X[r0:r0 + rows, :]),
              writes=[xk], chan=xk)
            xo, ok = xo_r.next()
            for hf, c0 in enumerate((0, 512)):
                pm, pk = PM.next()

                def mm(h, pm=pm, aT=aT, rows=rows, c0=c0):
                    for c in range(22):
                        ins = h.matmul(pm[:rows, :], aT[:, c, :rows], wres[:, c, c0:c0 + 512], start=(c == 0), stop=(c == 21))
                    return ins
                A("pe", mm, reads=[tk, f"fwres{c0}"], writes=[pk])
                A("dve", lambda h, xo=xo, xt=xt, pm=pm, rows=rows, c0=c0: h.tensor_tensor(
                    xo[:rows, c0:c0 + 512], pm[:rows, :], xt[:rows, c0:c0 + 512], ALU.add),
                  reads=[pk, xk], writes=[ok + f"h{hf}"])
            A("sp", lambda h, xo=xo, r0=r0, rows=rows: h.dma_start(out=X[r0:r0 + rows, :], in_=xo[:rows, :]),
              reads=[ok + "h0", ok + "h1"], writes=[f"X:{r0}"], chan=ok + "st")
        P.barrier()

    def final_norm():
        load_gain(i_nfin[0:1, :])
        yo_r = rot(2, [128, D], F32, "fyo")
        for (r0, rows) in tiles:
            xt, xk = xt_r.next()
            A("sp", lambda h, xt=xt, r0=r0, rows=rows: h.dma_start(out=xt[:rows, :], in_=X[r0:r0 + rows, :]),
              writes=[xk], chan=xk)
            st, sk = ss_r.next()
            rstd_ops(xt, xk, rows, st, sk)
            yo, yk = yo_r.next()
            A("dve", lambda h, yo=yo, xt=xt, st=st, rows=rows: h.scalar_tensor_tensor(
                yo[:rows, :], xt[:rows, :], st[:rows, :], gb[:rows, :], ALU.mult, ALU.mult),
              reads=[xk, sk, "gb"], writes=[yk])
            dst = o_yp[r0:r0 + rows, :] if r0 < TP else o_ys[:, :]
            A("sp", lambda h, yo=yo, dst=dst, rows=rows: h.dma_start(out=dst, in_=yo[:rows, :]),
              reads=[yk], writes=[f"y:{r0}"], chan=yk + "st")


    E_r = rot(4, [128, 512], BF16, "E")
    rd_r = rot(3, [128, 4], F32, "rd")

    def attn_step(kT_ap, kkeys, qT_ap, qkeys, nk, ncols, scale, bias_ap, bkeys, mask_ap, pvs, pok, vkeys, SB=None):
        pm, pk = (SB or PM).next()
        A("pe", lambda h: h.matmul(pm[:nk, :ncols], kT_ap, qT_ap, start=True, stop=True), reads=kkeys + qkeys, writes=[pk])
        E, ek = E_r.next()
        if bias_ap is None:
            A("act", lambda h: h.activation(out=E[:nk, :ncols], in_=pm[:nk, :ncols], func=AF.Exp, scale=scale),
              reads=[pk], writes=[ek])
        else:
            A("act", lambda h: h.activation(out=E[:nk, :ncols], in_=pm[:nk, :ncols], func=AF.Exp, bias=bias_ap, scale=scale),
              reads=[pk] + bkeys, writes=[ek])
        if mask_ap is not None:
            A("dve", lambda h: h.tensor_tensor(E[:nk, :ncols], E[:nk, :ncols], mask_ap, ALU.mult), reads=[ek], writes=[ek])

        if pvs is None:
            return E, ek

        def pvf(h):
            for (po_ap, c0, nqq, v_ap, st, sp_) in pvs:
                ins = h.matmul(po_ap, E[:nk, c0:c0 + nqq], v_ap, start=st, stop=sp_)
            return ins
        A("pe", pvf, reads=[ek] + vkeys, writes=[pok])

    class Defer:
        def __init__(self, depth=1):
            self.p = []
            self.depth = depth

        def push(self, fn):
            self.p.append(fn)
            while len(self.p) > self.depth:
                self.p.pop(0)()

        def flush(self):
            while self.p:
                self.p.pop(0)()

    def dram_copy(dst, src, key):
        A("sp", lambda h: h.dma_start(out=dst, in_=src), writes=[key], chan=key)

    def xattn(l):
        mt = [(0, 128), (128, 128)]
        with Phase():
            new_hT()
            norm_T(i_memp, i_nmem[l:l + 1, :], mt, hT, "hT")
            linear_to_dram(hT, "hT", i_wxk[l], 512, o_pmk[l], mt)
            linear_to_dram(hT, "hT", i_wxv[l], 512, o_pmv[l], mt)
        with Phase():
            new_hT()
            norm_T(X, i_nxa[l:l + 1, :], tiles, hT, "hT")
            linear_to_dram(hT, "hT", i_wxq[l], 512, XQ, tiles)
        with Phase():
            seqs = [(o_pmk[l], o_pmv[l], tiles[:NTP])] + [(i_cmk[l, b], i_cmv[l, b], [(TP + 4 * b, 4)]) for b in range(BPC)]
            kb_r = rot(2, [128, 2, 512], BF16, "xkb")
            kT_r = rot(2, [128, 4, 256], BF16, "xkT")
            va_r = rot(2, [128, 2, 4, 129], BF16, "xva")
            qb_r = rot(3, [128, 512], BF16, "xqb")
            qT_r = rot(2, [128, 512], BF16, "xqT")
            xo_r = rot(2, [128, 512], F32, "xxo")
            sc = 128 ** -0.5
            dfr = Defer()
            E_r.t = E_r.t
            for (kd, vd, qtl) in seqs:
                kb, kbk = kb_r.next()
                A("pool", lambda h, kb=kb, kd=kd: h.dma_start(out=kb[:], in_=kd.rearrange("(b p) n -> p b n", p=128)), writes=[kbk], chan=kbk)
                pt, pk = PT.next()
                ptb = pt[:].bitcast(BF16)

                def trk(h, kb=kb, ptb=ptb):
                    for hh in range(4):
                        for bl in range(2):
                            ins = h.transpose(ptb[:, (hh * 2 + bl) * 128:(hh * 2 + bl + 1) * 128], kb[:, bl, hh * 128:(hh + 1) * 128], identb[:])
                    return ins
                A("pe", trk, reads=[kbk], writes=[pk])
                kT, kTk = kT_r.next()
                A("act", lambda h, kT=kT, ptb=ptb: h.copy(kT[:].rearrange("p a b -> p (a b)"), ptb), reads=[pk], writes=[kTk])
                va, vak = va_r.next()
                A("pool", lambda h, va=va: h.memset(va[:, :, :, 128:129], 1.0), writes=[vak + "o"])
                for bl_ in range(2):
                    A("pool", lambda h, va=va, vd=vd, bl_=bl_: h.dma_start(
                        out=va[:, bl_, :, 0:128], in_=vd[bl_ * 128:(bl_ + 1) * 128, :].rearrange("p (h d) -> p h d", h=4)),
                      writes=[vak + f"b{bl_}"], chan=vak + f"b{bl_}")
                for (r0, rows) in qtl:
                    qb, qbk = qb_r.next()
                    A("pool", lambda h, qb=qb, r0=r0, rows=rows: h.dma_start(out=qb[:rows, :], in_=XQ[r0:r0 + rows, :]), writes=[qbk], chan=qbk)
                    pt, pk = PT.next()
                    ptb = pt[:].bitcast(BF16)

                    def trq(h, qb=qb, ptb=ptb, rows=rows):
                        for hh in range(4):
                            ins = h.transpose(ptb[:, hh * 128:hh * 128 + rows], qb[:rows, hh * 128:(hh + 1) * 128], identb[:rows, :rows])
                        return ins
                    A("pe", trq, reads=[qbk], writes=[pk])
                    qT, qTk = qT_r.next()
                    A("act", lambda h, qT=qT, ptb=ptb, rows=rows: h.copy(
                        qT[:, :4 * rows].rearrange("p (h q) -> p h q", h=4), ptb.rearrange("p (h t) -> p h t", h=8)[:, :4, :rows]),
                      reads=[pk], writes=[qTk])
                    xo, xok = xo_r.next()
                    Es = []
                    for bl in range(2):
                        pm, pk = PM.next()

                        def msx(h, pm=pm, bl=bl, kT=kT, qT=qT, rows=rows):
                            for hh in range(4):
                                ins = h.matmul(pm[:, hh * rows:(hh + 1) * rows], kT[:, hh, bl * 128:(bl + 1) * 128], qT[:, hh * rows:(hh + 1) * rows],
                                               start=True, stop=True)
                            return ins
                        A("pe", msx, reads=[kTk, qTk], writes=[pk])
                        E, ek = E_r.next()
                        A("act", lambda h, E=E, pm=pm, rows=rows: h.activation(out=E[:, :4 * rows], in_=pm[:, :4 * rows], func=AF.Exp, scale=sc),
                          reads=[pk], writes=[ek])
                        Es.append((E, ek))

                    def tail(Es=Es, va=va, vak=vak, xo=xo, xok=xok, rows=rows, r0=r0):
                        for pr in range(2):
                            po, pok = PX.next()

                            def pvf(h, po=po, pr=pr):
                                for hl in range(2):
                                    hh = pr * 2 + hl
                                    for bl in range(2):
                                        ins = h.matmul(po[:rows, hl * 129:(hl + 1) * 129], Es[bl][0][:128, hh * rows:(hh + 1) * rows], va[:, bl, hh, :],
                                                       start=(bl == 0), stop=(bl == 1))
                                return ins
                            A("pe", pvf, reads=[Es[0][1], Es[1][1], vak + "b0", vak + "b1", vak + "o"], writes=[pok])
                            rd, rdk = rd_r.next()
                            A("dve", lambda h, rd=rd, po=po: h.reciprocal(rd[:rows, 0:2].unsqueeze(2),
                                                                          po[:rows, :258].rearrange("p (h e) -> p h e", h=2)[:, :, 128:129]),
                              reads=[pok], writes=[rdk])
                            A("dve", lambda h, rd=rd, po=po, pr=pr: h.tensor_tensor(
                                xo[:rows, pr * 256:(pr + 1) * 256].rearrange("p (h d) -> p h d", h=2),
                                po[:rows, :258].rearrange("p (h e) -> p h e", h=2)[:, :, 0:128],
                                rd[:rows, 0:2].unsqueeze(2).broadcast_to([rows, 2, 128]), ALU.mult),
                              reads=[pok, rdk], writes=[xok + f"h{pr}"])
                        A("sp", lambda h: h.dma_start(out=XA[r0:r0 + rows, :], in_=xo[:rows, :]),
                          reads=[xok + "h0", xok + "h1"], writes=[f"XA:{r0}"], chan=xok + "st")
                    dfr.push(tail)
            dfr.flush()
        proj_residual(XA, 512, i_wxo[l], tiles)

    zt = sb([128, 1536])
    A("pool", lambda h: h.memset(zt[:], 0.0), writes=["zt"])

    def even_inproj():
        with Phase():
            new_hT()
            norm_T(X, i_nmix[0:1, :], tiles, hT, "hT")
            linear_to_dram(hT, "hT", i_wie, EVEN_PROJ, PROJ, tiles)
        A("sp", lambda h: h.dma_start(out=XPADP[0:3, :], in_=zt[0:3, :]), reads=["zt"], writes=["xpz"], chan="xpz")
        for t0 in range(0, TP, 512):
            tw = min(512, TP - t0)
            dram_copy(XPADP[3 + t0:3 + t0 + tw, :], PROJ[t0:t0 + tw, 1024:2560], f"xpp{t0}")
        dram_copy(XPADS[:, 0:3, :], i_conv[:, :, :], "xps0")
        dram_copy(XPADS[:, 3:7, :], PROJ[TP:TT, 1024:2560].rearrange("(b i) c -> b i c", i=4), "xps1")
        dram_copy(o_pswk[:, :], PROJ[TP - 128:TP, 3600:3856], "opswk")
        dram_copy(o_pswv[:, :], PROJ[TP - 128:TP, 3856:4112], "opswv")
        dram_copy(o_sswk[:, 0:124, :], i_cswk[:, 4:128, :], "osswk0")
        dram_copy(o_sswv[:, 0:124, :], i_cswv[:, 4:128, :], "osswv0")
        dram_copy(o_sswk[:, 124:128, :], PROJ[TP:TT, 3600:3856].rearrange("(b i) c -> b i c", i=4), "osswk1")
        dram_copy(o_sswv[:, 124:128, :], PROJ[TP:TT, 3856:4112].rearrange("(b i) c -> b i c", i=4), "osswv1")
        P.barrier()
        dram_copy(o_pconv[:, :], XPADP[TP:TP + 3, :], "opconv")
        dram_copy(o_sconv[:, :, :], XPADS[:, 4:7, :], "osconv")
        P.barrier()


    def conv_phase():
        with Phase():
            cwb = sb([128, 4, 1536])
            cbb = sb([128, 1536])
            for j in range(4):
                A("sp", lambda h, j=j: h.dma_start(out=cwb[:, j, :], in_=i_cw[j:j + 1, :].partition_broadcast(128)),
                  writes=[f"cwb{j}"], chan=f"cwb{j}")
            A("sp", lambda h: h.dma_start(out=cbb[:], in_=i_cb[0:1, :].partition_broadcast(128)), writes=["cbb"], chan="cbb")
            xj_r = [rot(3, [128, 1536], F32, f"cx{j}") for j in range(4)]
            acc_r = rot(3, [128, 1536], F32, "cacc")
            tmp_r = rot(3, [128, 1536], F32, "ctmp")
            for (r0, rows) in tiles:
                xs_ = []
                for j in range(4):
                    xj, xjk = xj_r[j].next()
                    if r0 < TP:
                        src = XPADP[r0 + j:r0 + j + rows, :]
                        A("sp", lambda h, xj=xj, src=src, rows=rows: h.dma_start(out=xj[:rows, :], in_=src), writes=[xjk], chan=xjk)
                    else:
                        for b in range(BPC):
                            A("sp", lambda h, xj=xj, b=b, j=j: h.dma_start(out=xj[4 * b:4 * b + 4, :], in_=XPADS[b, j:j + 4, :]),
                              writes=[xjk], chan=xjk)
                    xs_.append((xj, xjk))
                acc, ack = acc_r.next()
                A("dve", lambda h, acc=acc, x=xs_[0][0], rows=rows: h.tensor_tensor(acc[:rows, :], x[:rows, :], cwb[:rows, 0, :], ALU.mult),
                  reads=[xs_[0][1], "cwb0"], writes=[ack])
                for j in range(1, 4):
                    tmp, tk = tmp_r.next()
                    A("pool", lambda h, tmp=tmp, x=xs_[j][0], rows=rows, j=j: h.tensor_tensor(tmp[:rows, :], x[:rows, :], cwb[:rows, j, :], ALU.mult),
                      reads=[xs_[j][1], f"cwb{j}"], writes=[tk])
                    A("dve", lambda h, acc=acc, tmp=tmp, rows=rows: h.tensor_tensor(acc[:rows, :], acc[:rows, :], tmp[:rows, :], ALU.add),
                      reads=[tk, ack], writes=[ack])
                A("dve", lambda h, acc=acc, rows=rows: h.tensor_tensor(acc[:rows, :], acc[:rows, :], cbb[:rows, :], ALU.add),
                  reads=[ack, "cbb"], writes=[ack])
                tmp, tk = tmp_r.next()
                A("act", lambda h, tmp=tmp, acc=acc, rows=rows: h.activation(out=tmp[:rows, :], in_=acc[:rows, :], func=AF.Silu),
                  reads=[ack], writes=[tk])
                A("sp", lambda h, tmp=tmp, r0=r0, rows=rows: h.dma_start(out=XC[r0:r0 + rows, :], in_=tmp[:rows, :]),
                  reads=[tk], writes=[f"XC:{r0}"], chan=tk + "st")

    def ssd_phase():
        with Phase():
            dtb = sb([128, 16])
            aneg = sb([128, 16])
            dskb = sb([128, 16])
            A("sp", lambda h: h.dma_start(out=dtb[:], in_=i_dtb[0:1, :].partition_broadcast(128)), writes=["dtb"], chan="dtb")
            A("sp", lambda h: h.dma_start(out=aneg[:], in_=i_alog[0:1, :].partition_broadcast(128)), writes=["aneg"], chan="aneg")
            A("sp", lambda h: h.dma_start(out=dskb[:], in_=i_dsk[0:1, :].partition_broadcast(128)), writes=["dskb"], chan="dskb")
            A("act", lambda h: h.activation(out=aneg[:], in_=aneg[:], func=AF.Exp), reads=["aneg"], writes=["aneg"])
            A("dve", lambda h: h.tensor_scalar(aneg[:], aneg[:], -1.0, None, ALU.mult), reads=["aneg"], writes=["aneg"])
            load_gain(i_ssmn[0:1, :])
            state = sb([128, 1024])
            stateb = sb([128, 1024], BF16)
            stf = sb([128, 8, 128])
            xs_r = rot(2, [128, 1024], F32, "sxs")
            bc_r = rot(2, [128, 512], F32, "sbc")
            bcb_r = rot(2, [128, 512], BF16, "sbcb")
            bcT_r = rot(2, [128, 4, 128], BF16, "sbcT")
            z_r = rot(2, [128, 1024], F32, "sz")
            dt_r = rot(2, [128, 16], F32, "sdt")
            d1_r = rot(2, [128, 16], F32, "sd1")
            d2_r = rot(2, [128, 16], F32, "sd2")
            a_r = rot(2, [128, 16], F32, "sa")
            cs_r = rot(2, [128, 16], F32, "scs")
            ecs_r = rot(2, [128, 16], F32, "secs")
            edl_r = rot(2, [128, 16], F32, "sedl")
            rhsA_r = rot(2, [128, 16, 128], F32, "srhsA")
            dec_r = rot(2, [128, 16, 128], F32, "sdec")
            GT_r = rot(2, [128, 2, 128], F32, "sGT")
            sc_r = rot(2, [128, 16, 128], BF16, "ssc")
            xsb_r = rot(2, [128, 1024], BF16, "sxsb")
            identD = sb([128, 16, 128], BF16)
            A("dve", lambda h: h.tensor_tensor(identD[:], identf[:].unsqueeze(1).broadcast_to([128, 16, 128]),
                                               dskb[:].unsqueeze(2).broadcast_to([128, 16, 128]), ALU.mult), reads=["dskb"], writes=["identD"])
            xdt_r = rot(2, [128, 16, 64], BF16, "sxdt")
            xdw_r = rot(2, [128, 16, 64], BF16, "sxdw")
            t1_r = rot(2, [128, 1024], F32, "st1")
            t2_r = rot(2, [128, 1024], F32, "st2")
            yn_r = rot(2, [128, 1024], F32, "syn")

            def chunk(L, row0):
                xs, xsk = xs_r.next()
                A("sp", lambda h: h.dma_start(out=xs[:L, :], in_=XC[row0:row0 + L, 0:1024]), writes=[xsk], chan=xsk)
                bc, bck = bc_r.next()
                A("sp", lambda h: h.dma_start(out=bc[:L, :], in_=XC[row0:row0 + L, 1024:1536]), writes=[bck], chan=bck)
                z, zk = z_r.next()
                A("sp", lambda h: h.dma_start(out=z[:L, :], in_=PROJ[row0:row0 + L, 0:1024]), writes=[zk], chan=zk)
                dt, dtk = dt_r.next()
                A("sp", lambda h: h.dma_start(out=dt[:L, :], in_=PROJ[row0:row0 + L, 2560:2576]), writes=[dtk], chan=dtk)
                d1, d1k = d1_r.next()
                d2, d2k = d2_r.next()
                A("dve", lambda h: h.tensor_tensor(dt[:L, :], dt[:L, :], dtb[:L, :], ALU.add), reads=[dtk, "dtb"], writes=[dtk])
                A("dve", lambda h: h.tensor_scalar(d1[:L, :], dt[:L, :], -1.0, None, ALU.mult), reads=[dtk], writes=[d1k])
                A("dve", lambda h: h.tensor_tensor(d1[:L, :], d1[:L, :], dt[:L, :], ALU.min), reads=[dtk, d1k], writes=[d1k])
                A("act", lambda h: h.activation(out=d1[:L, :], in_=d1[:L, :], func=AF.Exp), reads=[d1k], writes=[d1k])
                A("dve", lambda h: h.tensor_scalar(d1[:L, :], d1[:L, :], 1.0, None, ALU.add), reads=[d1k], writes=[d1k])
                A("act", lambda h: h.activation(out=d1[:L, :], in_=d1[:L, :], func=AF.Ln), reads=[d1k], writes=[d1k])
                A("dve", lambda h: h.tensor_scalar(d2[:L, :], dt[:L, :], 0.0, None, ALU.max), reads=[dtk], writes=[d2k])
                A("dve", lambda h: h.tensor_tensor(dt[:L, :], d1[:L, :], d2[:L, :], ALU.add), reads=[d1k, d2k], writes=[dtk])
                a, ak = a_r.next()
                A("dve", lambda h: h.tensor_tensor(a[:L, :], dt[:L, :], aneg[:L, :], ALU.mult), reads=[dtk, "aneg"], writes=[ak])
                pc, pck = PT.next()
                A("pe", lambda h: h.matmul(pc[:L, 0:16], tri[:L, :L], a[:L, :], start=True, stop=True), reads=[ak], writes=[pck])
                cs, csk = cs_r.next()
                A("dve", lambda h: h.tensor_copy(cs[:L, :], pc[:L, 0:16]), reads=[pck], writes=[csk])
                ecs, ecsk = ecs_r.next()
                A("act", lambda h: h.activation(out=ecs[:L, :], in_=pc[:L, 0:16], func=AF.Exp), reads=[pck], writes=[ecsk])
                pl, plk = PT.next()
                A("pe", lambda h: h.matmul(pl[:, 0:16], ones[:L, :], a[:L, :], start=True, stop=True), reads=[ak], writes=[plk])
                edl, edlk = edl_r.next()
                A("act", lambda h: h.activation(out=edl[:], in_=pl[:, 0:16], func=AF.Exp), reads=[plk], writes=[edlk])
                rhsA, rak = rhsA_r.next()
                A("dve", lambda h: h.tensor_tensor(rhsA[:L, :, :L], tri[:L, :L].unsqueeze(1).broadcast_to([L, 16, L]),
                                                   a[:L, :].unsqueeze(2).broadcast_to([L, 16, L]), ALU.mult), reads=[ak], writes=[rak])
                dec, deck = dec_r.next()
                HG = max(1, min(16, 512 // L))
                for g0 in range(0, 16, HG):
                    pm, pmk = PM.next()
                    A("pe", lambda h, pm=pm, g0=g0: h.matmul(pm[:L, :HG * L], ones[:L, :L], rhsA[:L, g0:g0 + HG, :L], start=True, stop=True),
                      reads=[rak], writes=[pmk])
                    A("dve", lambda h, pm=pm, g0=g0: h.tensor_tensor(
                        dec[:L, g0:g0 + HG, :L], pm[:L, :HG * L].rearrange("p (a b) -> p a b", a=HG),
                        cs[:L, g0:g0 + HG].unsqueeze(2).broadcast_to([L, HG, L]), ALU.subtract),
                      reads=[pmk, csk], writes=[deck + f"g{g0}"])
                dks = [deck + f"g{g0}" for g0 in range(0, 16, HG)]
                A("act", lambda h: h.activation(out=dec[:L, :, :L], in_=dec[:L, :, :L], func=AF.Relu, scale=-1.0), reads=dks, writes=[deck])
                A("act", lambda h: h.activation(out=dec[:L, :, :L], in_=dec[:L, :, :L], func=AF.Exp, scale=-1.0), reads=[deck], writes=[deck])
                A("dve", lambda h: h.tensor_tensor(dec[:L, :, :L], dec[:L, :, :L], tri[:L, :L].unsqueeze(1).broadcast_to([L, 16, L]), ALU.mult),
                  reads=[deck], writes=[deck])
                bcb, bcbk = bcb_r.next()
                A("act", lambda h: h.copy(bcb[:L, :], bc[:L, :]), reads=[bck], writes=[bcbk])
                xsb, xsbk = xsb_r.next()
                A("act", lambda h: h.copy(xsb[:L, :], xs[:L, :]), reads=[xsk], writes=[xsbk])
                pt, ptk = PT.next()
                ptb = pt[:].bitcast(BF16)

                def trb(h):
                    for c in range(4):
                        ins = h.transpose(ptb[:, c * 128:c * 128 + L], bcb[:L, c * 128:(c + 1) * 128], identb[:L, :L])
                    return ins
                A("pe", trb, reads=[bcbk], writes=[ptk])
                bcT, bcTk = bcT_r.next()
                A("act", lambda h: h.copy(bcT[:, :, :L], ptb.rearrange("p (c t) -> p c t", c=8)[:, :4, :L]), reads=[ptk], writes=[bcTk])
                pg, pgk = PX.next()

                def mg(h):
                    for g in range(2):
                        ins = h.matmul(pg[:L, g * L:(g + 1) * L], bcT[:, g, :L], bcT[:, 2 + g, :L], start=True, stop=True)
                    return ins
                A("pe", mg, reads=[bcTk], writes=[pgk])
                GT, GTk = GT_r.next()
                A("act", lambda h: h.copy(GT[:L, :, :L], pg[:L, :2 * L].rearrange("p (g t) -> p g t", g=2)), reads=[pgk], writes=[GTk])
                sc, sck = sc_r.next()
                for g in range(2):
                    A("dve", lambda h, g=g: h.tensor_tensor(sc[:L, 8 * g:8 * g + 8, :L], dec[:L, 8 * g:8 * g + 8, :L],
                                                            GT[:L, g, :L].unsqueeze(1).broadcast_to([L, 8, L]), ALU.mult),
                      reads=[deck, GTk], writes=[sck + f"g{g}"])
                xdt, xdtk = xdt_r.next()
                A("dve", lambda h: h.tensor_tensor(xdt[:L], xs[:L, :].rearrange("p (h d) -> p h d", h=16),
                                                    dt[:L, :].unsqueeze(2).broadcast_to([L, 16, 64]), ALU.mult), reads=[xsk, dtk], writes=[xdtk])
                pys = []
                for half in range(2):
                    py, pyk = PM.next()

                    def my(h, py=py, half=half):
                        for hh in range(8):
                            hd = half * 8 + hh
                            h.matmul(py[:L, hh * 64:(hh + 1) * 64], sc[:L, hd, :L], xdt[:L, hd, :], start=True, stop=False)
                            ins = h.matmul(py[:L, hh * 64:(hh + 1) * 64], identD[:L, hd, :L], xsb[:L, hd * 64:(hd + 1) * 64], start=False, stop=True)
                        return ins
                    A("pe", my, reads=[sck + "g0", sck + "g1", xdtk, xsbk, "identD"], writes=[pyk])
                    pys.append((py, pyk))
                A("act", lambda h: h.copy(stateb[:], state[:]), reads=["state"], writes=["stateb"])
                pis = []
                for g in range(2):
                    pi, pik = PM.next()
                    A("pe", lambda h, pi=pi, g=g: h.matmul(pi[:L, :], bcT[:, 2 + g, :L], stateb[:, g * 512:(g + 1) * 512], start=True, stop=True),
                      reads=[bcTk, "stateb"], writes=[pik])
                    pis.append((pi, pik))
                t1, t1k = t1_r.next()
                t2, t2k = t2_r.next()
                for g in range(2):
                    A("dve", lambda h, g=g: h.tensor_tensor(
                        t1[:L, g * 512:(g + 1) * 512].rearrange("p (h d) -> p h d", h=8), pis[g][0][:L, :].rearrange("p (h d) -> p h d", h=8),
                        ecs[:L, 8 * g:8 * g + 8].unsqueeze(2).broadcast_to([L, 8, 64]), ALU.mult), reads=[pis[g][1], ecsk], writes=[t1k + f"a{g}"])
                    A("dve", lambda h, g=g: h.tensor_tensor(t1[:L, g * 512:(g + 1) * 512], t1[:L, g * 512:(g + 1) * 512], pys[g][0][:L, :], ALU.add),
                      reads=[pys[g][1], t1k + f"a{g}"], writes=[t1k + f"b{g}"])
                A("act", lambda h: h.activation(out=t2[:L, :], in_=z[:L, :], func=AF.Silu), reads=[zk], writes=[t2k])
                A("dve", lambda h: h.tensor_tensor(t1[:L, :], t1[:L, :], t2[:L, :], ALU.mult), reads=[t2k, t1k + "b0", t1k + "b1"], writes=[t1k])
                st, sk = ss_r.next()
                rstd_ops(t1, t1k, L, st, sk)
                yn, ynk = yn_r.next()
                A("dve", lambda h: h.scalar_tensor_tensor(yn[:L, :], t1[:L, :], st[:L, :], gb[:L, :], ALU.mult, ALU.mult),
                  reads=[t1k, sk, "gb"], writes=[ynk])
                A("sp", lambda h: h.dma_start(out=CAT[row0:row0 + L, 0:1024], in_=yn[:L, :]), reads=[ynk], writes=[f"CAT:{row0}"], chan=ynk + "st")
                xdw, xdwk = xdw_r.next()
                A("pool", lambda h: h.tensor_tensor(xdw[:L], xdt[:L], dec[:L, :, L - 1:L].broadcast_to([L, 16, 64]), ALU.mult),
                  reads=[xdtk, deck], writes=[xdwk])
                for g in range(2):
                    pst, pstk = PX.next()
                    A("pe", lambda h, pst=pst, g=g: h.matmul(pst[:, :], bcb[:L, g * 128:(g + 1) * 128], xdw[:L, 8 * g:8 * g + 8, :], start=True, stop=True),
                      reads=[bcbk, xdwk], writes=[pstk])
                    A("dve", lambda h, g=g: h.tensor_tensor(
                        state[:, g * 512:(g + 1) * 512].rearrange("p (h d) -> p h d", h=8), state[:, g * 512:(g + 1) * 512].rearrange("p (h d) -> p h d", h=8),
                        edl[:, 8 * g:8 * g + 8].unsqueeze(2).broadcast_to([128, 8, 64]), ALU.mult), reads=["state", "stateb", edlk], writes=["state"])
                    A("dve", lambda h, pst=pst, g=g: h.tensor_tensor(state[:, g * 512:(g + 1) * 512], state[:, g * 512:(g + 1) * 512], pst[:, :], ALU.add),
                      reads=["state", pstk], writes=["state"])

            def state_out(dst):
                for half in range(2):
                    pt, ptk = PM.next()

                    def trs(h, pt=pt, half=half):
                        for c in range(4):
                            cc = half * 4 + c
                            ins = h.transpose(pt[:, c * 128:(c + 1) * 128], state[:, cc * 128:(cc + 1) * 128], identf[:])
                        return ins
                    A("pe", trs, reads=["state"], writes=[ptk])
                    A("act", lambda h, pt=pt, half=half: h.copy(stf[:, half * 4:half * 4 + 4, :], pt[:, :].rearrange("p (c n) -> p c n", c=4)),
                      reads=[ptk], writes=[f"stf{half}"])
                A("sp", lambda h: h.dma_start(out=dst.rearrange("(c p) n -> p c n", p=128), in_=stf[:]), reads=["stf0", "stf1"], writes=["stout"], chan="stout")

            A("pool", lambda h: h.memset(state[:], 0.0), writes=["state"])
            for i in range(NTP):
                chunk(128, i * 128)
            state_out(o_pssm)
            for b in range(BPC):
                A("sp", lambda h, b=b: h.dma_start(out=stf[:], in_=i_ssm[b].rearrange("(c p) n -> p c n", p=128)),
                  reads=["stf0", "stf1", "stout"], writes=["stfin"], chan="stfin")
                for half in range(2):
                    pt, ptk = PM.next()

                    def trs(h, pt=pt, half=half):
                        for c in range(4):
                            ins = h.transpose(pt[:, c * 128:(c + 1) * 128], stf[:, half * 4 + c, :], identf[:])
                        return ins
                    A("pe", trs, reads=["stfin"], writes=[ptk])
                    A("act", lambda h, pt=pt, half=half: h.copy(state[:, half * 512:(half + 1) * 512], pt[:, :]), reads=[ptk, "stfin"], writes=["state"])
                chunk(4, TP + 4 * b)
                state_out(o_sssm[b])


    notri = sb([128, 128])
    A("dve", lambda h: h.tensor_scalar(notri[:], tri[:], -1.0, 1.0, ALU.mult, ALU.add), reads=["tri"], writes=["notri"])
    masks = {}
    for nq_ in (128, 4):
        mo = sb([128, 4 * nq_], BF16)
        mp = sb([128, 4 * nq_], BF16)
        A("dve", lambda h, mo=mo, nq_=nq_: h.tensor_copy(mo[:].rearrange("p (h q) -> p h q", h=4), tri[:, :nq_].unsqueeze(1).broadcast_to([128, 4, nq_])),
          reads=["tri"], writes=[f"mo{nq_}"])
        A("dve", lambda h, mp=mp, nq_=nq_: h.tensor_copy(mp[:].rearrange("p (h q) -> p h q", h=4), notri[:, :nq_].unsqueeze(1).broadcast_to([128, 4, nq_])),
          reads=["notri"], writes=[f"mp{nq_}"])
        masks[nq_] = (mo, mp)

    def swa_phase():
        with Phase():
            esink = sb([128, 16])
            A("sp", lambda h: h.dma_start(out=esink[:], in_=i_sink[0:1, :].partition_broadcast(128)), writes=["esink"], chan="esink")
            A("act", lambda h: h.activation(out=esink[:], in_=esink[:], func=AF.Exp), reads=["esink"], writes=["esink"])
            qb_r = rot(3, [128, 1024], BF16, "wqb")
            qT_r = rot(2, [128, 2048], BF16, "wqT")
            kb_r = rot(3, [128, 256], BF16, "wkb")
            kT_r = rot(5, [128, 4, 128], BF16, "wkT")
            va_r = rot(5, [128, 4, 65], BF16, "wva")
            yo_r = rot(2, [128, 1024], F32, "wyo")
            for t_ in kT_r.t:
                A("dve", lambda h, t_=t_: h.memset(t_[64:128, :, :], 0.0), writes=["wz"])
            for t_ in qT_r.t:
                A("dve", lambda h, t_=t_: h.memset(t_[64:128, :], 0.0), writes=["wz"])
            sc = 0.125
            dfr = Defer()

            def seq(r0, nq, blocks):
                mo, mp = masks[nq]
                qb, qbk = qb_r.next()
                A("pool", lambda h: h.dma_start(out=qb[:nq, :], in_=PROJ[r0:r0 + nq, 2576:3600]), writes=[qbk], chan=qbk)
                qT, qTk = qT_r.next()
                for g in range(2):
                    pt, pk = PT.next()
                    ptb = pt[:].bitcast(BF16)

                    def trq(h, ptb=ptb, g=g):
                        for hh in range(8):
                            hd = g * 8 + hh
                            ins = h.transpose(ptb[:64, hh * nq:(hh + 1) * nq], qb[:nq, hd * 64:(hd + 1) * 64], identb[:nq, :nq])
                        return ins
                    A("pe", trq, reads=[qbk], writes=[pk])
                    A("act", lambda h, ptb=ptb, g=g: h.copy(qT[:64, g * 8 * nq:(g + 1) * 8 * nq], ptb[:64, :8 * nq]), reads=[pk], writes=[qTk + f"g{g}"])
                blk = []
                for (ksrc, vsrc, nk, kind) in blocks:
                    kb, kbk = kb_r.next()
                    A("pool", lambda h, kb=kb, ksrc=ksrc, nk=nk: h.dma_start(out=kb[:nk, :], in_=ksrc), writes=[kbk], chan=kbk)
                    pt, pk = PT.next()
                    ptb = pt[:].bitcast(BF16)

                    def trk(h, ptb=ptb, kb=kb, nk=nk):
                        for j in range(4):
                            ins = h.transpose(ptb[:64, j * 128:j * 128 + nk], kb[:nk, j * 64:(j + 1) * 64], identb[:nk, :nk])
                        return ins
                    A("pe", trk, reads=[kbk], writes=[pk])
                    kT, kTk = kT_r.next()
                    A("act", lambda h, kT=kT, ptb=ptb, nk=nk: h.copy(kT[:64, :, :nk], ptb[:64, :512].rearrange("p (j t) -> p j t", j=4)[:, :, :nk]),
                      reads=[pk], writes=[kTk])
                    va, vak = va_r.next()
                    A("pool", lambda h, va=va: h.memset(va[:, :, 64:65], 1.0), writes=[vak + "o"])
                    A("pool", lambda h, va=va, vsrc=vsrc, nk=nk: h.dma_start(out=va[:nk, :, 0:64], in_=vsrc.rearrange("p (j d) -> p j d", j=4)),
                      writes=[vak], chan=vak)
                    blk.append((kT, kTk, va, vak, nk, kind))
                yo, yok = yo_r.next()
                for j in range(4):
                    Es = []
                    for bi, (kT, kTk, va, vak, nk, kind) in enumerate(blk):
                        m = mo if kind == "own" else mp
                        Es.append(attn_step(kT[:, j, :nk], [kTk, "wz"], qT[:, 4 * j * nq:(4 * j + 4) * nq], [qTk + "g0", qTk + "g1"], nk, 4 * nq, sc,
                                            None, [], m[:nk, :4 * nq], None, None, None))

                    def tail(Es=Es, j=j):
                        po, pok = PX.next()

                        def pvf(h):
                            for hq in range(4):
                                for bi, (kT, kTk, va, vak, nk, kind) in enumerate(blk):
                                    ins = h.matmul(po[:nq, hq * 65:(hq + 1) * 65], Es[bi][0][:nk, hq * nq:(hq + 1) * nq], va[:nk, j, :],
                                                   start=(bi == 0), stop=(bi == len(blk) - 1))
                            return ins
                        A("pe", pvf, reads=[e[1] for e in Es] + [b_[3] for b_ in blk] + [b_[3] + "o" for b_ in blk], writes=[pok])
                        rd, rdk = rd_r.next()
                        A("dve", lambda h: h.tensor_tensor(
                            rd[:nq, 0:4].unsqueeze(2), po[:nq, :260].rearrange("p (h e) -> p h e", h=4)[:, :, 64:65],
                            esink[:nq, 4 * j:4 * j + 4].unsqueeze(2), ALU.add), reads=[pok, "esink"], writes=[rdk])
                        A("dve", lambda h: h.reciprocal(rd[:nq, 0:4], rd[:nq, 0:4]), reads=[rdk], writes=[rdk])
                        for hq in range(4):
                            hd = 4 * j + hq
                            A("dve", lambda h, hq=hq, hd=hd: h.tensor_scalar(
                                yo[:nq, hd * 64:(hd + 1) * 64], po[:nq, hq * 65:hq * 65 + 64], rd[:nq, hq:hq + 1], None, ALU.mult),
                              reads=[pok, rdk], writes=[yok + f"h{hd}"])
                        if j == 3:
                            A("sp", lambda h: h.dma_start(out=CAT[r0:r0 + nq, 1024:2048], in_=yo[:nq, :]),
                              reads=[yok + f"h{hd}" for hd in range(16)], writes=[f"CATa:{r0}"], chan=yok + "st")
                    dfr.push(tail)

            for i in range(NTP):
                blocks = []
                if i > 0:
                    blocks.append((PROJ[(i - 1) * 128:i * 128, 3600:3856], PROJ[(i - 1) * 128:i * 128, 3856:4112], 128, "prev"))
                blocks.append((PROJ[i * 128:(i + 1) * 128, 3600:3856], PROJ[i * 128:(i + 1) * 128, 3856:4112], 128, "own"))
                seq(i * 128, 128, blocks)
            for b in range(BPC):
                r = TP + 4 * b
                seq(r, 4, [(i_cswk[b], i_cswv[b], 128, "prev"), (PROJ[r:r + 4, 3600:3856], PROJ[r:r + 4, 3856:4112], 4, "own")])
            dfr.flush()


    trib = sb([128, 128], BF16)
    A("dve", lambda h: h.tensor_copy(trib[:], tri[:]), reads=["tri"], writes=["trib"])
    iotaf = sb([128, 1])
    A("sp", lambda h: h.dma_start(out=iotaf[:], in_=i_iota[:, :]), writes=["iotaf"], chan="iotaf")

    def odd_inproj():
        with Phase():
            new_hT()
            norm_T(X, i_nmix[1:2, :], tiles, hT, "hT")
            linear_to_dram(hT, "hT", i_wio, ODD_PROJ, PROJ, tiles)
        with Phase():
            fbb = sb([128, 16])
            A("sp", lambda h: h.dma_start(out=fbb[:], in_=i_fb[0:1, :].partition_broadcast(128)), writes=["fbb"], chan="fbb")
            f_r = rot(2, [128, 16], F32, "of")
            d1_r = rot(2, [128, 16], F32, "od1")
            d2_r = rot(2, [128, 16], F32, "od2")
            for (r0, rows) in tiles:
                f, fk = f_r.next()
                d1, d1k = d1_r.next()
                d2, d2k = d2_r.next()
                A("sp", lambda h, f=f, r0=r0, rows=rows: h.dma_start(out=f[:rows, :], in_=PROJ[r0:r0 + rows, 1536:1552]), writes=[fk], chan=fk)
                A("dve", lambda h, f=f, rows=rows: h.tensor_tensor(f[:rows, :], f[:rows, :], fbb[:rows, :], ALU.add), reads=[fk, "fbb"], writes=[fk])
                A("dve", lambda h, f=f, d1=d1, rows=rows: h.tensor_scalar(d1[:rows, :], f[:rows, :], -1.0, None, ALU.mult), reads=[fk], writes=[d1k])
                A("dve", lambda h, f=f, d2=d2, d1=d1, rows=rows: h.tensor_tensor(d2[:rows, :], d1[:rows, :], f[:rows, :], ALU.min), reads=[fk, d1k], writes=[d2k])
                A("act", lambda h, d2=d2, rows=rows: h.activation(out=d2[:rows, :], in_=d2[:rows, :], func=AF.Exp), reads=[d2k], writes=[d2k])
                A("dve", lambda h, d2=d2, rows=rows: h.tensor_scalar(d2[:rows, :], d2[:rows, :], 1.0, None, ALU.add), reads=[d2k], writes=[d2k])
                A("act", lambda h, d2=d2, rows=rows: h.activation(out=d2[:rows, :], in_=d2[:rows, :], func=AF.Ln), reads=[d2k], writes=[d2k])
                A("dve", lambda h, d1=d1, rows=rows: h.tensor_scalar(d1[:rows, :], d1[:rows, :], 0.0, None, ALU.max), reads=[d1k], writes=[d1k])
                A("dve", lambda h, d1=d1, d2=d2, rows=rows: h.tensor_tensor(d1[:rows, :], d1[:rows, :], d2[:rows, :], ALU.add), reads=[d1k, d2k], writes=[d1k])
                A("dve", lambda h, f=f, d1=d1, rows=rows: h.tensor_scalar(f[:rows, :], d1[:rows, :], -1.0, None, ALU.mult), reads=[d1k, fk], writes=[fk])
                dst = o_pfl[r0:r0 + rows, :] if r0 < TP else o_sfl[:, :]
                A("sp", lambda h, f=f, dst=dst, rows=rows: h.dma_start(out=dst, in_=f[:rows, :]), reads=[fk], writes=[f"lf:{r0}"], chan=fk + "st")
            for t0 in range(0, TP, 1024):
                tw = min(1024, TP - t0)
                dram_copy(o_pfk[t0:t0 + tw, :], PROJ[t0:t0 + tw, 1024:1280], f"opfk{t0}")
                dram_copy(o_pfv[t0:t0 + tw, :], PROJ[t0:t0 + tw, 1280:1536], f"opfv{t0}")
            dram_copy(o_sfk[:, :], PROJ[TP:TT, 1024:1280], "osfk")
            dram_copy(o_sfv[:, :], PROJ[TP:TT, 1280:1536], "osfv")

    def fox_prompt():
        with Phase():
            kT = sb([128, 4, TP], BF16)
            vaug = sb([128, NTP, 4, 65], BF16)
            cum = sb([128, NTP, 16])
            cend = sb([128, NTP, 16])
            A("pool", lambda h: h.memset(vaug[:, :, :, 64:65], 1.0), writes=["fvo"])
            kb_r = rot(3, [128, 256], BF16, "fkb")
            A("dve", lambda h: h.memset(kT[64:128, :, :], 0.0), writes=["fz"])
            lf_r = rot(2, [128, 16], F32, "flf")
            for i in range(NTP):
                kb, kbk = kb_r.next()
                A("pool", lambda h, kb=kb, i=i: h.dma_start(out=kb[:], in_=PROJ[i * 128:(i + 1) * 128, 1024:1280]), writes=[kbk], chan=kbk)
                pt, pk = PT.next()
                ptb = pt[:].bitcast(BF16)

                def trk(h, ptb=ptb, kb=kb):
                    for j in range(4):
                        ins = h.transpose(ptb[:64, j * 128:(j + 1) * 128], kb[:, j * 64:(j + 1) * 64], identb[:])
                    return ins
                A("pe", trk, reads=[kbk], writes=[pk])
                A("act", lambda h, ptb=ptb, i=i: h.copy(kT[:64, :, i * 128:(i + 1) * 128], ptb[:64, :512].rearrange("p (j t) -> p j t", j=4)),
                  reads=[pk], writes=[f"fkT{i}"])
                A("pool", lambda h, i=i: h.dma_start(out=vaug[:, i, :, 0:64], in_=PROJ[i * 128:(i + 1) * 128, 1280:1536].rearrange("p (j d) -> p j d", j=4)),
                  writes=[f"fva{i}"], chan=f"fva{i % 4}")
                lf, lfk = lf_r.next()
                A("sp", lambda h, lf=lf, i=i: h.dma_start(out=lf[:], in_=o_pfl[i * 128:(i + 1) * 128, :]), writes=[lfk], chan=lfk)
                pc, pck = PT.next()
                A("pe", lambda h, pc=pc, lf=lf: h.matmul(pc[:, 0:16], tri[:], lf[:], start=True, stop=True), reads=[lfk], writes=[pck])
                pl, plk = PT.next()
                A("pe", lambda h, pl=pl, lf=lf: h.matmul(pl[:, 0:16], ones[:], lf[:], start=True, stop=True), reads=[lfk], writes=[plk])
                if i == 0:
                    A("dve", lambda h, pc=pc: h.tensor_copy(cum[:, 0, :], pc[:, 0:16]), reads=[pck], writes=["fcum0"])
                    A("dve", lambda h, pl=pl: h.tensor_copy(cend[:, 0, :], pl[:, 0:16]), reads=[plk], writes=["fcend0"])
                else:
                    A("dve", lambda h, pc=pc, i=i: h.tensor_tensor(cum[:, i, :], pc[:, 0:16], cend[:, i - 1, :], ALU.add),
                      reads=[pck, f"fcend{i - 1}"], writes=[f"fcum{i}"])
                    A("dve", lambda h, pl=pl, i=i: h.tensor_tensor(cend[:, i, :], pl[:, 0:16], cend[:, i - 1, :], ALU.add),
                      reads=[plk, f"fcend{i - 1}"], writes=[f"fcend{i}"])
            qb_r = rot(3, [128, 1024], BF16, "fqb")
            qTG = sb([128, 16 * 512], BF16)
            A("dve", lambda h: h.memset(qTG[64:128, :], 0.0), writes=["fz2"])
            yG = sb([128, 4, 1024])
            NG = (NTP + 3) // 4
            dfr = Defer(2)
            SB3 = Rot([pbank[6], pbank[7], pbank[1]], "fS")
            SB3.keys = ["pX0", "pX1", "pT1"]
            for G in range(NG):
                qbs = list(range(4 * G, min(4 * G + 4, NTP)))
                nqb = len(qbs)
                lastJ = qbs[-1]
                biasG = sb([128, NTP, 16]) if G == 0 else biasG
                for qi, qb_ in enumerate(qbs):
                    qb, qbk = qb_r.next()
                    A("pool", lambda h, qb=qb, qb_=qb_: h.dma_start(out=qb[:], in_=PROJ[qb_ * 128:(qb_ + 1) * 128, 0:1024]), writes=[qbk], chan=qbk)
                    for g in range(2):
                        pt, pk = PT.next()
                        ptb = pt[:].bitcast(BF16)

                        def trq(h, ptb=ptb, g=g, qb=qb):
                            for hh in range(8):
                                hd = g * 8 + hh
                                ins = h.transpose(ptb[:64, hh * 128:(hh + 1) * 128], qb[:, hd * 64:(hd + 1) * 64], identb[:])
                            return ins
                        A("pe", trq, reads=[qbk], writes=[pk])
                        A("act", lambda h, ptb=ptb, g=g, qi=qi: h.copy(
                            qTG[:64, :].rearrange("p (h c) -> p h c", h=16)[:, g * 8:(g + 1) * 8, qi * 128:(qi + 1) * 128],
                            ptb[:64, :].rearrange("p (h t) -> p h t", h=8)), reads=[pk], writes=[f"fqT{qi}g{g}"])
                for J in range(lastJ + 1):
                    A("dve", lambda h, J=J, lastJ=lastJ: h.tensor_tensor(biasG[:, J, :], cend[:, lastJ, :], cum[:, J, :], ALU.subtract),
                      reads=[f"fcend{lastJ}", f"fcum{J}"], writes=[f"fbias{J}"])
                qkeys = [f"fqT{qi}g{g}" for qi in range(nqb) for g in range(2)]
                for hd in range(16):
                    kvh = hd // 4
                    for J in range(lastJ + 1):
                        qlo = max(0, J - 4 * G)
                        ncols = (nqb - qlo) * 128
                        E, ek = attn_step(kT[:, kvh, J * 128:(J + 1) * 128], [f"fkT{J}", "fz"],
                                          qTG[:, hd * 512 + qlo * 128:hd * 512 + nqb * 128], qkeys + ["fz2"], 128, ncols, 0.125,
                                          biasG[:, J, hd:hd + 1], [f"fbias{J}"], None, None, None, None, SB=SB3)
                        if J >= 4 * G:
                            A("dve", lambda h, E=E: h.tensor_tensor(E[:, 0:128], E[:, 0:128], trib[:], ALU.mult), reads=[ek], writes=[ek])

                        def tail(E=E, ek=ek, J=J, qlo=qlo, kvh=kvh):
                            def pvf(h):
                                for qi in range(qlo, nqb):
                                    ins = h.matmul(PM.t[qi][:, 0:65], E[:, (qi - qlo) * 128:(qi - qlo + 1) * 128], vaug[:, J, kvh, :],
                                                   start=(J == 0), stop=(J == 4 * G + qi))
                                return ins
                            A("pe", pvf, reads=[ek, f"fva{J}", "fvo"], writes=[f"pM{qi}" for qi in range(qlo, nqb)])
                        dfr.push(tail)
                    dfr.flush()
                    for qi in range(nqb):
                        rd, rdk = rd_r.next()
                        A("dve", lambda h, rd=rd, qi=qi: h.reciprocal(rd[:, 0:1], PM.t[qi][:, 64:65]), reads=[f"pM{qi}"], writes=[rdk])
                        A("dve", lambda h, rd=rd, qi=qi, hd=hd: h.tensor_scalar(
                            yG[:, qi, hd * 64:(hd + 1) * 64], PM.t[qi][:, 0:64], rd[:, 0:1], None, ALU.mult),
                          reads=[f"pM{qi}", rdk], writes=[f"fyG{qi}h{hd}"])
                for qi, qb_ in enumerate(qbs):
                    A("sp", lambda h, qi=qi, qb_=qb_: h.dma_start(out=CAT[qb_ * 128:(qb_ + 1) * 128, 0:1024], in_=yG[:, qi, :]),
                      reads=[f"fyG{qi}h{hd}" for hd in range(16)], writes=[f"CATf:{qb_}"], chan=f"fyst{qi}")

    def fox_sample():
        with Phase():
            pti = sb([128, NPG], I32)
            ptc = sb([128, 1], I32)
            ptf = sb([128, NPG])
            idx_r = rot(2, [128, NPG], I32, "gidx")
            lfT_r = rot(2, [128, 2048], F32, "glfT")
            lfp = sb([128, NPG, 16])
            cumw = sb([128, NPG, 16])
            inc = [sb([128, NPG, 16]), sb([128, NPG, 16])]
            tot = sb([128, NPG, 16])
            biasP_r = rot(2, [128, NPG, 16], F32, "gbiasP")
            cref = sb([128, 16])
            lnew = sb([128, 16])
            cnew = sb([128, 16])
            biasN_r = rot(2, [128, 16], F32, "gbiasN")
            qf = sb([128, 1024])
            qb = sb([128, 1024], BF16)
            qTb_r = rot(2, [128, 64], BF16, "gqT")
            kvf_r = rot(4, [128, 2, 512], F32, "gkvf")
            kb_r = rot(3, [128, 2, 256], BF16, "gkb")
            kT_r = rot(3, [128, 8, 128], BF16, "gkT")
            va_r = rot(4, [128, 2, 4, 65], BF16, "gva")
            ar_r = rot(3, [128, 128], F32, "gar")
            E_g = rot(4, [128, 128], BF16, "gE")
            o_r = rot(2, [128, 64], F32, "go")
            for va_t in va_r.t:
                A("pool", lambda h, va_t=va_t: h.memset(va_t[:, :, :, 64:65], 1.0), writes=["gvao"])
            dfr = Defer()
            for b in range(BPC):
                r = TP + 4 * b
                idx, idxk = idx_r.next()
                A("sp", lambda h, b=b: h.dma_start(out=pti[:], in_=i_pt[b:b + 1, :].partition_broadcast(128)), writes=["pti"], chan="pti")
                A("sp", lambda h, b=b: h.dma_start(out=ptc[:NPG, :], in_=i_pt[b, :].rearrange("(j o) -> j o", o=1)), writes=["ptc"], chan="ptc")
                A("dve", lambda h: h.tensor_copy(ptf[:], pti[:]), reads=["pti"], writes=["ptf"])
                A("dve", lambda h: h.tensor_scalar(ptf[:], ptf[:], 128.0, iotaf[:, 0:1], ALU.mult, ALU.add), reads=["ptf", "iotaf"], writes=["ptf"])
                A("dve", lambda h, idx=idx: h.tensor_copy(idx[:], ptf[:]), reads=["ptf"], writes=[idxk])
                lfT, lfTk = lfT_r.next()
                A("pool", lambda h, lfT=lfT: h.indirect_dma_start(out=lfT[:NPG, :], out_offset=None, in_=i_flp[:, :],
                                                                  in_offset=bass.IndirectOffsetOnAxis(ap=ptc[:NPG, 0:1], axis=0)),
                  reads=["ptc"], writes=[lfTk], chan=lfTk)
                HB = max(1, min(16, 512 // NPG))
                for g0 in range(0, 16, HB):
                    pm, pmk = PM.next()

                    def trl(h, pm=pm, g0=g0, lfT=lfT):
                        for hh in range(HB):
                            ins = h.transpose(pm[:, hh * NPG:(hh + 1) * NPG],
                                              lfT[:NPG, :].rearrange("p (k h) -> p k h", h=16)[:, :, g0 + hh], identf[:NPG, :NPG])
                        return ins
                    A("pe", trl, reads=[lfTk], writes=[pmk])
                    A("act", lambda h, pm=pm, g0=g0: h.copy(lfp[:].rearrange("p j h -> p h j")[:, g0:g0 + HB, :],
                                                            pm[:, :HB * NPG].rearrange("p (h j) -> p h j", h=HB)),
                      reads=[pmk], writes=[f"lfp{g0}"])
                lkeys = [f"lfp{g0}" for g0 in range(0, 16, HB)]
                A("sp", lambda h, b=b: h.dma_start(out=lnew[:4, :], in_=o_sfl[4 * b:4 * b + 4, :]), writes=["lnew"], chan="lnew")
                A("sp", lambda h, r=r: h.dma_start(out=qf[:4, :], in_=PROJ[r:r + 4, 0:1024]), writes=["gqf"], chan="gqf")
                A("act", lambda h: h.copy(qb[:4, :], qf[:4, :]), reads=["gqf"], writes=["gqb"])
                pt, pk = PT.next()
                ptb = pt[:].bitcast(BF16)

                def trq(h, ptb=ptb):
                    for hd in range(16):
                        ins = h.transpose(ptb[:64, hd * 4:(hd + 1) * 4], qb[:4, hd * 64:(hd + 1) * 64], identb[:4, :4])
                    return ins
                A("pe", trq, reads=["gqb"], writes=[pk])
                qTb, qTk = qTb_r.next()
                A("act", lambda h, ptb=ptb, qTb=qTb: h.copy(qTb[:64, :], ptb[:64, 0:64]), reads=[pk], writes=[qTk])
                CW = NPG * 16
                for c0 in range(0, CW, 512):
                    cw = min(512, CW - c0)
                    pm, pmk = PM.next()
                    A("pe", lambda h, pm=pm, c0=c0, cw=cw: h.matmul(pm[:, :cw], tri[:], lfp[:].rearrange("p a b -> p (a b)")[:, c0:c0 + cw],
                                                                  start=True, stop=True), reads=lkeys, writes=[pmk])
                    A("dve", lambda h, pm=pm, c0=c0, cw=cw: h.tensor_copy(cumw[:].rearrange("p a b -> p (a b)")[:, c0:c0 + cw], pm[:, :cw]),
                      reads=[pmk], writes=[f"cumw{c0}"])
                    pm2, pm2k = PM.next()
                    A("pe", lambda h, pm2=pm2, c0=c0, cw=cw: h.matmul(pm2[:, :cw], ones[:], lfp[:].rearrange("p a b -> p (a b)")[:, c0:c0 + cw],
                                                                    start=True, stop=True), reads=lkeys, writes=[pm2k])
                    A("dve", lambda h, pm2=pm2, c0=c0, cw=cw: h.tensor_copy(tot[:].rearrange("p a b -> p (a b)")[:, c0:c0 + cw], pm2[:, :cw]),
                      reads=[pm2k], writes=[f"tot{c0}"])
                ck = [f"cumw{c0}" for c0 in range(0, CW, 512)]
                tk_ = [f"tot{c0}" for c0 in range(0, CW, 512)]
                A("dve", lambda h: h.tensor_copy(inc[0][:], tot[:]), reads=tk_, writes=["inc0"])
                cur = 0
                k_ = 1
                while k_ < NPG:
                    nx = 1 - cur
                    A("dve", lambda h, cur=cur, nx=nx, k_=k_: h.tensor_copy(inc[nx][:, :k_, :], inc[cur][:, :k_, :]), reads=[f"inc{cur}"], writes=[f"inc{nx}"])
                    A("dve", lambda h, cur=cur, nx=nx, k_=k_: h.tensor_tensor(inc[nx][:, k_:, :], inc[cur][:, k_:, :], inc[cur][:, :NPG - k_, :], ALU.add),
                      reads=[f"inc{cur}"], writes=[f"inc{nx}"])
                    cur = nx
                    k_ *= 2
                incf = inc[cur]
                ik = f"inc{cur}"
                A("dve", lambda h, incf=incf: h.tensor_tensor(cumw[:], cumw[:], incf[:], ALU.add), reads=ck + [ik], writes=["cumP"])
                A("dve", lambda h: h.tensor_tensor(cumw[:], cumw[:], tot[:], ALU.subtract), reads=tk_ + ["cumP"], writes=["cumP"])
                pl, plk = PT.next()
                A("pe", lambda h, pl=pl: h.matmul(pl[:, 0:16], ones[:4, :], lnew[:4, :], start=True, stop=True), reads=["lnew"], writes=[plk])
                A("dve", lambda h, pl=pl, incf=incf: h.tensor_tensor(cref[:], pl[:, 0:16], incf[:, NPG - 1, :], ALU.add), reads=[plk, ik], writes=["cref"])
                pc, pck = PT.next()
                A("pe", lambda h, pc=pc: h.matmul(pc[:4, 0:16], tri[:4, :4], lnew[:4, :], start=True, stop=True), reads=["lnew"], writes=[pck])
                A("dve", lambda h, pc=pc, incf=incf: h.tensor_tensor(cnew[:4, :], pc[:4, 0:16], incf[:4, NPG - 1, :], ALU.add), reads=[pck, ik], writes=["cnew"])
                biasN, bNk = biasN_r.next()
                biasP, bPk = biasP_r.next()
                A("dve", lambda h, biasN=biasN: h.tensor_tensor(biasN[:4, :], cref[:4, :], cnew[:4, :], ALU.subtract), reads=["cref", "cnew"], writes=[bNk])
                A("dve", lambda h, biasP=biasP: h.tensor_tensor(biasP[:], cref[:].unsqueeze(1).broadcast_to([128, NPG, 16]), cumw[:], ALU.subtract),
                  reads=["cref", "cumP"], writes=[bPk])
                PGS = cfg.get("PGS", 2) if NPG % 2 == 0 else 1
                steps = [(j0, PGS, False) for j0 in range(0, NPG, PGS)] + [(NPG, 1, True)]
                for (j0, npg, new) in steps:
                    nk = 4 if new else 128
                    kvf, kvk = kvf_r.next()
                    if new:
                        A("sp", lambda h, kvf=kvf, r=r: h.dma_start(out=kvf[:4, 0, :], in_=PROJ[r:r + 4, 1024:1536]),
                          writes=[kvk + "p0", kvk + "p1"], chan=kvk)
                    else:
                        for pg in range(npg):
                            A("pool", lambda h, kvf=kvf, j=j0 + pg, pg=pg, idx=idx: h.indirect_dma_start(
                                out=kvf[:, pg, :], out_offset=None, in_=i_fkv[:, :], in_offset=bass.IndirectOffsetOnAxis(ap=idx[:, j:j + 1], axis=0)),
                              reads=[idxk], writes=[kvk + f"p{pg}"], chan=kvk + f"p{pg}")
                    kvks = [kvk + "p0", kvk + "p1"] if new else [kvk + f"p{pg}" for pg in range(npg)]
                    kb, kbk = kb_r.next()
                    A("dve", lambda h, kb=kb, kvf=kvf, nk=nk, npg=npg: h.tensor_copy(kb[:nk, :npg, :], kvf[:nk, :npg, 0:256]), reads=kvks, writes=[kbk])
                    pt, pk = PT.next()
                    ptb = pt[:].bitcast(BF16)

                    def trk(h, ptb=ptb, kb=kb, nk=nk, npg=npg):
                        for pg in range(npg):
                            for jj in range(4):
                                c = (pg * 4 + jj) * 128
                                ins = h.transpose(ptb[:64, c:c + nk], kb[:nk, pg, jj * 64:(jj + 1) * 64], identb[:nk, :nk])
                        return ins
                    A("pe", trk, reads=[kbk], writes=[pk])
                    kT, kTk = kT_r.next()
                    A("act", lambda h, kT=kT, ptb=ptb, nk=nk, npg=npg: h.copy(
                        kT[:64, :npg * 4, :nk], ptb[:64, :npg * 512].rearrange("p (j t) -> p j t", j=npg * 4)[:, :, :nk]),
                      reads=[pk], writes=[kTk])
                    va, vak = va_r.next()
                    A("act", lambda h, va=va, kvf=kvf, nk=nk, npg=npg: h.copy(
                        va[:nk, :npg, :, 0:64], kvf[:nk, :npg, 256:512].rearrange("p g (j d) -> p g j d", j=4)),
                      reads=kvks + ["gvao"], writes=[vak])
                    ps_, psk = PX.next()

                    def ms(h, ps_=ps_, kT=kT, nk=nk, qTb=qTb, npg=npg):
                        for pg in range(npg):
                            for kvh in range(4):
                                ins = h.matmul(ps_[:nk, pg * 64 + kvh * 16:pg * 64 + (kvh + 1) * 16], kT[:64, pg * 4 + kvh, :nk],
                                               qTb[:64, kvh * 16:(kvh + 1) * 16], start=True, stop=True)
                        return ins
                    A("pe", ms, reads=[kTk, qTk], writes=[psk])
                    ar, ark = ar_r.next()
                    if new:
                        bsrc = biasN[:4, :].unsqueeze(1).unsqueeze(3).broadcast_to([4, 1, 16, 4])
                    else:
                        bsrc = biasP[:, j0:j0 + npg, :].unsqueeze(3).broadcast_to([128, npg, 16, 4])
                    A("dve", lambda h, ar=ar, ps_=ps_, bsrc=bsrc, nk=nk, npg=npg: h.scalar_tensor_tensor(
                        ar[:nk, :npg * 64].rearrange("p (g h q) -> p g h q", g=npg, h=16),
                        ps_[:nk, 0:npg * 64].rearrange("p (g h q) -> p g h q", g=npg, h=16), 0.125, bsrc, ALU.mult, ALU.add),
                      reads=[psk, bNk if new else bPk], writes=[ark])
                    E, ek = E_g.next()
                    A("act", lambda h, E=E, ar=ar, nk=nk, npg=npg: h.activation(out=E[:nk, :npg * 64], in_=ar[:nk, :npg * 64], func=AF.Exp),
                      reads=[ark], writes=[ek])
                    if new:
                        A("dve", lambda h, E=E: h.tensor_tensor(E[:4, :64].rearrange("p (h q) -> p h q", h=16), E[:4, :64].rearrange("p (h q) -> p h q", h=16),
                                                                trib[:4, :4].unsqueeze(1).broadcast_to([4, 16, 4]), ALU.mult), reads=[ek], writes=[ek])

                    def tail(E=E, ek=ek, va=va, vak=vak, nk=nk, j0=j0, npg=npg, new=new, r=r):
                        def pvf(h):
                            for kvh in range(4):
                                for pg in range(npg):
                                    j = j0 + pg
                                    ins = h.matmul(PM.t[kvh][:16, 0:65], E[:nk, pg * 64 + kvh * 16:pg * 64 + (kvh + 1) * 16], va[:nk, pg, kvh, :],
                                                   start=(j == 0), stop=(j == NPG))
                            return ins
                        A("pe", pvf, reads=[ek, vak], writes=["pM0", "pM1", "pM2", "pM3"])
                        if new:
                            for kvh in range(4):
                                rd, rdk = rd_r.next()
                                A("dve", lambda h, rd=rd, kvh=kvh: h.reciprocal(rd[:16, 0:1], PM.t[kvh][:16, 64:65]), reads=[f"pM{kvh}"], writes=[rdk])
                                o, ok = o_r.next()
                                A("dve", lambda h, o=o, rd=rd, kvh=kvh: h.tensor_scalar(o[:16, :], PM.t[kvh][:16, 0:64], rd[:16, 0:1], None, ALU.mult),
                                  reads=[f"pM{kvh}", rdk], writes=[ok])
                                for hl in range(4):
                                    hd = 4 * kvh + hl
                                    A("sp", lambda h, o=o, hl=hl, hd=hd: h.dma_start(out=CAT[r:r + 4, hd * 64:(hd + 1) * 64], in_=o[hl * 4:(hl + 1) * 4, :]),
                                      reads=[ok], writes=[f"CATs:{r}:{hd}"], chan=f"gst{hl}")
                    dfr.push(tail)
                dfr.flush()

    A("sp", lambda h: h.dma_start(out=X[0:TP, :], in_=i_xp[:, :]), writes=["Xinit"], chan="xi0")
    A("sp", lambda h: h.dma_start(out=X[TP:TT, :], in_=i_xs[:, :]), writes=["Xinit2"], chan="xi1")
    P.barrier()

    STAGE = cfg.get("STAGE", 99)
    for l in range(2):
        if STAGE == 1:
            xattn(0)
            break
        if l == 0:
            even_inproj()
            conv_phase()
            ssd_phase()
            swa_phase()
            proj_residual(CAT, 2048, i_woe, tiles)
        else:
            odd_inproj()
            fox_prompt()
            fox_sample()
            proj_residual(CAT, 1024, i_woo, tiles)
        xattn(l)
        with Phase():
            ffn(l)
    with Phase():
        final_norm()

    P.finalize()
    es.close()
    return nc, dict(nsems=P.nsems, maxval=P.maxval, nops=len(P.ops))


def make_in_maps(cfg, inputs):
    SEQ, DEC_BATCH, PAST = cfg["SEQ"], cfg["DEC_BATCH"], cfg["PAST"]
    BPC = DEC_BATCH // NCORES
    f = lambda a: np.ascontiguousarray(np.asarray(a))
    g = inputs
    npool = g["cache_fox_k"].shape[1]
    shared = {
        "fox_kv": np.concatenate([np.asarray(g["cache_fox_k"][0]).reshape(npool * 128, 256),
                                  np.asarray(g["cache_fox_v"][0]).reshape(npool * 128, 256)], axis=1),
        "fox_lp": f(g["cache_fox_logf"][0]).reshape(npool, 2048),
        "norm_mix": f(g["norm_mix"]), "norm_xa": f(g["norm_xa"]), "norm_mem": f(g["norm_mem"]),
        "norm_ffn": f(g["norm_ffn"]), "w_in_even": f(g["w_in_even"][0]), "conv_w": f(g["conv_w"][0]),
        "conv_b": f(g["conv_b"]), "dt_bias": f(g["dt_bias"]), "a_log": f(g["a_log"]), "d_skip": f(g["d_skip"]),
        "ssm_norm": f(g["ssm_norm"]), "swa_sink": f(g["swa_sink"]), "w_out_even": f(g["w_out_even"][0]),
        "w_in_odd": f(g["w_in_odd"][0]), "fox_fb": f(g["fox_fb"]), "w_out_odd": f(g["w_out_odd"][0]),
        "w_xq": f(g["w_xq"]), "w_xk": f(g["w_xk"]), "w_xv": f(g["w_xv"]), "w_xo": f(g["w_xo"]),
        "w_ffn_in": f(g["w_ffn_in"]), "w_ffn_out": f(g["w_ffn_out"]),
        "norm_final": f(g["norm_final"]).reshape(1, D),
        "iota": np.arange(128, dtype=np.float32).reshape(128, 1),
    }
    maps = []
    for c in range(NCORES):
        b0, b1 = c * BPC, (c + 1) * BPC
        sq = c % 4
        m = dict(shared)
        m["xp"] = f(g["x_prompt"][sq])
        m["xs"] = f(g["x_sample"][b0:b1]).reshape(BPC * 4, D)
        m["state_ssm"] = f(g["state_ssm"][0, b0:b1]).reshape(BPC, 1024, 128)
        m["state_conv"] = f(g["state_conv"][0, b0:b1])
        m["cswk"] = f(g["cache_swa_k"][0, b0:b1]).reshape(BPC, 128, 256)
        m["cswv"] = f(g["cache_swa_v"][0, b0:b1]).reshape(BPC, 128, 256)
        m["cmk"] = f(g["cache_mem_k"][:, b0:b1]).reshape(2, BPC, 256, 512)
        m["cmv"] = f(g["cache_mem_v"][:, b0:b1]).reshape(2, BPC, 256, 512)
        m["pt"] = f(g["page_table"][b0:b1]).astype(np.int32)
        m["memp"] = f(g["mem_prompt"][sq])
        maps.append(m)
    return maps


def gather_outputs(cfg, res):
    SEQ, DEC_BATCH = cfg["SEQ"], cfg["DEC_BATCH"]
    BPC = DEC_BATCH // NCORES
    R = res
    st = lambda name, cores: np.stack([np.asarray(R[c][name]) for c in cores])
    cat = lambda name: np.concatenate([np.asarray(R[c][name]) for c in range(NCORES)], axis=0)
    pc = range(4)
    y_p = st("y_p", pc)
    y_s = cat("y_s").reshape(DEC_BATCH, 4, D)
    p_ssm = st("p_ssm", pc).reshape(1, 4, 16, 64, 128)
    p_conv = st("p_conv", pc).reshape(1, 4, 3, 1536)
    p_swk = st("p_swk", pc).reshape(1, 4, 128, 4, 64)
    p_swv = st("p_swv", pc).reshape(1, 4, 128, 4, 64)
    p_fk = st("p_fk", pc).reshape(1, 4, SEQ, 4, 64)
    p_fv = st("p_fv", pc).reshape(1, 4, SEQ, 4, 64)
    p_fl = st("p_fl", pc).reshape(1, 4, SEQ, 16)
    p_mk = np.swapaxes(st("p_mk", pc), 0, 1).reshape(2, 4, 256, 4, 128)
    p_mv = np.swapaxes(st("p_mv", pc), 0, 1).reshape(2, 4, 256, 4, 128)
    s_ssm = cat("s_ssm").reshape(1, DEC_BATCH, 16, 64, 128)
    s_conv = cat("s_conv").reshape(1, DEC_BATCH, 3, 1536)
    s_swk = cat("s_swk").reshape(1, DEC_BATCH, 128, 4, 64)
    s_swv = cat("s_swv").reshape(1, DEC_BATCH, 128, 4, 64)
    s_fk = cat("s_fk").reshape(1, DEC_BATCH, 4, 4, 64)
    s_fv = cat("s_fv").reshape(1, DEC_BATCH, 4, 4, 64)
    s_fl = cat("s_fl").reshape(1, DEC_BATCH, 4, 16)
    outs = (y_p, y_s, p_ssm, p_conv, p_swk, p_swv, p_fk, p_fv, p_fl, p_mk, p_mv,
            s_ssm, s_conv, s_swk, s_swv, s_fk, s_fv, s_fl)
    return tuple(np.ascontiguousarray(o, dtype=np.float32) for o in outs)


def run(cfg, inputs, debug=()):
    nc, info = build(cfg, debug)
    maps = make_in_maps(cfg, inputs)
    res = run_bass_kernel_spmd(nc, maps, core_ids=list(range(NCORES)))
    return res.results, info


def kernel(**inputs):
    res, _ = run(CFG, inputs)
    return gather_outputs(CFG, res)
```
